# Optimizing a Trainium2 kernel written in Bass

```python
import jax, jax.numpy as jnp
from jax import lax
import numpy as np

D_MODEL = 1024
BATCH = 4
SEQ = 4096
DEPTH = 4

N_HEADS = 16
HEAD_DIM = 64
N_KV_HEADS = 4
ROT_DIM = HEAD_DIM // 4
ROPE_THETA = 500000.0
MOBA_BLOCK = 256
MOBA_TOPK = 3
MOBA_QCHUNK = 32
WINDOW = 128
D_FF = 2816
CONV_W = 3
N_A = DEPTH // 2
N_B = DEPTH - N_A
EPS = 1e-6
NEG = -1e30

kernel_name = 'yoco_moba_swa_sink_convffn'


def rmsnorm(x, g):
    xf = x.astype(jnp.float32)
    y = xf * lax.rsqrt(jnp.mean(xf * xf, axis=-1, keepdims=True) + EPS)
    return (y * g.astype(jnp.float32)).astype(x.dtype)


def rope_tables(positions):
    inv_freq = ROPE_THETA ** (-jnp.arange(0, ROT_DIM, 2, dtype=jnp.float32) / ROT_DIM)
    ang = positions.astype(jnp.float32)[..., None] * inv_freq
    return jnp.cos(ang)[:, :, None, :], jnp.sin(ang)[:, :, None, :]


def partial_rope(x, cos, sin):
    half = ROT_DIM // 2
    x1 = x[..., :half].astype(jnp.float32)
    x2 = x[..., half:ROT_DIM].astype(jnp.float32)
    r1 = (x1 * cos - x2 * sin).astype(x.dtype)
    r2 = (x2 * cos + x1 * sin).astype(x.dtype)
    return jnp.concatenate([r1, r2, x[..., ROT_DIM:]], axis=-1)


def moba_attention(q, k, v):
    B_, H_, S_, dh = q.shape
    nb = -(-S_ // MOBA_BLOCK)
    pad = nb * MOBA_BLOCK - S_
    kp = jnp.pad(k, ((0, 0), (0, 0), (0, pad), (0, 0)))
    vp = jnp.pad(v, ((0, 0), (0, 0), (0, pad), (0, 0)))
    kb = kp.reshape(B_, H_, nb, MOBA_BLOCK, dh)
    vb = vp.reshape(B_, H_, nb, MOBA_BLOCK, dh)
    k_mean = jnp.mean(kb.astype(jnp.float32), axis=3)
    q_blk = jnp.arange(S_) // MOBA_BLOCK
    gate = jnp.einsum('bhsd,bhnd->bhsn', q.astype(jnp.float32), k_mean)
    past = jnp.arange(nb)[None, :] < q_blk[:, None]
    gate = jnp.where(past, gate, NEG)
    n_sel = min(MOBA_TOPK, nb)
    _, sel_idx = lax.top_k(gate, n_sel)
    sel_valid = jnp.arange(n_sel)[None, :] < q_blk[:, None]
    scale = HEAD_DIM ** -0.5
    bi = jnp.arange(B_)[:, None, None, None]
    hi = jnp.arange(H_)[None, :, None, None]
    qpos_local = jnp.arange(MOBA_QCHUNK)
    kpos_local = jnp.arange(MOBA_BLOCK)

    def chunk(c):
        q0 = c * MOBA_QCHUNK
        qc = lax.dynamic_slice_in_dim(q, q0, MOBA_QCHUNK, axis=2)
        idx = lax.dynamic_slice_in_dim(sel_idx, q0, MOBA_QCHUNK, axis=2)
        valid = lax.dynamic_slice_in_dim(sel_valid, q0, MOBA_QCHUNK, axis=0)
        k_g = kb[bi, hi, idx]
        v_g = vb[bi, hi, idx]
        l_sel = jnp.einsum('bhqd,bhqnpd->bhqnp', qc, k_g).astype(jnp.float32) * scale
        l_sel = jnp.where(valid[None, None, :, :, None], l_sel, NEG)
        l_sel = l_sel.reshape(B_, H_, MOBA_QCHUNK, n_sel * MOBA_BLOCK)
        own0 = (q0 // MOBA_BLOCK) * MOBA_BLOCK
        k_own = lax.dynamic_slice_in_dim(kp, own0, MOBA_BLOCK, axis=2)
        v_own = lax.dynamic_slice_in_dim(vp, own0, MOBA_BLOCK, axis=2)
        l_own = jnp.einsum('bhqd,bhpd->bhqp', qc, k_own).astype(jnp.float32) * scale
        causal = (own0 + kpos_local)[None, :] <= (q0 + qpos_local)[:, None]
        l_own = jnp.where(causal, l_own, NEG)
        p = jax.nn.softmax(jnp.concatenate([l_sel, l_own], axis=-1), axis=-1).astype(v.dtype)
        p_sel = p[..., :n_sel * MOBA_BLOCK].reshape(B_, H_, MOBA_QCHUNK, n_sel, MOBA_BLOCK)
        p_own = p[..., n_sel * MOBA_BLOCK:]
        return (jnp.einsum('bhqnp,bhqnpd->bhqd', p_sel, v_g)
                + jnp.einsum('bhqp,bhpd->bhqd', p_own, v_own))

    out = lax.map(chunk, jnp.arange(S_ // MOBA_QCHUNK))
    return out.transpose(1, 2, 0, 3, 4).reshape(B_, H_, S_, dh)


def swa_sink_attention(q, k, v, sinks):
    B_, Hq, S_, dh = q.shape
    G = Hq // N_KV_HEADS
    nq = S_ // WINDOW
    qb = q.reshape(B_, N_KV_HEADS, G, nq, WINDOW, dh)
    kb = k.reshape(B_, N_KV_HEADS, nq, WINDOW, dh)
    vb = v.reshape(B_, N_KV_HEADS, nq, WINDOW, dh)
    kcat = jnp.concatenate([jnp.pad(kb, ((0, 0), (0, 0), (1, 0), (0, 0), (0, 0)))[:, :, :-1], kb], axis=3)
    vcat = jnp.concatenate([jnp.pad(vb, ((0, 0), (0, 0), (1, 0), (0, 0), (0, 0)))[:, :, :-1], vb], axis=3)
    logits = jnp.einsum('bkgnqd,bknpd->bkgnqp', qb, kcat).astype(jnp.float32) * (HEAD_DIM ** -0.5)
    rel = (jnp.arange(WINDOW)[:, None] + WINDOW) - jnp.arange(2 * WINDOW)[None, :]
    band = (rel >= 0) & (rel < WINDOW)
    key_pos = jnp.arange(nq)[:, None, None] * WINDOW - WINDOW + jnp.arange(2 * WINDOW)[None, None, :]
    mask = band[None] & (key_pos >= 0)
    logits = jnp.where(mask, logits, NEG)
    sink = sinks.astype(jnp.float32).reshape(N_KV_HEADS, G)[None, :, :, None, None, None]
    m = jnp.maximum(jnp.max(logits, axis=-1, keepdims=True), sink)
    e = jnp.exp(logits - m)
    p = e / (jnp.sum(e, axis=-1, keepdims=True) + jnp.exp(sink - m))
    out = jnp.einsum('bkgnqp,bknpd->bkgnqd', p.astype(v.dtype), vcat)
    return out.reshape(B_, Hq, S_, dh)


def conv_ffn(h, w_up, conv_w, conv_b, w_down):
    S_ = h.shape[1]
    u = h @ w_up
    up = jnp.pad(u, ((0, 0), (CONV_W - 1, 0), (0, 0)))
    u = conv_b + sum(conv_w[j] * up[:, j:j + S_] for j in range(CONV_W))
    gate, val = jnp.split(u, 2, axis=-1)
    return (jax.nn.silu(gate) * val) @ w_down


def setup_inputs(seed: int = 0) -> dict:
    key = jax.random.key(seed)
    ks = jax.random.split(key, 20)
    D = D_MODEL
    kv_w = 2 * N_KV_HEADS * HEAD_DIM
    res_scale = (2 * DEPTH) ** -0.5

    def nrm(k, shape, fan_in, extra=1.0):
        return jax.random.normal(k, shape, jnp.float32) * (fan_in ** -0.5) * extra

    def gain(k, shape):
        return 1.0 + 0.02 * jax.random.normal(k, shape, jnp.float32)

    offset = jax.random.randint(ks[1], (BATCH, 1), 0, 1024, dtype=jnp.int32)
    positions = offset + jnp.arange(SEQ, dtype=jnp.int32)[None, :]
    return {
        'x': jax.random.normal(ks[0], (BATCH, SEQ, D), jnp.float32),
        'positions': positions,
        'attn_norm': gain(ks[2], (DEPTH, D)),
        'w_qkv_a': nrm(ks[3], (N_A, D, 3 * N_HEADS * HEAD_DIM), D),
        'w_o_a': nrm(ks[4], (N_A, N_HEADS * HEAD_DIM, D), N_HEADS * HEAD_DIM, res_scale),
        'kv_norm': gain(ks[5], (D,)),
        'w_kv_b': nrm(ks[6], (D, kv_w), D),
        'w_q_b': nrm(ks[7], (N_B, D, N_HEADS * HEAD_DIM), D),
        'sinks_b': 0.5 * jax.random.normal(ks[8], (N_B, N_HEADS), jnp.float32),
        'w_o_b': nrm(ks[9], (N_B, N_HEADS * HEAD_DIM, D), N_HEADS * HEAD_DIM, res_scale),
        'ffn_norm': gain(ks[10], (DEPTH, D)),
        'w_up': nrm(ks[11], (DEPTH, D, 2 * D_FF), D),
        'conv_w': nrm(ks[12], (DEPTH, CONV_W, 2 * D_FF), CONV_W),
        'conv_b': 0.01 * jax.random.normal(ks[13], (DEPTH, 2 * D_FF), jnp.float32),
        'w_down': nrm(ks[14], (DEPTH, D_FF, D), D_FF, res_scale),
        'final_norm': gain(ks[15], (D,)),
    }


def reference(x, positions, attn_norm, w_qkv_a, w_o_a, kv_norm, w_kv_b, w_q_b, sinks_b,
              w_o_b, ffn_norm, w_up, conv_w, conv_b, w_down, final_norm):
    B_, S_, _ = x.shape
    cos, sin = rope_tables(positions)

    def heads(t, n):
        return t.reshape(B_, S_, n, HEAD_DIM)

    def to_bhsd(t):
        return t.transpose(0, 2, 1, 3)

    shared_k = None
    shared_v = None
    for l in range(DEPTH):
        if l < N_A:
            h = rmsnorm(x, attn_norm[l])
            q, k, v = jnp.split(h @ w_qkv_a[l], 3, axis=-1)
            q = partial_rope(heads(q, N_HEADS), cos, sin)
            k = partial_rope(heads(k, N_HEADS), cos, sin)
            o = moba_attention(to_bhsd(q), to_bhsd(k), to_bhsd(heads(v, N_HEADS)))
            w_o = w_o_a[l]
        else:
            if l == N_A:
                hk = rmsnorm(x, kv_norm)
                k, v = jnp.split(hk @ w_kv_b, 2, axis=-1)
                shared_k = to_bhsd(partial_rope(heads(k, N_KV_HEADS), cos, sin))
                shared_v = to_bhsd(heads(v, N_KV_HEADS))
            i = l - N_A
            h = rmsnorm(x, attn_norm[l])
            q = partial_rope(heads(h @ w_q_b[i], N_HEADS), cos, sin)
            o = swa_sink_attention(to_bhsd(q), shared_k, shared_v, sinks_b[i])
            w_o = w_o_b[i]
        x = x + o.transpose(0, 2, 1, 3).reshape(B_, S_, N_HEADS * HEAD_DIM) @ w_o
        x = x + conv_ffn(rmsnorm(x, ffn_norm[l]), w_up[l], conv_w[l], conv_b[l], w_down[l])
    return rmsnorm(x, final_norm)
```

```python
import numpy as np
import ml_dtypes
import contextlib
import concourse.bass as bass
import concourse.mybir as mybir
from concourse.bass_utils import run_bass_kernel_spmd

F32 = mybir.dt.float32
BF16 = mybir.dt.bfloat16
I32 = mybir.dt.int32
U8 = mybir.dt.uint8
AF = mybir.ActivationFunctionType
ALU = mybir.AluOpType
AX = mybir.AxisListType

D = 1024
NH = 16
HD = 64
NKV = 4
DFF = 2816
NPAIR = DFF // 128
BLK = 256
WIN = 128
EPS = 1e-6
THETA = 500000.0
NEGB = -30000.0
TWO_PI = 6.283185307179586
C1 = 6.28125
C2 = TWO_PI - C1
ARENA_BYTES = 188 * 1024


class Buf:
    __slots__ = ("w", "r")

    def __init__(self):
        self.w = None
        self.r = {}


def bufs(n):
    return [Buf() for _ in range(n)]


class Prog:
    ENG = ("pe", "act", "dve", "pool", "sp")

    def __init__(self, nc, stack):
        self.nc = nc
        self.streams = {k: [] for k in self.ENG}
        self.sem = {}
        self.cnt = {}
        for k in ("pe", "act", "dve", "pool"):
            self._mk("s_" + k, stack)
        self.dsem = {"sp": [], "pool": [], "act": []}
        for k, n in (("sp", 14), ("pool", 6), ("act", 2)):
            for i in range(n):
                nm = "d_%s%d" % (k, i)
                self._mk(nm, stack)
                self.dsem[k].append(nm)
        self.drr = {"sp": 0, "pool": 0, "act": 0}
        self.waited = {k: {} for k in self.ENG}
        self.ninstr = 0

    def _mk(self, name, stack):
        self.sem[name] = stack.enter_context(self.nc.semaphore(name))
        self.cnt[name] = 0

    def wait(self, eng, tok):
        if tok is None:
            return
        name, val = tok
        if eng == "pe" and name == "s_pe":
            return
        if self.waited[eng].get(name, 0) >= val:
            return
        self.waited[eng][name] = val
        self.streams[eng].append(("w", name, val))

    def _deps(self, eng, reads, writes):
        for b in reads:
            self.wait(eng, b.w)
        for b in writes:
            self.wait(eng, b.w)
            for it in b.r.items():
                self.wait(eng, it)

    def _commit(self, tok, reads, writes):
        for b in reads:
            if b.r.get(tok[0], 0) < tok[1]:
                b.r[tok[0]] = tok[1]
        for b in writes:
            b.w = tok
            b.r = {}

    def op(self, eng, fn, reads=(), writes=()):
        self._deps(eng, reads, writes)
        name = "s_" + eng
        self.cnt[name] += 1
        tok = (name, self.cnt[name])
        self.streams[eng].append(("i", fn, name, 1))
        self._commit(tok, reads, writes)
        self.ninstr += 1
        return tok

    def pe_group(self, fns, reads=(), writes=()):
        self._deps("pe", reads, writes)
        for fn in fns[:-1]:
            self.streams["pe"].append(("i", fn, None, 0))
        self.cnt["s_pe"] += 1
        tok = ("s_pe", self.cnt["s_pe"])
        self.streams["pe"].append(("i", fns[-1], "s_pe", 1))
        self._commit(tok, reads, writes)
        self.ninstr += len(fns)
        return tok

    def dma(self, eng, out, in_, reads=(), writes=()):
        self._deps(eng, reads, writes)
        lst = self.dsem[eng]
        name = lst[self.drr[eng] % len(lst)]
        self.drr[eng] += 1
        if self.cnt[name] > 0:
            self.wait(eng, (name, self.cnt[name]))
        self.cnt[name] += 16
        tok = (name, self.cnt[name])
        self.streams[eng].append(("i", (lambda e, o=out, i=in_: e.dma_start(out=o, in_=i)), name, 16))
        self._commit(tok, reads, writes)
        self.ninstr += 1
        return tok

    def barrier(self):
        for eng in self.ENG:
            for name, c in self.cnt.items():
                if c > 0:
                    self.wait(eng, (name, c))

    def run(self, eng, e):
        sem = self.sem
        for it in self.streams[eng]:
            if it[0] == "w":
                e.wait_ge(sem[it[1]], it[2])
            else:
                ins = it[1](e)
                if it[2] is not None:
                    ins.then_inc(sem[it[2]], it[3])


class Arena:
    def __init__(self, ap, nbytes):
        self.ap = ap
        self.nbytes = nbytes
        self.off = 0

    def reset(self, off=0):
        self.off = off

    def alloc(self, shape, dtype, parts=128):
        esz = {F32: 4, BF16: 2, I32: 4, U8: 1}[dtype]
        n = 1
        for s in shape:
            n *= s
        nb = (n * esz + 63) // 64 * 64
        assert self.off + nb <= self.nbytes, ("arena overflow", self.off, nb, self.nbytes)
        a = self.ap[0:parts, self.off:self.off + n * esz]
        self.off += nb
        if dtype != U8:
            a = a.bitcast(dtype)
        if len(shape) == 2:
            a = a.rearrange("p (a b) -> p a b", a=shape[0])
        elif len(shape) == 3:
            a = a.rearrange("p (a b c) -> p a b c", a=shape[0], b=shape[1])
        return a


def mm(out, lhsT, rhs, start=True, stop=True):
    return lambda e: e.matmul(out, lhsT=lhsT, rhs=rhs, start=start, stop=stop)


def TT(o, a, b, op):
    return lambda e: e.tensor_tensor(out=o, in0=a, in1=b, op=op)


def TS(o, a, s1, s2, op0, op1=None):
    if op1 is None:
        return lambda e: e.tensor_scalar(out=o, in0=a, scalar1=s1, scalar2=None, op0=op0)
    return lambda e: e.tensor_scalar(out=o, in0=a, scalar1=s1, scalar2=s2, op0=op0, op1=op1)


def STT(o, a, s, b, op0, op1):
    return lambda e: e.scalar_tensor_tensor(out=o, in0=a, scalar=s, in1=b, op0=op0, op1=op1)


def TR(o, a, op):
    return lambda e: e.tensor_reduce(out=o, in_=a, axis=AX.X, op=op)


def RCP(o, a):
    return lambda e: e.reciprocal(out=o, in_=a)


def CP(o, a):
    return lambda e: e.tensor_copy(out=o, in_=a)


def ACP(o, a):
    return lambda e: e.copy(out=o, in_=a)


def ACTV(o, a, f, **kw):
    return lambda e: e.activation(out=o, in_=a, func=f, **kw)


def MS(o, v):
    return lambda e: e.memset(o, v)


def TSS(o, a, s, op):
    return lambda e: e.tensor_single_scalar(out=o, in_=a, scalar=s, op=op)


def TP(o, a, ident):
    return lambda e: e.transpose(o, a, ident)


def build_program(T, debug=False):
    assert T % 512 == 0
    NT = T // 512
    NB = T // BLK
    NS = T // 128
    assert NB <= 16
    nc = bass.Bass("TRN2", target_bir_lowering=False)

    def din(name, shape, dt):
        return nc.dram_tensor(name, shape, dt, kind="ExternalInput").ap()

    def dscr(name, shape, dt):
        return nc.dram_tensor(name, shape, dt, kind=("ExternalOutput" if debug else "Internal")).ap()

    xT_in = din("xT", [D, T], F32)
    pos_in = din("pos", [1, T], I32)
    gains_in = din("gains", [128, 80], F32)
    invf_in = din("invf", [128, 1], F32)
    perm_in = din("perm", [128, 128], BF16)
    identb_in = din("identb", [128, 128], BF16)
    identf_in = din("identf", [128, 128], F32)
    erows_in = din("erows", [16, T], BF16)
    pastb_in = din("pastb", [128, NS * 16], F32)
    past01_in = din("past01", [128, NS * 16], F32)
    own01_in = din("own01", [128, NS * 16], F32)
    causal_in = din("causal2", [128, 512], BF16)
    bandp_in = din("bandprev", [128, 512], BF16)
    bando_in = din("bandown", [128, 512], BF16)
    wqkv_in = din("wqkv", [2, D, 3 * D], F32)
    woa_in = din("woa", [2, D, D], F32)
    wkvb_in = din("wkvb", [D, 512], F32)
    wqb_in = din("wqb", [2, D, D], F32)
    wob_in = din("wob", [2, D, D], F32)
    sinks_in = din("sinks", [1, 32], F32)
    wup_in = din("wup", [4, 2 * NPAIR, 128, 8 * 128], F32)
    wdn_in = din("wdn", [4, DFF, D], F32)
    cw_in = din("cw", [4, 128, 3 * 2 * NPAIR], F32)
    cb_in = din("cb", [4, 128, 2 * NPAIR], F32)
    yT_out = nc.dram_tensor("yT", [D, T], F32, kind="ExternalOutput").ap()

    Xs = dscr("Xs", [D, T], F32)
    Ms = dscr("Ms", [D, T], F32)
    qTs = dscr("qTs", [D, T], BF16)
    kTs = dscr("kTs", [D, T], BF16)
    vs = dscr("vs", [T, D], BF16)
    oTs = dscr("oTs", [D, T], BF16)
    aTs = dscr("aTs", [DFF, T], BF16)
    Cs = dscr("Cs", [128, T], F32)
    Ss = dscr("Ss", [128, T], F32)
    kTsh = dscr("kTsh", [256, T], BF16)
    vsh = dscr("vsh", [T, 256], BF16)

    stack = contextlib.ExitStack()
    with stack:
        P = Prog(nc, stack)
        arena_t = stack.enter_context(nc.sbuf_tensor("arena", [128, ARENA_BYTES], U8))
        A = Arena(arena_t, ARENA_BYTES)
        banks = [stack.enter_context(nc.psum_tensor("bank%d" % i, [128, 512], F32)) for i in range(8)]
        BK = [b[:, :] for b in banks]
        bkb = bufs(8)

        dX = bufs(NT); dM = bufs(NT)
        dq = Buf(); dk = Buf(); dv = Buf(); do = Buf(); da = Buf(); dcs = Buf()
        dksh = Buf(); dvsh = Buf()

        gains = A.alloc([80], F32)
        invf = A.alloc([1], F32)
        perm = A.alloc([128], BF16)
        identb = A.alloc([128], BF16)
        identf = A.alloc([128], F32)
        onesf = A.alloc([128], F32)
        cwt = A.alloc([4 * 3 * 2 * NPAIR], F32)
        cbt = A.alloc([4 * 2 * NPAIR], F32)
        sinkt = A.alloc([32], F32)
        esink = A.alloc([32], F32)
        cbuf = Buf()
        for dst, src_ in ((gains, gains_in[:, :]), (invf, invf_in[:, :]), (perm, perm_in[:, :]),
                          (identb, identb_in[:, :]), (identf, identf_in[:, :]),
                          (sinkt, sinks_in[0:1, :].broadcast_to([128, 32]))):
            P.dma("sp", dst, src_, writes=[cbuf])
        NCW = 3 * 2 * NPAIR
        NCB = 2 * NPAIR
        for l in range(4):
            P.dma("sp", cwt[:, l * NCW:(l + 1) * NCW], cw_in[l, :, :], writes=[cbuf])
            P.dma("sp", cbt[:, l * NCB:(l + 1) * NCB], cb_in[l, :, :], writes=[cbuf])
        P.op("dve", MS(onesf, 1.0), writes=[cbuf])
        P.op("act", ACTV(esink, sinkt, AF.Exp), reads=[cbuf], writes=[cbuf])
        P.barrier()
        BASE = A.off

        def stage_rope():
            A.reset(BASE)
            W = 512
            pi_ = A.alloc([W], I32); ang = A.alloc([W], F32); t_ = A.alloc([W], F32)
            ki = A.alloc([W], I32); kf = A.alloc([W], F32); r_ = A.alloc([W], F32)
            m_ = A.alloc([W], F32); rc = A.alloc([W], F32)
            so = A.alloc([W], F32); co = A.alloc([W], F32)
            b = Buf()
            PI = float(np.pi)

            def D_(fn):
                P.op("dve", fn, reads=[b, cbuf], writes=[b])

            def fix(rr):
                D_(TSS(m_, rr, PI, ALU.is_gt))
                D_(STT(rr, m_, -TWO_PI, rr, ALU.mult, ALU.add))
                D_(TSS(m_, rr, -PI, ALU.is_lt))
                D_(STT(rr, m_, TWO_PI, rr, ALU.mult, ALU.add))
                D_(TS(rr, rr, 3.141592, -3.141592, ALU.min, ALU.max))

            for ci in range(T // W):
                cs = slice(ci * W, (ci + 1) * W)
                P.dma("sp", pi_, pos_in[0:1, cs].broadcast_to([128, W]), writes=[b])
                D_(CP(ang, pi_))
                D_(TS(ang, ang, invf[:, 0:1], None, ALU.mult))
                D_(TS(t_, ang, 1.0 / TWO_PI, None, ALU.mult))
                D_(CP(ki, t_))
                D_(CP(kf, ki))
                D_(STT(r_, kf, -C1, ang, ALU.mult, ALU.add))
                D_(STT(r_, kf, -C2, r_, ALU.mult, ALU.add))
                fix(r_)
                D_(TS(rc, r_, PI / 2, None, ALU.add))
                fix(rc)
                P.op("act", ACTV(so, r_, AF.Sin), reads=[b], writes=[b])
                P.op("act", ACTV(co, rc, AF.Sin), reads=[b], writes=[b])
                P.dma("sp", Ss[:, cs], so, reads=[b], writes=[dcs])
                P.dma("sp", Cs[:, cs], co, reads=[b], writes=[dcs])
            P.barrier()

        def norm_rstd(xt, xtb, sq, sqb, ssbank, rstd, rstdb):
            P.op("act", ACTV(sq, xt, AF.Square), reads=[xtb], writes=[sqb])
            P.pe_group([mm(BK[ssbank], onesf, sq[:, c, :], start=(c == 0), stop=(c == 7)) for c in range(8)],
                       reads=[sqb, cbuf], writes=[bkb[ssbank]])
            P.op("act", ACTV(rstd, BK[ssbank], AF.Sqrt, scale=1.0 / D, bias=EPS), reads=[bkb[ssbank]], writes=[rstdb])
            P.op("dve", RCP(rstd, rstd), reads=[rstdb], writes=[rstdb])

        def norm_apply(xt, xtb, rstd, rstdb, gi, outs, outb):
            for c in range(8):
                P.op("dve", STT(outs[c], xt[:, c, :], gains[:, gi * 8 + c:gi * 8 + c + 1], rstd, ALU.mult, ALU.mult),
                     reads=[xtb, rstdb, cbuf], writes=[outb])

        def load_w(dst, src_ap, wb):
            P.dma("pool", dst, src_ap.rearrange("(kc p) n -> p kc n", p=128), writes=[wb])

        def stage_proj(l, Xsrc, dXsrc):
            A.reset(BASE)
            moba = l < 2
            mk_kv = (l == 2)
            wb = Buf()
            wkv = None
            if moba:
                wsb = A.alloc([8, 3 * D], BF16)
                load_w(wsb, wqkv_in[l], wb)
                nqk = 16
            else:
                wsb = A.alloc([8, D], BF16)
                load_w(wsb, wqb_in[l - 2], wb)
                nqk = 8
                if mk_kv:
                    wkv = A.alloc([8, 512], BF16)
                    load_w(wkv, wkvb_in, wb)
            xt = [A.alloc([8, 512], F32) for _ in range(2)]; xtb = bufs(2)
            ct = [A.alloc([512], F32) for _ in range(2)]; st = [A.alloc([512], F32) for _ in range(2)]; csb = bufs(2)
            sq = A.alloc([8, 512], F32); sqb = Buf()
            rstd = A.alloc([512], F32); rstdb = Buf()
            hT = [A.alloc([8, 512], BF16) for _ in range(2)]; hTb = bufs(2)
            hK = None; hKb = None
            if mk_kv:
                hK = A.alloc([8, 512], BF16); hKb = Buf()
            qb = [A.alloc([512], BF16) for _ in range(2)]; qbb = bufs(2)
            t1 = [A.alloc([512], F32) for _ in range(2)]; t1b = bufs(2)
            u1 = [A.alloc([512], F32) for _ in range(2)]; u1b = bufs(2)
            qr = [A.alloc([512], BF16) for _ in range(3)]; qrb = bufs(3)
            vt = [A.alloc([4, 1024], BF16) for _ in range(2)]; vtb = bufs(2)
            SSB, PA, PB, PV = 0, (1, 2), (3, 4), (5, 6)

            def load(i):
                b = i % 2
                cs = slice(i * 512, (i + 1) * 512)
                P.dma("sp", xt[b], Xsrc[:, cs].rearrange("(c p) n -> p c n", p=128), reads=[dXsrc[i]], writes=[xtb[b]])
                P.dma("sp", ct[b], Cs[:, cs], reads=[dcs], writes=[csb[b]])
                P.dma("sp", st[b], Ss[:, cs], reads=[dcs], writes=[csb[b]])

            state = {"k": 0, "kv": 0}

            def qk_chunk(b, wts, hsrc, hsrcb, dst_ap, dbuf):
                k = state["k"]; state["k"] += 1
                pa = PA[k % 2]; pb = PB[k % 2]; r2 = k % 2; r3 = k % 3
                P.pe_group([mm(BK[pa], wts[kc], hsrc[:, kc, :], start=(kc == 0), stop=(kc == 7)) for kc in range(8)],
                           reads=[wb, hsrcb], writes=[bkb[pa]])
                P.op("act", ACP(qb[r2], BK[pa]), reads=[bkb[pa]], writes=[qbb[r2]])
                P.pe_group([mm(BK[pb], perm, qb[r2])], reads=[qbb[r2], cbuf], writes=[bkb[pb]])
                P.op("pool", TT(t1[r2], qb[r2], ct[b], ALU.mult), reads=[qbb[r2], csb[b]], writes=[t1b[r2]])
                P.op("dve", TT(u1[r2], BK[pb], st[b], ALU.mult), reads=[bkb[pb], csb[b]], writes=[u1b[r2]])
                P.op("pool", TT(qr[r3], u1[r2], t1[r2], ALU.add), reads=[u1b[r2], t1b[r2]], writes=[qrb[r3]])
                P.dma("sp", dst_ap, qr[r3], reads=[qrb[r3]], writes=[dbuf])

            load(0)
            for i in range(NT):
                b = i % 2
                cs = slice(i * 512, (i + 1) * 512)
                if i + 1 < NT:
                    load(i + 1)
                norm_rstd(xt[b], xtb[b], sq, sqb, SSB, rstd, rstdb)
                norm_apply(xt[b], xtb[b], rstd, rstdb, l, [hT[b][:, c, :] for c in range(8)], hTb[b])
                if mk_kv:
                    norm_apply(xt[b], xtb[b], rstd, rstdb, 8, [hK[:, c, :] for c in range(8)], hKb)
                for m in range(nqk):
                    if m < 8:
                        dst, dbuf = qTs[m * 128:(m + 1) * 128, cs], dq
                    else:
                        dst, dbuf = kTs[(m - 8) * 128:(m - 7) * 128, cs], dk
                    qk_chunk(b, [wsb[:, kc, m * 128:(m + 1) * 128] for kc in range(8)], hT[b], hTb[b], dst, dbuf)
                if mk_kv:
                    for m in range(2):
                        qk_chunk(b, [wkv[:, kc, m * 128:(m + 1) * 128] for kc in range(8)], hK, hKb,
                                 kTsh[m * 128:(m + 1) * 128, cs], dksh)
                if moba or mk_kv:
                    vb = i % 2
                    ngrp = 2 if moba else 1
                    for s in range(4):
                        for g in range(ngrp):
                            pv = PV[state["kv"] % 2]; state["kv"] += 1
                            if moba:
                                fns = [mm(BK[pv], hT[b][:, kc, s * 128:(s + 1) * 128], wsb[:, kc, 2048 + g * 512:2048 + (g + 1) * 512],
                                          start=(kc == 0), stop=(kc == 7)) for kc in range(8)]
                                P.pe_group(fns, reads=[wb, hTb[b]], writes=[bkb[pv]])
                                P.op("act", ACP(vt[vb][:, s, g * 512:(g + 1) * 512], BK[pv]), reads=[bkb[pv]], writes=[vtb[vb]])
                            else:
                                fns = [mm(BK[pv][:, 0:256], hK[:, kc, s * 128:(s + 1) * 128], wkv[:, kc, 256:512],
                                          start=(kc == 0), stop=(kc == 7)) for kc in range(8)]
                                P.pe_group(fns, reads=[wb, hKb], writes=[bkb[pv]])
                                P.op("act", ACP(vt[vb][:, s, 0:256], BK[pv][:, 0:256]), reads=[bkb[pv]], writes=[vtb[vb]])
                    if moba:
                        P.dma("sp", vs[cs, :].rearrange("(s p) f -> p s f", p=128), vt[vb], reads=[vtb[vb]], writes=[dv])
                    else:
                        P.dma("sp", vsh[cs, :].rearrange("(s p) f -> p s f", p=128), vt[vb][:, :, 0:256], reads=[vtb[vb]], writes=[dvsh])
            P.barrier()

        def stage_moba():
            A.reset(BASE)
            G = NS * 16
            pastb = A.alloc([G], F32); past01 = A.alloc([G], F32); own01 = A.alloc([G], F32)
            causal = A.alloc([2, 256], BF16)
            tb = Buf()
            P.dma("sp", pastb, pastb_in[:, :], writes=[tb]); P.dma("sp", past01, past01_in[:, :], writes=[tb])
            P.dma("sp", own01, own01_in[:, :], writes=[tb])
            P.dma("sp", causal, causal_in[:, :].rearrange("p (a b) -> p a b", a=2), writes=[tb])
            Kaug = [A.alloc([T], BF16) for _ in range(2)]; Kb = bufs(2)
            Qaug = [A.alloc([T], BF16) for _ in range(2)]; Qhb = bufs(2); Qlb = bufs(2)
            Vh = [A.alloc([NS, 128], BF16) for _ in range(2)]; Vb = bufs(2)
            oh = [A.alloc([T], BF16, parts=64) for _ in range(2)]; ohb = bufs(2)
            km = A.alloc([16], F32); kmb16 = A.alloc([16], BF16); kmb = Buf()
            gm = A.alloc([G], F32); g2 = A.alloc([G], F32); eq = A.alloc([G], F32); sel = A.alloc([G], F32)
            mx = A.alloc([NS], F32); mbb = A.alloc([G], BF16); gb = Buf()
            Pt = [A.alloc([512], BF16) for _ in range(3)]; Ptb = bufs(3)
            rden = A.alloc([256], F32); rdb = Buf()
            GB, TB_, SB, OB = 0, 1, (2, 3, 4), (5, 6)
            tbk = BK[TB_].bitcast(BF16)

            def g3(ap):
                return ap.rearrange("p (a b) -> p a b", b=16)

            def bc(ap):
                return ap.unsqueeze(2).to_broadcast([128, NS, 16])

            P.op("dve", MS(km, 0.0), writes=[kmb])
            for b in range(2):
                P.dma("sp", Kaug[b][0:16, :], erows_in[:, :], writes=[Kb[b]])
                P.op("dve", MS(Qaug[b][0:16, :], 0.0), writes=[Qlb[b]])
                P.op("dve", MS(Vh[b][:, :, 64:128], 1.0), writes=[Vb[b]])

            def load(h):
                b = h % 2
                P.dma("sp", Kaug[b][16:80, :], kTs[h * 64:(h + 1) * 64, :], reads=[dk], writes=[Kb[b]])
                P.dma("sp", Qaug[b][16:80, :], qTs[h * 64:(h + 1) * 64, :], reads=[dq], writes=[Qhb[b]])
                P.dma("sp", Vh[b][:, :, 0:64], vs[:, h * 64:(h + 1) * 64].rearrange("(c p) d -> p c d", p=128), reads=[dv], writes=[Vb[b]])

            def GD(fn, extra=()):
                P.op("dve", fn, reads=[gb] + list(extra), writes=[gb])

            load(0)
            sk = 0
            for h in range(NH):
                b = h % 2
                if h + 1 < NH:
                    load(h + 1)
                P.op("dve", TR(km[0:80, 0:NB], Kaug[b][0:80, :].rearrange("p (n k) -> p n k", k=BLK), ALU.add), reads=[Kb[b]], writes=[kmb])
                P.op("dve", MS(km[0:16, :], 0.0), reads=[kmb], writes=[kmb])
                P.op("dve", TS(kmb16[0:80, :], km[0:80, :], 1.0 / BLK, None, ALU.mult), reads=[kmb], writes=[kmb])
                P.pe_group([mm(BK[GB][:, s * 16:(s + 1) * 16], Qaug[b][0:80, s * 128:(s + 1) * 128], kmb16[0:80, :]) for s in range(NS)],
                           reads=[Qhb[b], Qlb[b], kmb], writes=[bkb[GB]])
                P.op("dve", TT(gm, BK[GB][:, 0:G], pastb, ALU.add), reads=[bkb[GB], tb], writes=[gb])
                GD(TR(mx, g3(gm), ALU.max))
                GD(TT(g3(eq), g3(gm), bc(mx), ALU.is_equal))
                GD(STT(g2, eq, -1e9, gm, ALU.mult, ALU.add))
                GD(TR(mx, g3(g2), ALU.max))
                GD(TT(g3(eq), g3(g2), bc(mx), ALU.is_equal))
                GD(STT(g2, eq, -1e9, g2, ALU.mult, ALU.add))
                GD(TR(mx, g3(g2), ALU.max))
                GD(TT(g3(sel), g3(gm), bc(mx), ALU.is_ge))
                GD(TT(sel, sel, past01, ALU.mult), [tb])
                GD(TT(sel, sel, own01, ALU.add), [tb])
                GD(TS(mbb, sel, -NEGB, NEGB, ALU.mult, ALU.add))
                for grp in range((NS + 7) // 8):
                    n8 = min(8, NS - grp * 8)
                    P.pe_group([TP(tbk[0:16, j * 128:(j + 1) * 128], mbb[:, s * 16:(s + 1) * 16], identb)
                                for j, s in enumerate(range(grp * 8, grp * 8 + n8))],
                               reads=[gb, cbuf], writes=[bkb[TB_]])
                    P.op("act", ACP(Qaug[b][0:16, grp * 1024:grp * 1024 + n8 * 128], tbk[0:16, 0:n8 * 128]),
                         reads=[bkb[TB_]], writes=[Qlb[b]])
                for i in range(NB):
                    ob = OB[i % 2]
                    qs = slice(i * BLK, (i + 1) * BLK)
                    nmm = 2 * (i + 1)
                    cnt = 0
                    for n in range(i + 1):
                        sb = SB[sk % 3]; pt = sk % 3; sk += 1
                        fns = []
                        for c in range(2):
                            ks = slice((2 * n + c) * 128, (2 * n + c + 1) * 128)
                            if n < i:
                                fns.append(mm(BK[sb][:, c * 256:(c + 1) * 256], Kaug[b][0:80, ks], Qaug[b][0:80, qs]))
                            else:
                                fns.append(mm(BK[sb][:, c * 256:(c + 1) * 256], Kaug[b][0:80, ks], Qaug[b][0:80, qs], start=True, stop=False))
                                fns.append(mm(BK[sb][:, c * 256:(c + 1) * 256], identb, causal[:, c, :], start=False, stop=True))
                        P.pe_group(fns, reads=[Kb[b], Qhb[b], Qlb[b], tb, cbuf], writes=[bkb[sb]])
                        P.op("act", ACTV(Pt[pt], BK[sb], AF.Exp, scale=0.125), reads=[bkb[sb]], writes=[Ptb[pt]])
                        fns = []
                        for c in range(2):
                            fns.append(mm(BK[ob][:, 0:256], Vh[b][:, 2 * n + c, :], Pt[pt][:, c * 256:(c + 1) * 256],
                                          start=(cnt == 0), stop=(cnt == nmm - 1)))
                            cnt += 1
                        P.pe_group(fns, reads=[Vb[b], Ptb[pt]], writes=[bkb[ob]])
                    P.op("dve", RCP(rden[64:128, :], BK[ob][64:128, 0:256]), reads=[bkb[ob]], writes=[rdb])
                    P.op("dve", TT(oh[b][0:64, qs], BK[ob][0:64, 0:256], rden[64:128, :], ALU.mult),
                         reads=[bkb[ob], rdb], writes=[ohb[b]])
                P.dma("sp", oTs[h * 64:(h + 1) * 64, :], oh[b], reads=[ohb[b]], writes=[do])
            P.barrier()

        def stage_swa(l):
            A.reset(BASE)
            li = l - 2
            bandp = A.alloc([512], BF16); bando = A.alloc([512], BF16)
            es = A.alloc([16, 128], F32)
            tb = Buf()
            P.dma("sp", bandp, bandp_in[:, :], writes=[tb]); P.dma("sp", bando, bando_in[:, :], writes=[tb])
            P.op("dve", CP(es, esink[:, li * 16:(li + 1) * 16].unsqueeze(2).to_broadcast([128, 16, 128])), reads=[cbuf], writes=[tb])
            Kg = [A.alloc([T], BF16, parts=64) for _ in range(2)]; Kb = bufs(2)
            Vg = [A.alloc([NS, 128], BF16) for _ in range(2)]; Vb = bufs(2)
            Qg = [A.alloc([4, T], BF16, parts=64) for _ in range(2)]; Qb = bufs(2)
            og = [A.alloc([4, T], BF16, parts=64) for _ in range(2)]; ogb = bufs(2)
            Pt = [A.alloc([512], BF16) for _ in range(4)]; Ptb = bufs(4)
            den = A.alloc([512], F32); rdb = Buf()
            SB, OB = (0, 1, 2, 3), (4, 5)
            for b in range(2):
                P.op("dve", MS(Vg[b][:, :, 64:128], 1.0), writes=[Vb[b]])

            def load(g):
                b = g % 2
                P.dma("sp", Kg[b], kTsh[g * 64:(g + 1) * 64, :], reads=[dksh], writes=[Kb[b]])
                P.dma("sp", Vg[b][:, :, 0:64], vsh[:, g * 64:(g + 1) * 64].rearrange("(c p) d -> p c d", p=128), reads=[dvsh], writes=[Vb[b]])
                for hh in range(4):
                    hd = g * 4 + hh
                    P.dma("sp", Qg[b][:, hh, :], qTs[hd * 64:(hd + 1) * 64, :], reads=[dq], writes=[Qb[b]])

            load(0)
            sk = 0
            for g in range(NKV):
                b = g % 2
                if g + 1 < NKV:
                    load(g + 1)
                for s in range(NS):
                    ob = OB[s % 2]
                    qs = slice(s * 128, (s + 1) * 128)
                    chunks = ([(s - 1, bandp)] if s > 0 else []) + [(s, bando)]
                    pts = []
                    for (kc, band) in chunks:
                        sb = SB[sk % 4]; pt = sk % 4; sk += 1
                        P.pe_group([mm(BK[sb].rearrange("p (a b) -> p a b", a=4), Kg[b][:, kc * 128:(kc + 1) * 128], Qg[b][:, :, qs], start=True, stop=False),
                                    mm(BK[sb], identb, band, start=False, stop=True)],
                                   reads=[Kb[b], Qb[b], tb, cbuf], writes=[bkb[sb]])
                        P.op("act", ACTV(Pt[pt], BK[sb], AF.Exp, scale=0.125), reads=[bkb[sb]], writes=[Ptb[pt]])
                        pts.append((kc, pt))
                    P.pe_group([mm(BK[ob], Vg[b][:, kc, :], Pt[pt], start=(j == 0), stop=(j == len(pts) - 1)) for j, (kc, pt) in enumerate(pts)],
                               reads=[Vb[b]] + [Ptb[pt] for _, pt in pts], writes=[bkb[ob]])
                    P.op("dve", TT(den[64:128, :], BK[ob][64:128, :], es[64:128, g * 4:(g + 1) * 4, :].rearrange("p a b -> p (a b)"), ALU.add),
                         reads=[bkb[ob], tb], writes=[rdb])
                    P.op("dve", RCP(den[64:128, :], den[64:128, :]), reads=[rdb], writes=[rdb])
                    P.op("dve", TT(og[b][:, :, qs], BK[ob][0:64, :].rearrange("p (a b) -> p a b", a=4),
                                   den[64:128, :].rearrange("p (a b) -> p a b", a=4), ALU.mult),
                         reads=[bkb[ob], rdb], writes=[ogb[b]])
                for hh in range(4):
                    hd = g * 4 + hh
                    P.dma("sp", oTs[hd * 64:(hd + 1) * 64, :], og[b][:, hh, :], reads=[ogb[b]], writes=[do])
            P.barrier()

        def stage_oproj_ffn_up(l, Xsrc, dXsrc):
            A.reset(BASE)
            wb = Buf()
            hall = A.alloc([8, T], BF16); hallb = Buf()
            mark = A.off
            wo = A.alloc([8, D], BF16)
            load_w(wo, (woa_in[l] if l < 2 else wob_in[l - 2]), wb)
            ot = [A.alloc([8, 512], BF16) for _ in range(2)]; otb = bufs(2)
            xt = [A.alloc([8, 512], F32) for _ in range(2)]; xtb = bufs(2)
            xm = [A.alloc([8, 512], F32) for _ in range(2)]; xmb = bufs(2)
            rstd = A.alloc([512], F32); rstdb = Buf()
            SSB, PO = 0, (1, 2, 3)

            def load(i):
                b = i % 2
                cs = slice(i * 512, (i + 1) * 512)
                P.dma("sp", ot[b], oTs[:, cs].rearrange("(c p) n -> p c n", p=128), reads=[do], writes=[otb[b]])
                P.dma("sp", xt[b], Xsrc[:, cs].rearrange("(c p) n -> p c n", p=128), reads=[dXsrc[i]], writes=[xtb[b]])

            load(0)
            k = 0
            for i in range(NT):
                b = i % 2
                cs = slice(i * 512, (i + 1) * 512)
                if i + 1 < NT:
                    load(i + 1)
                for m in range(8):
                    po = PO[k % 3]; k += 1
                    P.pe_group([mm(BK[po], wo[:, kc, m * 128:(m + 1) * 128], ot[b][:, kc, :], start=(kc == 0), stop=(kc == 7)) for kc in range(8)],
                               reads=[wb, otb[b]], writes=[bkb[po]])
                    P.op("dve", TT(xm[b][:, m, :], BK[po], xt[b][:, m, :], ALU.add), reads=[bkb[po], xtb[b]], writes=[xmb[b]])
                P.dma("sp", Ms[:, cs].rearrange("(c p) n -> p c n", p=128), xm[b], reads=[xmb[b]], writes=[dM[i]])
                norm_rstd(xm[b], xmb[b], xt[b], xtb[b], SSB, rstd, rstdb)
                norm_apply(xm[b], xmb[b], rstd, rstdb, 4 + l, [hall[:, c, cs] for c in range(8)], hallb)
            P.barrier()

            A.reset(mark)
            wg = [A.alloc([8, 128], BF16) for _ in range(2)]; wv = [A.alloc([8, 128], BF16) for _ in range(2)]; wgb = bufs(2)
            dg = [A.alloc([6, 128], BF16) for _ in range(2)]; dgb = bufs(2)
            ug = [A.alloc([514], BF16) for _ in range(2)]; uv = [A.alloc([514], BF16) for _ in range(2)]; ugb = bufs(2); uvb = bufs(2)
            sg = [A.alloc([512], F32) for _ in range(2)]; sgb = bufs(2)
            at = [A.alloc([512], BF16) for _ in range(3)]; atb = bufs(3)
            PG, PVv, PCG, PCV = (0, 1), (2, 3), (4, 5), (6, 7)

            def loadw(j):
                b = j % 2
                P.dma("pool", wg[b], wup_in[l, j].rearrange("p (kc n) -> p kc n", kc=8), writes=[wgb[b]])
                P.dma("pool", wv[b], wup_in[l, NPAIR + j].rearrange("p (kc n) -> p kc n", kc=8), writes=[wgb[b]])

            loadw(0)
            k = 0
            for j in range(NPAIR):
                b = j % 2
                if j + 1 < NPAIR:
                    loadw(j + 1)
                for tap in range(3):
                    for gv in range(2):
                        col = l * NCW + tap * 2 * NPAIR + gv * NPAIR + j
                        P.op("dve", TS(dg[b][:, tap * 2 + gv, :], identf, cwt[:, col:col + 1], None, ALU.mult), reads=[cbuf], writes=[dgb[b]])
                P.op("dve", MS(ug[0][:, 0:2], 0.0), writes=[ugb[0]])
                P.op("dve", MS(uv[0][:, 0:2], 0.0), writes=[uvb[0]])
                cg = l * NCB + j
                cv = l * NCB + NPAIR + j
                for i in range(NT):
                    cs = slice(i * 512, (i + 1) * 512)
                    r = k % 2; r3 = k % 3; k += 1
                    ub = i % 2
                    pg, pv, pcg, pcv = PG[r], PVv[r], PCG[r], PCV[r]
                    P.pe_group([mm(BK[pg], wg[b][:, kc, :], hall[:, kc, cs], start=(kc == 0), stop=(kc == 7)) for kc in range(8)],
                               reads=[wgb[b], hallb], writes=[bkb[pg]])
                    P.pe_group([mm(BK[pv], wv[b][:, kc, :], hall[:, kc, cs], start=(kc == 0), stop=(kc == 7)) for kc in range(8)],
                               reads=[wgb[b], hallb], writes=[bkb[pv]])
                    P.op("act", ACP(ug[ub][:, 2:514], BK[pg]), reads=[bkb[pg]], writes=[ugb[ub]])
                    P.op("act", ACP(uv[ub][:, 2:514], BK[pv]), reads=[bkb[pv]], writes=[uvb[ub]])
                    if i + 1 < NT:
                        P.op("pool", CP(ug[1 - ub][:, 0:2], ug[ub][:, 512:514]), reads=[ugb[ub]], writes=[ugb[1 - ub]])
                        P.op("pool", CP(uv[1 - ub][:, 0:2], uv[ub][:, 512:514]), reads=[uvb[ub]], writes=[uvb[1 - ub]])
                    P.pe_group([mm(BK[pcg], dg[b][:, tap * 2 + 0, :], ug[ub][:, tap:tap + 512], start=(tap == 0), stop=(tap == 2)) for tap in range(3)],
                               reads=[dgb[b], ugb[ub]], writes=[bkb[pcg]])
                    P.pe_group([mm(BK[pcv], dg[b][:, tap * 2 + 1, :], uv[ub][:, tap:tap + 512], start=(tap == 0), stop=(tap == 2)) for tap in range(3)],
                               reads=[dgb[b], uvb[ub]], writes=[bkb[pcv]])
                    P.op("act", ACTV(sg[r], BK[pcg], AF.Silu, bias=cbt[:, cg:cg + 1]), reads=[bkb[pcg], cbuf], writes=[sgb[r]])
                    P.op("dve", STT(at[r3], BK[pcv], cbt[:, cv:cv + 1], sg[r], ALU.add, ALU.mult),
                         reads=[bkb[pcv], sgb[r], cbuf], writes=[atb[r3]])
                    P.dma("sp", aTs[j * 128:(j + 1) * 128, cs], at[r3], reads=[atb[r3]], writes=[da])
            P.barrier()

        def stage_ffn_down(l):
            A.reset(BASE)
            last = (l == 3)
            wb = Buf()
            wd = A.alloc([NPAIR, D], BF16)
            load_w(wd, wdn_in[l], wb)
            at = [A.alloc([NPAIR, 512], BF16) for _ in range(2)]; atb = bufs(2)
            xm = [A.alloc([8, 512], F32) for _ in range(2)]; xmb = bufs(2)
            xo = [A.alloc([8, 512], F32) for _ in range(2)]; xob = bufs(2)
            rstd = A.alloc([512], F32); rstdb = Buf()
            SSB, PO = 0, (1, 2, 3)
            dyo = Buf()

            def load(i):
                b = i % 2
                cs = slice(i * 512, (i + 1) * 512)
                P.dma("sp", at[b], aTs[:, cs].rearrange("(c p) n -> p c n", p=128), reads=[da], writes=[atb[b]])
                P.dma("sp", xm[b], Ms[:, cs].rearrange("(c p) n -> p c n", p=128), reads=[dM[i]], writes=[xmb[b]])

            load(0)
            k = 0
            for i in range(NT):
                b = i % 2
                cs = slice(i * 512, (i + 1) * 512)
                if i + 1 < NT:
                    load(i + 1)
                for m in range(8):
                    po = PO[k % 3]; k += 1
                    P.pe_group([mm(BK[po], wd[:, kc, m * 128:(m + 1) * 128], at[b][:, kc, :], start=(kc == 0), stop=(kc == NPAIR - 1)) for kc in range(NPAIR)],
                               reads=[wb, atb[b]], writes=[bkb[po]])
                    P.op("dve", TT(xo[b][:, m, :], BK[po], xm[b][:, m, :], ALU.add), reads=[bkb[po], xmb[b]], writes=[xob[b]])
                if not last:
                    P.dma("sp", Xs[:, cs].rearrange("(c p) n -> p c n", p=128), xo[b], reads=[xob[b]], writes=[dX[i]])
                else:
                    norm_rstd(xo[b], xob[b], xm[b], xmb[b], SSB, rstd, rstdb)
                    norm_apply(xo[b], xob[b], rstd, rstdb, 9, [xo[b][:, c, :] for c in range(8)], xob[b])
                    P.dma("sp", yT_out[:, cs].rearrange("(c p) n -> p c n", p=128), xo[b], reads=[xob[b]], writes=[dyo])
            P.barrier()

        stage_rope()
        dXin = bufs(NT)
        for l in range(4):
            Xsrc, dXsrc = (xT_in, dXin) if l == 0 else (Xs, dX)
            stage_proj(l, Xsrc, dXsrc)
            if l < 2:
                stage_moba()
            else:
                stage_swa(l)
            stage_oproj_ffn_up(l, Xsrc, dXsrc)
            stage_ffn_down(l)
        P.barrier()

        with nc.Block() as block:
            @block.tensor
            def _(e):
                P.run("pe", e)

            @block.scalar
            def _(e):
                P.run("act", e)

            @block.vector
            def _(e):
                P.run("dve", e)

            @block.gpsimd
            def _(e):
                P.run("pool", e)

            @block.sync
            def _(e):
                P.run("sp", e)
    return nc


def host_constants(T):
    bf = ml_dtypes.bfloat16
    NS = T // 128
    p = np.arange(128)
    d = p % 64
    invf8 = (np.float32(THETA) ** (-np.arange(0, 16, 2, dtype=np.float32) / np.float32(16))).astype(np.float32)
    invf = np.where(d < 16, invf8[d % 8], np.float32(0)).astype(np.float32).reshape(128, 1)
    perm = np.zeros((128, 128), np.float32)
    for m in range(128):
        dm = m % 64
        if dm < 8:
            perm[m + 8, m] = -1.0
        elif dm < 16:
            perm[m - 8, m] = 1.0
    ident = np.eye(128, dtype=np.float32)
    erows = np.zeros((16, T), np.float32)
    for n in range(T // BLK):
        erows[n, n * BLK:(n + 1) * BLK] = 1.0
    past01 = np.zeros((128, NS, 16), np.float32)
    own01 = np.zeros((128, NS, 16), np.float32)
    for s in range(NS):
        qb = (s * 128) // BLK
        past01[:, s, :qb] = 1.0
        own01[:, s, qb] = 1.0
    pastb = (past01 - 1.0) * 1e4
    kk = np.arange(128)[:, None]
    causal2 = np.zeros((128, 2, 256), np.float32)
    qq = np.arange(256)[None, :]
    for c in range(2):
        causal2[:, c, :] = np.where(c * 128 + kk <= qq, 0.0, NEGB)
    q1 = np.arange(128)[None, :]
    bandprev = np.tile(np.where(kk > q1, 0.0, NEGB), (1, 4))
    bandown = np.tile(np.where(kk <= q1, 0.0, NEGB), (1, 4))
    return {
        "invf": invf, "perm": perm.astype(bf), "identb": ident.astype(bf), "identf": ident,
        "erows": erows.astype(bf), "pastb": pastb.reshape(128, -1).astype(np.float32),
        "past01": past01.reshape(128, -1), "own01": own01.reshape(128, -1),
        "causal2": causal2.reshape(128, 512).astype(bf), "bandprev": bandprev.astype(bf), "bandown": bandown.astype(bf),
    }


def host_weights(attn_norm, w_qkv_a, w_o_a, kv_norm, w_kv_b, w_q_b, sinks_b, w_o_b, ffn_norm, w_up, conv_w, conv_b, w_down, final_norm):
    f = np.float32
    allg = np.concatenate([np.asarray(attn_norm, f), np.asarray(ffn_norm, f), np.asarray(kv_norm, f)[None], np.asarray(final_norm, f)[None]], 0)
    gains = np.ascontiguousarray(allg.reshape(10, 8, 128).transpose(2, 0, 1).reshape(128, 80))
    wup = np.asarray(w_up, f).reshape(4, 8, 128, 2 * NPAIR, 128).transpose(0, 3, 2, 1, 4)
    wup = np.ascontiguousarray(wup).reshape(4, 2 * NPAIR, 128, 1024)
    cw = np.asarray(conv_w, f).reshape(4, 3, 2 * NPAIR, 128).transpose(0, 3, 1, 2)
    cw = np.ascontiguousarray(cw).reshape(4, 128, 3 * 2 * NPAIR)
    cb = np.ascontiguousarray(np.asarray(conv_b, f).reshape(4, 2 * NPAIR, 128).transpose(0, 2, 1))
    return {
        "gains": gains,
        "wqkv": np.ascontiguousarray(np.asarray(w_qkv_a, f)), "woa": np.ascontiguousarray(np.asarray(w_o_a, f)),
        "wkvb": np.ascontiguousarray(np.asarray(w_kv_b, f)), "wqb": np.ascontiguousarray(np.asarray(w_q_b, f)),
        "wob": np.ascontiguousarray(np.asarray(w_o_b, f)), "sinks": np.ascontiguousarray(np.asarray(sinks_b, f).reshape(1, 32)),
        "wup": wup, "wdn": np.ascontiguousarray(np.asarray(w_down, f)), "cw": cw, "cb": cb,
    }


_CACHE = {}


def run_cores(x, positions, weights, T, debug=False):
    n = x.shape[0]
    key = (T, debug)
    if key not in _CACHE:
        _CACHE[key] = build_program(T, debug)
    nc = _CACHE[key]
    consts = host_constants(T)
    shared = dict(consts)
    shared.update(weights)
    in_maps = []
    for b in range(n):
        m = dict(shared)
        m["xT"] = np.ascontiguousarray(np.asarray(x[b], np.float32).T)
        m["pos"] = np.ascontiguousarray(np.asarray(positions[b], np.int32).reshape(1, T))
        in_maps.append(m)
    res = run_bass_kernel_spmd(nc, in_maps, core_ids=list(range(n)))
    return res


def kernel(x, positions, attn_norm, w_qkv_a, w_o_a, kv_norm, w_kv_b, w_q_b, sinks_b, w_o_b, ffn_norm, w_up, conv_w, conv_b, w_down, final_norm):
    x = np.asarray(x)
    B, T, _ = x.shape
    weights = host_weights(attn_norm, w_qkv_a, w_o_a, kv_norm, w_kv_b, w_q_b, sinks_b, w_o_b, ffn_norm, w_up, conv_w, conv_b, w_down, final_norm)
    res = run_cores(x, np.asarray(positions), weights, T)
    out = np.stack([np.ascontiguousarray(r["yT"].T) for r in res.results], 0)
    return out.astype(np.float32)
```

```python
import numpy as np
import ml_dtypes
import contextlib
import concourse.bass as bass
import concourse.mybir as mybir
from concourse.bass_utils import run_bass_kernel_spmd

F32 = mybir.dt.float32
BF16 = mybir.dt.bfloat16
I32 = mybir.dt.int32
U8 = mybir.dt.uint8
AF = mybir.ActivationFunctionType
ALU = mybir.AluOpType
AX = mybir.AxisListType

D = 1024
NH = 16
HD = 64
NKV = 4
DFF = 2816
NPAIR = DFF // 128
BLK = 256
WIN = 128
EPS = 1e-6
THETA = 500000.0
NEGB = -30000.0
TWO_PI = 6.283185307179586
C1 = 6.28125
C2 = TWO_PI - C1
ARENA_BYTES = 188 * 1024


class Buf:
    __slots__ = ("w", "r")

    def __init__(self):
        self.w = None
        self.r = {}


def bufs(n):
    return [Buf() for _ in range(n)]


class Prog:
    ENG = ("pe", "act", "dve", "pool", "sp")

    def __init__(self, nc, stack):
        self.nc = nc
        self.streams = {k: [] for k in self.ENG}
        self.sem = {}
        self.cnt = {}
        for k in ("pe", "act", "dve", "pool"):
            self._mk("s_" + k, stack)
        self.dsem = {"sp": [], "pool": [], "act": []}
        for k, n in (("sp", 14), ("pool", 6), ("act", 2)):
            for i in range(n):
                nm = "d_%s%d" % (k, i)
                self._mk(nm, stack)
                self.dsem[k].append(nm)
        self.drr = {"sp": 0, "pool": 0, "act": 0}
        self.waited = {k: {} for k in self.ENG}
        self.ninstr = 0

    def _mk(self, name, stack):
        self.sem[name] = stack.enter_context(self.nc.semaphore(name))
        self.cnt[name] = 0

    def wait(self, eng, tok):
        if tok is None:
            return
        name, val = tok
        if eng == "pe" and name == "s_pe":
            return
        if self.waited[eng].get(name, 0) >= val:
            return
        self.waited[eng][name] = val
        self.streams[eng].append(("w", name, val))

    def _deps(self, eng, reads, writes):
        for b in reads:
            self.wait(eng, b.w)
        for b in writes:
            self.wait(eng, b.w)
            for it in b.r.items():
                self.wait(eng, it)

    def _commit(self, tok, reads, writes):
        for b in reads:
            if b.r.get(tok[0], 0) < tok[1]:
                b.r[tok[0]] = tok[1]
        for b in writes:
            b.w = tok
            b.r = {}

    def op(self, eng, fn, reads=(), writes=()):
        self._deps(eng, reads, writes)
        name = "s_" + eng
        self.cnt[name] += 1
        tok = (name, self.cnt[name])
        self.streams[eng].append(("i", fn, name, 1))
        self._commit(tok, reads, writes)
        self.ninstr += 1
        return tok

    def pe_group(self, fns, reads=(), writes=()):
        self._deps("pe", reads, writes)
        for fn in fns[:-1]:
            self.streams["pe"].append(("i", fn, None, 0))
        self.cnt["s_pe"] += 1
        tok = ("s_pe", self.cnt["s_pe"])
        self.streams["pe"].append(("i", fns[-1], "s_pe", 1))
        self._commit(tok, reads, writes)
        self.ninstr += len(fns)
        return tok

    def dma(self, eng, out, in_, reads=(), writes=()):
        self._deps(eng, reads, writes)
        lst = self.dsem[eng]
        name = lst[self.drr[eng] % len(lst)]
        self.drr[eng] += 1
        if self.cnt[name] > 0:
            self.wait(eng, (name, self.cnt[name]))
        self.cnt[name] += 16
        tok = (name, self.cnt[name])
        self.streams[eng].append(("i", (lambda e, o=out, i=in_: e.dma_start(out=o, in_=i)), name, 16))
        self._commit(tok, reads, writes)
        self.ninstr += 1
        return tok

    def barrier(self):
        for eng in self.ENG:
            for name, c in self.cnt.items():
                if c > 0:
                    self.wait(eng, (name, c))

    def check(self):
        pos = {k: 0 for k in self.ENG}
        val = {k: 0 for k in self.cnt}
        progress = True
        while progress:
            progress = False
            for k in self.ENG:
                st = self.streams[k]
                while pos[k] < len(st):
                    it = st[pos[k]]
                    if it[0] == "w":
                        if val[it[1]] < it[2]:
                            break
                    elif it[2] is not None:
                        val[it[2]] += it[3]
                    pos[k] += 1
                    progress = True
        stuck = {k: (pos[k], len(self.streams[k])) for k in self.ENG if pos[k] < len(self.streams[k])}
        assert not stuck, ("DEADLOCK in emitted program", stuck, {k: self.streams[k][pos[k]] for k in stuck})
        for k, c in self.cnt.items():
            assert val[k] == c, (k, val[k], c)

    def run(self, eng, e):
        sem = self.sem
        for it in self.streams[eng]:
            if it[0] == "w":
                e.wait_ge(sem[it[1]], it[2])
            else:
                ins = it[1](e)
                if it[2] is not None:
                    ins.then_inc(sem[it[2]], it[3])


class Arena:
    def __init__(self, ap, nbytes):
        self.ap = ap
        self.nbytes = nbytes
        self.off = 0

    def reset(self, off=0):
        self.off = off

    def alloc(self, shape, dtype, parts=128):
        esz = {F32: 4, BF16: 2, I32: 4, U8: 1}[dtype]
        n = 1
        for s in shape:
            n *= s
        nb = (n * esz + 63) // 64 * 64
        assert self.off + nb <= self.nbytes, ("arena overflow", self.off, nb, self.nbytes)
        a = self.ap[0:parts, self.off:self.off + n * esz]
        self.off += nb
        if dtype != U8:
            a = a.bitcast(dtype)
        if len(shape) == 2:
            a = a.rearrange("p (a b) -> p a b", a=shape[0])
        elif len(shape) == 3:
            a = a.rearrange("p (a b c) -> p a b c", a=shape[0], b=shape[1])
        return a


def mm(out, lhsT, rhs, start=True, stop=True):
    return lambda e: e.matmul(out, lhsT=lhsT, rhs=rhs, start=start, stop=stop)


def TT(o, a, b, op):
    return lambda e: e.tensor_tensor(out=o, in0=a, in1=b, op=op)


def TS(o, a, s1, s2, op0, op1=None):
    if op1 is None:
        return lambda e: e.tensor_scalar(out=o, in0=a, scalar1=s1, scalar2=None, op0=op0)
    return lambda e: e.tensor_scalar(out=o, in0=a, scalar1=s1, scalar2=s2, op0=op0, op1=op1)


def STT(o, a, s, b, op0, op1):
    return lambda e: e.scalar_tensor_tensor(out=o, in0=a, scalar=s, in1=b, op0=op0, op1=op1)


def TR(o, a, op):
    return lambda e: e.tensor_reduce(out=o, in_=a, axis=AX.X, op=op)


def RCP(o, a):
    return lambda e: e.reciprocal(out=o, in_=a)


def CP(o, a):
    return lambda e: e.tensor_copy(out=o, in_=a)


def ACP(o, a):
    return lambda e: e.copy(out=o, in_=a)


def ACTV(o, a, f, **kw):
    return lambda e: e.activation(out=o, in_=a, func=f, **kw)


def MS(o, v):
    return lambda e: e.memset(o, v)


def TSS(o, a, s, op):
    return lambda e: e.tensor_single_scalar(out=o, in_=a, scalar=s, op=op)


def TP(o, a, ident):
    return lambda e: e.transpose(o, a, ident)


def build_program(T, debug=False):
    assert T % 512 == 0
    NT = T // 512
    NB = T // BLK
    NS = T // 128
    assert NB <= 16
    nc = bass.Bass("TRN2", target_bir_lowering=False)

    def din(name, shape, dt):
        return nc.dram_tensor(name, shape, dt, kind="ExternalInput").ap()

    def dscr(name, shape, dt):
        return nc.dram_tensor(name, shape, dt, kind=("ExternalOutput" if debug else "Internal")).ap()

    xT_in = din("xT", [D, T], F32)
    pos_in = din("pos", [1, T], I32)
    gains_in = din("gains", [128, 80], F32)
    invf_in = din("invf", [128, 1], F32)
    perm_in = din("perm", [128, 128], BF16)
    identb_in = din("identb", [128, 128], BF16)
    identf_in = din("identf", [128, 128], F32)
    erows_in = din("erows", [16, T], BF16)
    pastb_in = din("pastb", [128, NS * 16], F32)
    past01_in = din("past01", [128, NS * 16], F32)
    own01_in = din("own01", [128, NS * 16], F32)
    causal_in = din("causal2", [128, 512], BF16)
    bandp_in = din("bandprev", [128, 512], BF16)
    bando_in = din("bandown", [128, 512], BF16)
    wqkv_in = din("wqkv", [2, D, 3 * D], F32)
    woa_in = din("woa", [2, D, D], F32)
    wkvb_in = din("wkvb", [D, 512], F32)
    wqb_in = din("wqb", [2, D, D], F32)
    wob_in = din("wob", [2, D, D], F32)
    sinks_in = din("sinks", [1, 32], F32)
    wup_in = din("wup", [4, 2 * NPAIR, 128, 8 * 128], F32)
    wdn_in = din("wdn", [4, DFF, D], F32)
    cw_in = din("cw", [4, 128, 3 * 2 * NPAIR], F32)
    cb_in = din("cb", [4, 128, 2 * NPAIR], F32)
    yT_out = nc.dram_tensor("yT", [D, T], F32, kind="ExternalOutput").ap()

    Xs = dscr("Xs", [D, T], F32)
    Ms = dscr("Ms", [D, T], F32)
    qTs = dscr("qTs", [D, T], BF16)
    kTs = dscr("kTs", [D, T], BF16)
    vs = dscr("vs", [T, D], BF16)
    oTs = dscr("oTs", [D, T], BF16)
    aTs = dscr("aTs", [DFF, T], BF16)
    Cs = dscr("Cs", [128, T], F32)
    Ss = dscr("Ss", [128, T], F32)
    kTsh = dscr("kTsh", [256, T], BF16)
    vsh = dscr("vsh", [T, 256], BF16)

    stack = contextlib.ExitStack()
    with stack:
        P = Prog(nc, stack)
        arena_t = stack.enter_context(nc.sbuf_tensor("arena", [128, ARENA_BYTES], U8))
        A = Arena(arena_t, ARENA_BYTES)
        banks = [stack.enter_context(nc.psum_tensor("bank%d" % i, [128, 512], F32)) for i in range(8)]
        BK = [b[:, :] for b in banks]
        bkb = bufs(8)

        dX = bufs(NT); dM = bufs(NT)
        dq = Buf(); dk = Buf(); dv = Buf(); do = Buf(); da = Buf(); dcs = Buf()
        dksh = Buf(); dvsh = Buf()

        gains = A.alloc([80], F32)
        invf = A.alloc([1], F32)
        perm = A.alloc([128], BF16)
        identb = A.alloc([128], BF16)
        identf = A.alloc([128], F32)
        onesf = A.alloc([128], F32)
        cwt = A.alloc([4 * 3 * 2 * NPAIR], F32)
        cbt = A.alloc([4 * 2 * NPAIR], F32)
        sinkt = A.alloc([32], F32)
        esink = A.alloc([32], F32)
        cbuf = Buf()
        for dst, src_ in ((gains, gains_in[:, :]), (invf, invf_in[:, :]), (perm, perm_in[:, :]),
                          (identb, identb_in[:, :]), (identf, identf_in[:, :]),
                          (sinkt, sinks_in[0:1, :].broadcast_to([128, 32]))):
            P.dma("sp", dst, src_, writes=[cbuf])
        NCW = 3 * 2 * NPAIR
        NCB = 2 * NPAIR
        for l in range(4):
            P.dma("sp", cwt[:, l * NCW:(l + 1) * NCW], cw_in[l, :, :], writes=[cbuf])
            P.dma("sp", cbt[:, l * NCB:(l + 1) * NCB], cb_in[l, :, :], writes=[cbuf])
        P.op("dve", MS(onesf, 1.0), writes=[cbuf])
        P.op("act", ACTV(esink, sinkt, AF.Exp), reads=[cbuf], writes=[cbuf])
        P.barrier()
        BASE = A.off

        def stage_rope():
            A.reset(BASE)
            W = 512
            pi_ = A.alloc([W], I32); ang = A.alloc([W], F32); t_ = A.alloc([W], F32)
            ki = A.alloc([W], I32); kf = A.alloc([W], F32); r_ = A.alloc([W], F32)
            m_ = A.alloc([W], F32); rc = A.alloc([W], F32)
            so = A.alloc([W], F32); co = A.alloc([W], F32)
            b = Buf()
            PI = float(np.pi)

            def D_(fn):
                P.op("dve", fn, reads=[b, cbuf], writes=[b])

            def fix(rr):
                D_(TSS(m_, rr, PI, ALU.is_gt))
                D_(STT(rr, m_, -TWO_PI, rr, ALU.mult, ALU.add))
                D_(TSS(m_, rr, -PI, ALU.is_lt))
                D_(STT(rr, m_, TWO_PI, rr, ALU.mult, ALU.add))
                D_(TS(rr, rr, 3.141592, -3.141592, ALU.min, ALU.max))

            for ci in range(T // W):
                cs = slice(ci * W, (ci + 1) * W)
                P.dma("sp", pi_, pos_in[0:1, cs].broadcast_to([128, W]), writes=[b])
                D_(CP(ang, pi_))
                D_(TS(ang, ang, invf[:, 0:1], None, ALU.mult))
                D_(TS(t_, ang, 1.0 / TWO_PI, None, ALU.mult))
                D_(CP(ki, t_))
                D_(CP(kf, ki))
                D_(STT(r_, kf, -C1, ang, ALU.mult, ALU.add))
                D_(STT(r_, kf, -C2, r_, ALU.mult, ALU.add))
                fix(r_)
                D_(TS(rc, r_, PI / 2, None, ALU.add))
                fix(rc)
                P.op("act", ACTV(so, r_, AF.Sin), reads=[b], writes=[b])
                P.op("act", ACTV(co, rc, AF.Sin), reads=[b], writes=[b])
                P.dma("sp", Ss[:, cs], so, reads=[b], writes=[dcs])
                P.dma("sp", Cs[:, cs], co, reads=[b], writes=[dcs])
            P.barrier()

        def norm_rstd(xt, xtb, sq, sqb, ssbank, rstd, rstdb):
            P.op("act", ACTV(sq, xt, AF.Square), reads=[xtb], writes=[sqb])
            P.pe_group([mm(BK[ssbank], onesf, sq[:, c, :], start=(c == 0), stop=(c == 7)) for c in range(8)],
                       reads=[sqb, cbuf], writes=[bkb[ssbank]])
            P.op("act", ACTV(rstd, BK[ssbank], AF.Sqrt, scale=1.0 / D, bias=EPS), reads=[bkb[ssbank]], writes=[rstdb])
            P.op("dve", RCP(rstd, rstd), reads=[rstdb], writes=[rstdb])

        def norm_apply(xt, xtb, rstd, rstdb, gi, outs, outb):
            for c in range(8):
                P.op("dve", STT(outs[c], xt[:, c, :], gains[:, gi * 8 + c:gi * 8 + c + 1], rstd, ALU.mult, ALU.mult),
                     reads=[xtb, rstdb, cbuf], writes=[outb])

        def load_w(dst, src_ap, wb):
            P.dma("pool", dst, src_ap.rearrange("(kc p) n -> p kc n", p=128), writes=[wb])

        def stage_proj(l, Xsrc, dXsrc):
            A.reset(BASE)
            moba = l < 2
            mk_kv = (l == 2)
            wb = Buf()
            wkv = None
            if moba:
                wsb = A.alloc([8, 3 * D], BF16)
                load_w(wsb, wqkv_in[l], wb)
                nqk = 16
            else:
                wsb = A.alloc([8, D], BF16)
                load_w(wsb, wqb_in[l - 2], wb)
                nqk = 8
                if mk_kv:
                    wkv = A.alloc([8, 512], BF16)
                    load_w(wkv, wkvb_in, wb)
            xt = [A.alloc([8, 512], F32) for _ in range(2)]; xtb = bufs(2)
            ct = [A.alloc([512], F32) for _ in range(2)]; st = [A.alloc([512], F32) for _ in range(2)]; csb = bufs(2)
            sq = A.alloc([8, 512], F32); sqb = Buf()
            rstd = A.alloc([512], F32); rstdb = Buf()
            hT = [A.alloc([8, 512], BF16) for _ in range(2)]; hTb = bufs(2)
            hK = None; hKb = None
            if mk_kv:
                hK = A.alloc([8, 512], BF16); hKb = Buf()
            qb = [A.alloc([512], BF16) for _ in range(2)]; qbb = bufs(2)
            t1 = [A.alloc([512], F32) for _ in range(2)]; t1b = bufs(2)
            u1 = [A.alloc([512], F32) for _ in range(2)]; u1b = bufs(2)
            qr = [A.alloc([512], BF16) for _ in range(3)]; qrb = bufs(3)
            vt = [A.alloc([4, 1024], BF16) for _ in range(2)]; vtb = bufs(2)
            SSB, PA, PB, PV = 0, (1, 2), (3, 4), (5, 6)

            def load(i):
                b = i % 2
                cs = slice(i * 512, (i + 1) * 512)
                P.dma("sp", xt[b], Xsrc[:, cs].rearrange("(c p) n -> p c n", p=128), reads=[dXsrc[i]], writes=[xtb[b]])
                P.dma("sp", ct[b], Cs[:, cs], reads=[dcs], writes=[csb[b]])
                P.dma("sp", st[b], Ss[:, cs], reads=[dcs], writes=[csb[b]])

            state = {"k": 0, "kv": 0}

            def qk_chunk(b, wts, hsrc, hsrcb, dst_ap, dbuf):
                k = state["k"]; state["k"] += 1
                pa = PA[k % 2]; pb = PB[k % 2]; r2 = k % 2; r3 = k % 3
                P.pe_group([mm(BK[pa], wts[kc], hsrc[:, kc, :], start=(kc == 0), stop=(kc == 7)) for kc in range(8)],
                           reads=[wb, hsrcb], writes=[bkb[pa]])
                P.op("act", ACP(qb[r2], BK[pa]), reads=[bkb[pa]], writes=[qbb[r2]])
                P.pe_group([mm(BK[pb], perm, qb[r2])], reads=[qbb[r2], cbuf], writes=[bkb[pb]])
                P.op("pool", TT(t1[r2], qb[r2], ct[b], ALU.mult), reads=[qbb[r2], csb[b]], writes=[t1b[r2]])
                P.op("dve", TT(u1[r2], BK[pb], st[b], ALU.mult), reads=[bkb[pb], csb[b]], writes=[u1b[r2]])
                P.op("pool", TT(qr[r3], u1[r2], t1[r2], ALU.add), reads=[u1b[r2], t1b[r2]], writes=[qrb[r3]])
                P.dma("sp", dst_ap, qr[r3], reads=[qrb[r3]], writes=[dbuf])

            load(0)
            for i in range(NT):
                b = i % 2
                cs = slice(i * 512, (i + 1) * 512)
                if i + 1 < NT:
                    load(i + 1)
                norm_rstd(xt[b], xtb[b], sq, sqb, SSB, rstd, rstdb)
                norm_apply(xt[b], xtb[b], rstd, rstdb, l, [hT[b][:, c, :] for c in range(8)], hTb[b])
                if mk_kv:
                    norm_apply(xt[b], xtb[b], rstd, rstdb, 8, [hK[:, c, :] for c in range(8)], hKb)
                for m in range(nqk):
                    if m < 8:
                        dst, dbuf = qTs[m * 128:(m + 1) * 128, cs], dq
                    else:
                        dst, dbuf = kTs[(m - 8) * 128:(m - 7) * 128, cs], dk
                    qk_chunk(b, [wsb[:, kc, m * 128:(m + 1) * 128] for kc in range(8)], hT[b], hTb[b], dst, dbuf)
                if mk_kv:
                    for m in range(2):
                        qk_chunk(b, [wkv[:, kc, m * 128:(m + 1) * 128] for kc in range(8)], hK, hKb,
                                 kTsh[m * 128:(m + 1) * 128, cs], dksh)
                if moba or mk_kv:
                    vb = i % 2
                    ngrp = 2 if moba else 1
                    for s in range(4):
                        for g in range(ngrp):
                            pv = PV[state["kv"] % 2]; state["kv"] += 1
                            if moba:
                                fns = [mm(BK[pv], hT[b][:, kc, s * 128:(s + 1) * 128], wsb[:, kc, 2048 + g * 512:2048 + (g + 1) * 512],
                                          start=(kc == 0), stop=(kc == 7)) for kc in range(8)]
                                P.pe_group(fns, reads=[wb, hTb[b]], writes=[bkb[pv]])
                                P.op("act", ACP(vt[vb][:, s, g * 512:(g + 1) * 512], BK[pv]), reads=[bkb[pv]], writes=[vtb[vb]])
                            else:
                                fns = [mm(BK[pv][:, 0:256], hK[:, kc, s * 128:(s + 1) * 128], wkv[:, kc, 256:512],
                                          start=(kc == 0), stop=(kc == 7)) for kc in range(8)]
                                P.pe_group(fns, reads=[wb, hKb], writes=[bkb[pv]])
                                P.op("act", ACP(vt[vb][:, s, 0:256], BK[pv][:, 0:256]), reads=[bkb[pv]], writes=[vtb[vb]])
                    if moba:
                        P.dma("sp", vs[cs, :].rearrange("(s p) f -> p s f", p=128), vt[vb], reads=[vtb[vb]], writes=[dv])
                    else:
                        P.dma("sp", vsh[cs, :].rearrange("(s p) f -> p s f", p=128), vt[vb][:, :, 0:256], reads=[vtb[vb]], writes=[dvsh])
            P.barrier()

        def stage_moba():
            A.reset(BASE)
            G = NS * 16
            pastb = A.alloc([G], F32); past01 = A.alloc([G], F32); own01 = A.alloc([G], F32)
            causal = A.alloc([2, 256], BF16)
            tb = Buf()
            P.dma("sp", pastb, pastb_in[:, :], writes=[tb]); P.dma("sp", past01, past01_in[:, :], writes=[tb])
            P.dma("sp", own01, own01_in[:, :], writes=[tb])
            P.dma("sp", causal, causal_in[:, :].rearrange("p (a b) -> p a b", a=2), writes=[tb])
            Kaug = [A.alloc([T], BF16) for _ in range(2)]; Kb = bufs(2)
            Qaug = [A.alloc([T], BF16) for _ in range(2)]; Qhb = bufs(2); Qlb = bufs(2)
            Vh = [A.alloc([NS, 128], BF16) for _ in range(2)]; Vb = bufs(2)
            oh = [A.alloc([T], BF16, parts=64) for _ in range(2)]; ohb = bufs(2)
            km = A.alloc([16], F32); kmb16 = A.alloc([16], BF16); kmb = Buf()
            gm = A.alloc([G], F32); g2 = A.alloc([G], F32); eq = A.alloc([G], F32); sel = A.alloc([G], F32)
            mx = A.alloc([NS], F32); mbb = A.alloc([G + 128], BF16); gb = Buf()
            Pt = [A.alloc([512], BF16) for _ in range(4)]; Ptb = bufs(4)
            rden = A.alloc([256], F32); rdb = Buf()
            GB, TB_, SB, OB = 0, 1, (2, 3, 4, 7), (5, 6)
            tbk = BK[TB_].bitcast(BF16)

            def g3(ap):
                return ap.rearrange("p (a b) -> p a b", b=16)

            def bc(ap):
                return ap.unsqueeze(2).to_broadcast([128, NS, 16])

            P.op("dve", MS(km, 0.0), writes=[kmb])
            P.op("dve", MS(mbb, 0.0), writes=[gb])
            for b in range(2):
                P.dma("sp", Kaug[b][0:16, :], erows_in[:, :], writes=[Kb[b]])
                P.op("dve", MS(Qaug[b][0:16, :], 0.0), writes=[Qlb[b]])
                P.op("dve", MS(Vh[b][:, :, 64:128], 1.0), writes=[Vb[b]])

            def load(h):
                b = h % 2
                P.dma("sp", Kaug[b][16:80, :], kTs[h * 64:(h + 1) * 64, :], reads=[dk], writes=[Kb[b]])
                P.dma("sp", Qaug[b][16:80, :], qTs[h * 64:(h + 1) * 64, :], reads=[dq], writes=[Qhb[b]])
                P.dma("sp", Vh[b][:, :, 0:64], vs[:, h * 64:(h + 1) * 64].rearrange("(c p) d -> p c d", p=128), reads=[dv], writes=[Vb[b]])

            def GD(fn, extra=()):
                P.op("dve", fn, reads=[gb] + list(extra), writes=[gb])

            def gate_front(h):
                b = h % 2
                P.op("dve", TR(km[0:80, 0:NB], Kaug[b][0:80, :].rearrange("p (n k) -> p n k", k=BLK), ALU.add), reads=[Kb[b]], writes=[kmb])
                P.op("dve", MS(km[0:16, :], 0.0), reads=[kmb], writes=[kmb])
                P.op("dve", TS(kmb16[0:80, :], km[0:80, :], 1.0 / BLK, None, ALU.mult), reads=[kmb], writes=[kmb])
                P.pe_group([mm(BK[GB][:, s * 16:(s + 1) * 16], Qaug[b][0:80, s * 128:(s + 1) * 128], kmb16[0:80, :]) for s in range(NS)],
                           reads=[Qhb[b], Qlb[b], kmb], writes=[bkb[GB]])
                P.op("dve", TT(gm, BK[GB][:, 0:G], pastb, ALU.add), reads=[bkb[GB], tb], writes=[gb])
                GD(TR(mx, g3(gm), ALU.max))
                GD(TT(g3(eq), g3(gm), bc(mx), ALU.is_equal))
                GD(STT(g2, eq, -1e9, gm, ALU.mult, ALU.add))
                GD(TR(mx, g3(g2), ALU.max))
                GD(TT(g3(eq), g3(g2), bc(mx), ALU.is_equal))
                GD(STT(g2, eq, -1e9, g2, ALU.mult, ALU.add))
                GD(TR(mx, g3(g2), ALU.max))
                GD(TT(g3(sel), g3(gm), bc(mx), ALU.is_ge))
                GD(TT(sel, sel, past01, ALU.mult), [tb])
                GD(TT(sel, sel, own01, ALU.add), [tb])
                GD(TS(mbb[:, 0:G], sel, -NEGB, NEGB, ALU.mult, ALU.add))

            def gate_back(h):
                b = h % 2
                for grp in range((NS + 7) // 8):
                    n8 = min(8, NS - grp * 8)
                    P.pe_group([TP(tbk[:, j * 128:(j + 1) * 128], mbb[:, s * 16:s * 16 + 128], identb)
                                for j, s in enumerate(range(grp * 8, grp * 8 + n8))],
                               reads=[gb, cbuf], writes=[bkb[TB_]])
                    P.op("act", ACP(Qaug[b][0:16, grp * 1024:grp * 1024 + n8 * 128], tbk[0:16, 0:n8 * 128]),
                         reads=[bkb[TB_]], writes=[Qlb[b]])

            LOOK = 2
            load(0)
            gate_front(0)
            gate_back(0)
            sk = 0
            for h in range(NH):
                b = h % 2
                if h + 1 < NH:
                    load(h + 1)
                items = [(i, n) for i in range(NB) for n in range(i + 1)]
                slot = {}

                def emit_qk(t):
                    nonlocal sk
                    i, n = items[t]
                    sb = SB[sk % len(SB)]; pt = sk % len(Pt); sk += 1
                    slot[t] = (sb, pt)
                    qs = slice(i * BLK, (i + 1) * BLK)
                    fns = []
                    for c in range(2):
                        ks = slice((2 * n + c) * 128, (2 * n + c + 1) * 128)
                        if n < i:
                            fns.append(mm(BK[sb][:, c * 256:(c + 1) * 256], Kaug[b][0:80, ks], Qaug[b][0:80, qs]))
                        else:
                            fns.append(mm(BK[sb][:, c * 256:(c + 1) * 256], Kaug[b][0:80, ks], Qaug[b][0:80, qs], start=True, stop=False))
                            fns.append(mm(BK[sb][:, c * 256:(c + 1) * 256], identb, causal[:, c, :], start=False, stop=True))
                    P.pe_group(fns, reads=[Kb[b], Qhb[b], Qlb[b], tb, cbuf], writes=[bkb[sb]])
                    P.op("act", ACTV(Pt[pt], BK[sb], AF.Exp, scale=0.125), reads=[bkb[sb]], writes=[Ptb[pt]])

                def emit_pv(t):
                    i, n = items[t]
                    sb, pt = slot.pop(t)
                    ob = OB[i % 2]
                    qs = slice(i * BLK, (i + 1) * BLK)
                    fns = [mm(BK[ob][:, 0:256], Vh[b][:, 2 * n + c, :], Pt[pt][:, c * 256:(c + 1) * 256],
                              start=(n == 0 and c == 0), stop=(n == i and c == 1)) for c in range(2)]
                    P.pe_group(fns, reads=[Vb[b], Ptb[pt]], writes=[bkb[ob]])
                    if n == i:
                        P.op("dve", RCP(rden[64:128, :], BK[ob][64:128, 0:256]), reads=[bkb[ob]], writes=[rdb])
                        P.op("dve", TT(oh[b][0:64, qs], BK[ob][0:64, 0:256], rden[64:128, :], ALU.mult),
                             reads=[bkb[ob], rdb], writes=[ohb[b]])

                NI = len(items)
                gf_blk = min(1, NB - 1)
                gb_blk = max(gf_blk, NB - 3)
                for t in range(min(LOOK, NI)):
                    emit_qk(t)
                for t in range(NI):
                    if t + LOOK < NI:
                        emit_qk(t + LOOK)
                    emit_pv(t)
                    i_, n_ = items[t]
                    if h + 1 < NH and n_ == i_:
                        if i_ == gf_blk:
                            gate_front(h + 1)
                        if i_ == gb_blk:
                            gate_back(h + 1)
                P.dma("sp", oTs[h * 64:(h + 1) * 64, :], oh[b], reads=[ohb[b]], writes=[do])
            P.barrier()

        def stage_swa(l):
            A.reset(BASE)
            li = l - 2
            bandp = A.alloc([512], BF16); bando = A.alloc([512], BF16)
            es = A.alloc([16, 128], F32)
            tb = Buf()
            P.dma("sp", bandp, bandp_in[:, :], writes=[tb]); P.dma("sp", bando, bando_in[:, :], writes=[tb])
            P.op("dve", CP(es, esink[:, li * 16:(li + 1) * 16].unsqueeze(2).to_broadcast([128, 16, 128])), reads=[cbuf], writes=[tb])
            Kg = [A.alloc([T], BF16, parts=64) for _ in range(2)]; Kb = bufs(2)
            Vg = [A.alloc([NS, 128], BF16) for _ in range(2)]; Vb = bufs(2)
            Qg = [A.alloc([4, T], BF16, parts=64) for _ in range(2)]; Qb = bufs(2)
            og = [A.alloc([4, T], BF16, parts=64) for _ in range(2)]; ogb = bufs(2)
            Pt = [A.alloc([512], BF16) for _ in range(4)]; Ptb = bufs(4)
            den = A.alloc([512], F32); rdb = Buf()
            SB, OB = (0, 1, 2, 3), (4, 5)
            for b in range(2):
                P.op("dve", MS(Vg[b][:, :, 64:128], 1.0), writes=[Vb[b]])

            def load(g):
                b = g % 2
                P.dma("sp", Kg[b], kTsh[g * 64:(g + 1) * 64, :], reads=[dksh], writes=[Kb[b]])
                P.dma("sp", Vg[b][:, :, 0:64], vsh[:, g * 64:(g + 1) * 64].rearrange("(c p) d -> p c d", p=128), reads=[dvsh], writes=[Vb[b]])
                for hh in range(4):
                    hd = g * 4 + hh
                    P.dma("sp", Qg[b][:, hh, :], qTs[hd * 64:(hd + 1) * 64, :], reads=[dq], writes=[Qb[b]])

            load(0)
            sk = 0
            for g in range(NKV):
                b = g % 2
                if g + 1 < NKV:
                    load(g + 1)
                for s in range(NS):
                    ob = OB[s % 2]
                    qs = slice(s * 128, (s + 1) * 128)
                    chunks = ([(s - 1, bandp)] if s > 0 else []) + [(s, bando)]
                    pts = []
                    for (kc, band) in chunks:
                        sb = SB[sk % 4]; pt = sk % 4; sk += 1
                        P.pe_group([mm(BK[sb].rearrange("p (a b) -> p a b", a=4), Kg[b][:, kc * 128:(kc + 1) * 128], Qg[b][:, :, qs], start=True, stop=False),
                                    mm(BK[sb], identb, band, start=False, stop=True)],
                                   reads=[Kb[b], Qb[b], tb, cbuf], writes=[bkb[sb]])
                        P.op("act", ACTV(Pt[pt], BK[sb], AF.Exp, scale=0.125), reads=[bkb[sb]], writes=[Ptb[pt]])
                        pts.append((kc, pt))
                    P.pe_group([mm(BK[ob], Vg[b][:, kc, :], Pt[pt], start=(j == 0), stop=(j == len(pts) - 1)) for j, (kc, pt) in enumerate(pts)],
                               reads=[Vb[b]] + [Ptb[pt] for _, pt in pts], writes=[bkb[ob]])
                    P.op("dve", TT(den[64:128, :], BK[ob][64:128, :], es[64:128, g * 4:(g + 1) * 4, :].rearrange("p a b -> p (a b)"), ALU.add),
                         reads=[bkb[ob], tb], writes=[rdb])
                    P.op("dve", RCP(den[64:128, :], den[64:128, :]), reads=[rdb], writes=[rdb])
                    P.op("dve", TT(og[b][:, :, qs], BK[ob][0:64, :].rearrange("p (a b) -> p a b", a=4),
                                   den[64:128, :].rearrange("p (a b) -> p a b", a=4), ALU.mult),
                         reads=[bkb[ob], rdb], writes=[ogb[b]])
                for hh in range(4):
                    hd = g * 4 + hh
                    P.dma("sp", oTs[hd * 64:(hd + 1) * 64, :], og[b][:, hh, :], reads=[ogb[b]], writes=[do])
            P.barrier()

        def stage_oproj_ffn_up(l, Xsrc, dXsrc):
            A.reset(BASE)
            wb = Buf()
            hall = A.alloc([8, T], BF16); hallb = Buf()
            mark = A.off
            wo = A.alloc([8, D], BF16)
            load_w(wo, (woa_in[l] if l < 2 else wob_in[l - 2]), wb)
            ot = [A.alloc([8, 512], BF16) for _ in range(2)]; otb = bufs(2)
            xt = [A.alloc([8, 512], F32) for _ in range(2)]; xtb = bufs(2)
            xm = [A.alloc([8, 512], F32) for _ in range(2)]; xmb = bufs(2)
            rstd = A.alloc([512], F32); rstdb = Buf()
            SSB, PO = 0, (1, 2, 3)

            def load(i):
                b = i % 2
                cs = slice(i * 512, (i + 1) * 512)
                P.dma("sp", ot[b], oTs[:, cs].rearrange("(c p) n -> p c n", p=128), reads=[do], writes=[otb[b]])
                P.dma("sp", xt[b], Xsrc[:, cs].rearrange("(c p) n -> p c n", p=128), reads=[dXsrc[i]], writes=[xtb[b]])

            load(0)
            k = 0
            for i in range(NT):
                b = i % 2
                cs = slice(i * 512, (i + 1) * 512)
                if i + 1 < NT:
                    load(i + 1)
                for m in range(8):
                    po = PO[k % 3]; k += 1
                    P.pe_group([mm(BK[po], wo[:, kc, m * 128:(m + 1) * 128], ot[b][:, kc, :], start=(kc == 0), stop=(kc == 7)) for kc in range(8)],
                               reads=[wb, otb[b]], writes=[bkb[po]])
                    P.op("dve", TT(xm[b][:, m, :], BK[po], xt[b][:, m, :], ALU.add), reads=[bkb[po], xtb[b]], writes=[xmb[b]])
                P.dma("sp", Ms[:, cs].rearrange("(c p) n -> p c n", p=128), xm[b], reads=[xmb[b]], writes=[dM[i]])
                norm_rstd(xm[b], xmb[b], xt[b], xtb[b], SSB, rstd, rstdb)
                norm_apply(xm[b], xmb[b], rstd, rstdb, 4 + l, [hall[:, c, cs] for c in range(8)], hallb)
            P.barrier()

            A.reset(mark)
            wg = [A.alloc([8, 128], BF16) for _ in range(2)]; wv = [A.alloc([8, 128], BF16) for _ in range(2)]; wgb = bufs(2)
            dg = [A.alloc([6, 128], BF16) for _ in range(2)]; dgb = bufs(2)
            ug = [A.alloc([514], BF16) for _ in range(2)]; uv = [A.alloc([514], BF16) for _ in range(2)]; ugb = bufs(2); uvb = bufs(2)
            sg = [A.alloc([512], F32) for _ in range(2)]; sgb = bufs(2)
            at = [A.alloc([512], BF16) for _ in range(3)]; atb = bufs(3)
            PG, PVv, PCG, PCV = (0, 1), (2, 3), (4, 5), (6, 7)

            def loadw(j):
                b = j % 2
                P.dma("pool", wg[b], wup_in[l, j].rearrange("p (kc n) -> p kc n", kc=8), writes=[wgb[b]])
                P.dma("pool", wv[b], wup_in[l, NPAIR + j].rearrange("p (kc n) -> p kc n", kc=8), writes=[wgb[b]])

            loadw(0)
            k = 0
            for j in range(NPAIR):
                b = j % 2
                if j + 1 < NPAIR:
                    loadw(j + 1)
                for tap in range(3):
                    for gv in range(2):
                        col = l * NCW + tap * 2 * NPAIR + gv * NPAIR + j
                        P.op("dve", TS(dg[b][:, tap * 2 + gv, :], identf, cwt[:, col:col + 1], None, ALU.mult), reads=[cbuf], writes=[dgb[b]])
                P.op("dve", MS(ug[0][:, 0:2], 0.0), writes=[ugb[0]])
                P.op("dve", MS(uv[0][:, 0:2], 0.0), writes=[uvb[0]])
                cg = l * NCB + j
                cv = l * NCB + NPAIR + j
                for i in range(NT):
                    cs = slice(i * 512, (i + 1) * 512)
                    r = k % 2; r3 = k % 3; k += 1
                    ub = i % 2
                    pg, pv, pcg, pcv = PG[r], PVv[r], PCG[r], PCV[r]
                    P.pe_group([mm(BK[pg], wg[b][:, kc, :], hall[:, kc, cs], start=(kc == 0), stop=(kc == 7)) for kc in range(8)],
                               reads=[wgb[b], hallb], writes=[bkb[pg]])
                    P.pe_group([mm(BK[pv], wv[b][:, kc, :], hall[:, kc, cs], start=(kc == 0), stop=(kc == 7)) for kc in range(8)],
                               reads=[wgb[b], hallb], writes=[bkb[pv]])
                    P.op("act", ACP(ug[ub][:, 2:514], BK[pg]), reads=[bkb[pg]], writes=[ugb[ub]])
                    P.op("act", ACP(uv[ub][:, 2:514], BK[pv]), reads=[bkb[pv]], writes=[uvb[ub]])
                    if i + 1 < NT:
                        P.op("pool", CP(ug[1 - ub][:, 0:2], ug[ub][:, 512:514]), reads=[ugb[ub]], writes=[ugb[1 - ub]])
                        P.op("pool", CP(uv[1 - ub][:, 0:2], uv[ub][:, 512:514]), reads=[uvb[ub]], writes=[uvb[1 - ub]])
                    P.pe_group([mm(BK[pcg], dg[b][:, tap * 2 + 0, :], ug[ub][:, tap:tap + 512], start=(tap == 0), stop=(tap == 2)) for tap in range(3)],
                               reads=[dgb[b], ugb[ub]], writes=[bkb[pcg]])
                    P.pe_group([mm(BK[pcv], dg[b][:, tap * 2 + 1, :], uv[ub][:, tap:tap + 512], start=(tap == 0), stop=(tap == 2)) for tap in range(3)],
                               reads=[dgb[b], uvb[ub]], writes=[bkb[pcv]])
                    P.op("act", ACTV(sg[r], BK[pcg], AF.Silu, bias=cbt[:, cg:cg + 1]), reads=[bkb[pcg], cbuf], writes=[sgb[r]])
                    P.op("dve", STT(at[r3], BK[pcv], cbt[:, cv:cv + 1], sg[r], ALU.add, ALU.mult),
                         reads=[bkb[pcv], sgb[r], cbuf], writes=[atb[r3]])
                    P.dma("sp", aTs[j * 128:(j + 1) * 128, cs], at[r3], reads=[atb[r3]], writes=[da])
            P.barrier()

        def stage_ffn_down(l):
            A.reset(BASE)
            last = (l == 3)
            wb = Buf()
            wd = A.alloc([NPAIR, D], BF16)
            load_w(wd, wdn_in[l], wb)
            at = [A.alloc([NPAIR, 512], BF16) for _ in range(2)]; atb = bufs(2)
            xm = [A.alloc([8, 512], F32) for _ in range(2)]; xmb = bufs(2)
            xo = [A.alloc([8, 512], F32) for _ in range(2)]; xob = bufs(2)
            rstd = A.alloc([512], F32); rstdb = Buf()
            SSB, PO = 0, (1, 2, 3)
            dyo = Buf()

            def load(i):
                b = i % 2
                cs = slice(i * 512, (i + 1) * 512)
                P.dma("sp", at[b], aTs[:, cs].rearrange("(c p) n -> p c n", p=128), reads=[da], writes=[atb[b]])
                P.dma("sp", xm[b], Ms[:, cs].rearrange("(c p) n -> p c n", p=128), reads=[dM[i]], writes=[xmb[b]])

            load(0)
            k = 0
            for i in range(NT):
                b = i % 2
                cs = slice(i * 512, (i + 1) * 512)
                if i + 1 < NT:
                    load(i + 1)
                for m in range(8):
                    po = PO[k % 3]; k += 1
                    P.pe_group([mm(BK[po], wd[:, kc, m * 128:(m + 1) * 128], at[b][:, kc, :], start=(kc == 0), stop=(kc == NPAIR - 1)) for kc in range(NPAIR)],
                               reads=[wb, atb[b]], writes=[bkb[po]])
                    P.op("dve", TT(xo[b][:, m, :], BK[po], xm[b][:, m, :], ALU.add), reads=[bkb[po], xmb[b]], writes=[xob[b]])
                if not last:
                    P.dma("sp", Xs[:, cs].rearrange("(c p) n -> p c n", p=128), xo[b], reads=[xob[b]], writes=[dX[i]])
                else:
                    norm_rstd(xo[b], xob[b], xm[b], xmb[b], SSB, rstd, rstdb)
                    norm_apply(xo[b], xob[b], rstd, rstdb, 9, [xo[b][:, c, :] for c in range(8)], xob[b])
                    P.dma("sp", yT_out[:, cs].rearrange("(c p) n -> p c n", p=128), xo[b], reads=[xob[b]], writes=[dyo])
            P.barrier()

        stage_rope()
        dXin = bufs(NT)
        for l in range(4):
            Xsrc, dXsrc = (xT_in, dXin) if l == 0 else (Xs, dX)
            stage_proj(l, Xsrc, dXsrc)
            if l < 2:
                stage_moba()
            else:
                stage_swa(l)
            stage_oproj_ffn_up(l, Xsrc, dXsrc)
            stage_ffn_down(l)
        P.barrier()

        P.check()
        with nc.Block() as block:
            @block.tensor
            def _(e):
                P.run("pe", e)

            @block.scalar
            def _(e):
                P.run("act", e)

            @block.vector
            def _(e):
                P.run("dve", e)

            @block.gpsimd
            def _(e):
                P.run("pool", e)

            @block.sync
            def _(e):
                P.run("sp", e)
    return nc


def host_constants(T):
    bf = ml_dtypes.bfloat16
    NS = T // 128
    p = np.arange(128)
    d = p % 64
    invf8 = (np.float32(THETA) ** (-np.arange(0, 16, 2, dtype=np.float32) / np.float32(16))).astype(np.float32)
    invf = np.where(d < 16, invf8[d % 8], np.float32(0)).astype(np.float32).reshape(128, 1)
    perm = np.zeros((128, 128), np.float32)
    for m in range(128):
        dm = m % 64
        if dm < 8:
            perm[m + 8, m] = -1.0
        elif dm < 16:
            perm[m - 8, m] = 1.0
    ident = np.eye(128, dtype=np.float32)
    erows = np.zeros((16, T), np.float32)
    for n in range(T // BLK):
        erows[n, n * BLK:(n + 1) * BLK] = 1.0
    past01 = np.zeros((128, NS, 16), np.float32)
    own01 = np.zeros((128, NS, 16), np.float32)
    for s in range(NS):
        qb = (s * 128) // BLK
        past01[:, s, :qb] = 1.0
        own01[:, s, qb] = 1.0
    pastb = (past01 - 1.0) * 1e4
    kk = np.arange(128)[:, None]
    causal2 = np.zeros((128, 2, 256), np.float32)
    qq = np.arange(256)[None, :]
    for c in range(2):
        causal2[:, c, :] = np.where(c * 128 + kk <= qq, 0.0, NEGB)
    q1 = np.arange(128)[None, :]
    bandprev = np.tile(np.where(kk > q1, 0.0, NEGB), (1, 4))
    bandown = np.tile(np.where(kk <= q1, 0.0, NEGB), (1, 4))
    return {
        "invf": invf, "perm": perm.astype(bf), "identb": ident.astype(bf), "identf": ident,
        "erows": erows.astype(bf), "pastb": pastb.reshape(128, -1).astype(np.float32),
        "past01": past01.reshape(128, -1), "own01": own01.reshape(128, -1),
        "causal2": causal2.reshape(128, 512).astype(bf), "bandprev": bandprev.astype(bf), "bandown": bandown.astype(bf),
    }


def host_weights(attn_norm, w_qkv_a, w_o_a, kv_norm, w_kv_b, w_q_b, sinks_b, w_o_b, ffn_norm, w_up, conv_w, conv_b, w_down, final_norm):
    f = np.float32
    allg = np.concatenate([np.asarray(attn_norm, f), np.asarray(ffn_norm, f), np.asarray(kv_norm, f)[None], np.asarray(final_norm, f)[None]], 0)
    gains = np.ascontiguousarray(allg.reshape(10, 8, 128).transpose(2, 0, 1).reshape(128, 80))
    wup = np.asarray(w_up, f).reshape(4, 8, 128, 2 * NPAIR, 128).transpose(0, 3, 2, 1, 4)
    wup = np.ascontiguousarray(wup).reshape(4, 2 * NPAIR, 128, 1024)
    cw = np.asarray(conv_w, f).reshape(4, 3, 2 * NPAIR, 128).transpose(0, 3, 1, 2)
    cw = np.ascontiguousarray(cw).reshape(4, 128, 3 * 2 * NPAIR)
    cb = np.ascontiguousarray(np.asarray(conv_b, f).reshape(4, 2 * NPAIR, 128).transpose(0, 2, 1))
    return {
        "gains": gains,
        "wqkv": np.ascontiguousarray(np.asarray(w_qkv_a, f)), "woa": np.ascontiguousarray(np.asarray(w_o_a, f)),
        "wkvb": np.ascontiguousarray(np.asarray(w_kv_b, f)), "wqb": np.ascontiguousarray(np.asarray(w_q_b, f)),
        "wob": np.ascontiguousarray(np.asarray(w_o_b, f)), "sinks": np.ascontiguousarray(np.asarray(sinks_b, f).reshape(1, 32)),
        "wup": wup, "wdn": np.ascontiguousarray(np.asarray(w_down, f)), "cw": cw, "cb": cb,
    }


_CACHE = {}


def run_cores(x, positions, weights, T, debug=False):
    n = x.shape[0]
    key = (T, debug)
    if key not in _CACHE:
        _CACHE[key] = build_program(T, debug)
    nc = _CACHE[key]
    consts = host_constants(T)
    shared = dict(consts)
    shared.update(weights)
    in_maps = []
    for b in range(n):
        m = dict(shared)
        m["xT"] = np.ascontiguousarray(np.asarray(x[b], np.float32).T)
        m["pos"] = np.ascontiguousarray(np.asarray(positions[b], np.int32).reshape(1, T))
        in_maps.append(m)
    res = run_bass_kernel_spmd(nc, in_maps, core_ids=list(range(n)))
    return res


def kernel(x, positions, attn_norm, w_qkv_a, w_o_a, kv_norm, w_kv_b, w_q_b, sinks_b, w_o_b, ffn_norm, w_up, conv_w, conv_b, w_down, final_norm):
    x = np.asarray(x)
    B, T, _ = x.shape
    weights = host_weights(attn_norm, w_qkv_a, w_o_a, kv_norm, w_kv_b, w_q_b, sinks_b, w_o_b, ffn_norm, w_up, conv_w, conv_b, w_down, final_norm)
    res = run_cores(x, np.asarray(positions), weights, T)
    out = np.stack([np.ascontiguousarray(r["yT"].T) for r in res.results], 0)
    return out.astype(np.float32)
```

```python
import numpy as np
import ml_dtypes
import contextlib
import concourse.bass as bass
import concourse.mybir as mybir
from concourse.bass_utils import run_bass_kernel_spmd

F32 = mybir.dt.float32
BF16 = mybir.dt.bfloat16
I32 = mybir.dt.int32
U8 = mybir.dt.uint8
AF = mybir.ActivationFunctionType
ALU = mybir.AluOpType
AX = mybir.AxisListType

D = 1024
NH = 16
HD = 64
NKV = 4
DFF = 2816
NPAIR = DFF // 128
BLK = 256
WIN = 128
EPS = 1e-6
THETA = 500000.0
NEGB = -30000.0
TWO_PI = 6.283185307179586
C1 = 6.28125
C2 = TWO_PI - C1
ARENA_BYTES = 188 * 1024


class Buf:
    __slots__ = ("w", "r")

    def __init__(self):
        self.w = None
        self.r = {}


def bufs(n):
    return [Buf() for _ in range(n)]


class Prog:
    ENG = ("pe", "act", "dve", "pool", "sp")

    def __init__(self, nc, stack):
        self.nc = nc
        self.streams = {k: [] for k in self.ENG}
        self.sem = {}
        self.cnt = {}
        for k in ("pe", "act", "dve", "pool"):
            self._mk("s_" + k, stack)
        self.dsem = {"sp": [], "pool": [], "act": []}
        for k, n in (("sp", 14), ("pool", 6), ("act", 2)):
            for i in range(n):
                nm = "d_%s%d" % (k, i)
                self._mk(nm, stack)
                self.dsem[k].append(nm)
        self.drr = {"sp": 0, "pool": 0, "act": 0}
        self.waited = {k: {} for k in self.ENG}
        self.ninstr = 0

    def _mk(self, name, stack):
        self.sem[name] = stack.enter_context(self.nc.semaphore(name))
        self.cnt[name] = 0

    def wait(self, eng, tok):
        if tok is None:
            return
        name, val = tok
        if eng == "pe" and name == "s_pe":
            return
        if self.waited[eng].get(name, 0) >= val:
            return
        self.waited[eng][name] = val
        self.streams[eng].append(("w", name, val))

    def _deps(self, eng, reads, writes):
        for b in reads:
            self.wait(eng, b.w)
        for b in writes:
            self.wait(eng, b.w)
            for it in b.r.items():
                self.wait(eng, it)

    def _commit(self, tok, reads, writes):
        for b in reads:
            if b.r.get(tok[0], 0) < tok[1]:
                b.r[tok[0]] = tok[1]
        for b in writes:
            b.w = tok
            b.r = {}

    def op(self, eng, fn, reads=(), writes=()):
        self._deps(eng, reads, writes)
        name = "s_" + eng
        self.cnt[name] += 1
        tok = (name, self.cnt[name])
        self.streams[eng].append(("i", fn, name, 1))
        self._commit(tok, reads, writes)
        self.ninstr += 1
        return tok

    def pe_group(self, fns, reads=(), writes=()):
        self._deps("pe", reads, writes)
        for fn in fns[:-1]:
            self.streams["pe"].append(("i", fn, None, 0))
        self.cnt["s_pe"] += 1
        tok = ("s_pe", self.cnt["s_pe"])
        self.streams["pe"].append(("i", fns[-1], "s_pe", 1))
        self._commit(tok, reads, writes)
        self.ninstr += len(fns)
        return tok

    def dma(self, eng, out, in_, reads=(), writes=()):
        self._deps(eng, reads, writes)
        lst = self.dsem[eng]
        name = lst[self.drr[eng] % len(lst)]
        self.drr[eng] += 1
        if self.cnt[name] > 0:
            self.wait(eng, (name, self.cnt[name]))
        self.cnt[name] += 16
        tok = (name, self.cnt[name])
        self.streams[eng].append(("i", (lambda e, o=out, i=in_: e.dma_start(out=o, in_=i)), name, 16))
        self._commit(tok, reads, writes)
        self.ninstr += 1
        return tok

    def barrier(self):
        for eng in self.ENG:
            for name, c in self.cnt.items():
                if c > 0:
                    self.wait(eng, (name, c))

    def check(self):
        pos = {k: 0 for k in self.ENG}
        val = {k: 0 for k in self.cnt}
        progress = True
        while progress:
            progress = False
            for k in self.ENG:
                st = self.streams[k]
                while pos[k] < len(st):
                    it = st[pos[k]]
                    if it[0] == "w":
                        if val[it[1]] < it[2]:
                            break
                    elif it[2] is not None:
                        val[it[2]] += it[3]
                    pos[k] += 1
                    progress = True
        stuck = {k: (pos[k], len(self.streams[k])) for k in self.ENG if pos[k] < len(self.streams[k])}
        assert not stuck, ("DEADLOCK in emitted program", stuck, {k: self.streams[k][pos[k]] for k in stuck})
        for k, c in self.cnt.items():
            assert val[k] == c, (k, val[k], c)

    def run(self, eng, e):
        sem = self.sem
        for it in self.streams[eng]:
            if it[0] == "w":
                e.wait_ge(sem[it[1]], it[2])
            else:
                ins = it[1](e)
                if it[2] is not None:
                    ins.then_inc(sem[it[2]], it[3])


class Arena:
    def __init__(self, ap, nbytes):
        self.ap = ap
        self.nbytes = nbytes
        self.off = 0

    def reset(self, off=0):
        self.off = off

    def alloc(self, shape, dtype, parts=128):
        esz = {F32: 4, BF16: 2, I32: 4, U8: 1}[dtype]
        n = 1
        for s in shape:
            n *= s
        nb = (n * esz + 63) // 64 * 64
        assert self.off + nb <= self.nbytes, ("arena overflow", self.off, nb, self.nbytes)
        a = self.ap[0:parts, self.off:self.off + n * esz]
        self.off += nb
        if dtype != U8:
            a = a.bitcast(dtype)
        if len(shape) == 2:
            a = a.rearrange("p (a b) -> p a b", a=shape[0])
        elif len(shape) == 3:
            a = a.rearrange("p (a b c) -> p a b c", a=shape[0], b=shape[1])
        return a


def mm(out, lhsT, rhs, start=True, stop=True):
    return lambda e: e.matmul(out, lhsT=lhsT, rhs=rhs, start=start, stop=stop)


def TT(o, a, b, op):
    return lambda e: e.tensor_tensor(out=o, in0=a, in1=b, op=op)


def TS(o, a, s1, s2, op0, op1=None):
    if op1 is None:
        return lambda e: e.tensor_scalar(out=o, in0=a, scalar1=s1, scalar2=None, op0=op0)
    return lambda e: e.tensor_scalar(out=o, in0=a, scalar1=s1, scalar2=s2, op0=op0, op1=op1)


def STT(o, a, s, b, op0, op1):
    return lambda e: e.scalar_tensor_tensor(out=o, in0=a, scalar=s, in1=b, op0=op0, op1=op1)


def TR(o, a, op):
    return lambda e: e.tensor_reduce(out=o, in_=a, axis=AX.X, op=op)


def RCP(o, a):
    return lambda e: e.reciprocal(out=o, in_=a)


def CP(o, a):
    return lambda e: e.tensor_copy(out=o, in_=a)


def ACP(o, a):
    return lambda e: e.copy(out=o, in_=a)


def ACTV(o, a, f, **kw):
    return lambda e: e.activation(out=o, in_=a, func=f, **kw)


def MS(o, v):
    return lambda e: e.memset(o, v)


def TSS(o, a, s, op):
    return lambda e: e.tensor_single_scalar(out=o, in_=a, scalar=s, op=op)


def TP(o, a, ident):
    return lambda e: e.transpose(o, a, ident)


def build_program(T, debug=False):
    assert T % 512 == 0
    NT = T // 512
    NB = T // BLK
    NS = T // 128
    assert NB <= 16
    nc = bass.Bass("TRN2", target_bir_lowering=False)

    def din(name, shape, dt):
        return nc.dram_tensor(name, shape, dt, kind="ExternalInput").ap()

    def dscr(name, shape, dt):
        return nc.dram_tensor(name, shape, dt, kind=("ExternalOutput" if debug else "Internal")).ap()

    xT_in = din("xT", [D, T], F32)
    pos_in = din("pos", [1, T], I32)
    gains_in = din("gains", [128, 80], F32)
    invf_in = din("invf", [128, 1], F32)
    perm_in = din("perm", [128, 128], BF16)
    identb_in = din("identb", [128, 128], BF16)
    identf_in = din("identf", [128, 128], F32)
    erows_in = din("erows", [16, T], BF16)
    pastb_in = din("pastb", [128, NS * 16], F32)
    past01_in = din("past01", [128, NS * 16], F32)
    own01_in = din("own01", [128, NS * 16], F32)
    causal_in = din("causal2", [128, 512], BF16)
    bandp_in = din("bandprev", [128, 512], BF16)
    bando_in = din("bandown", [128, 512], BF16)
    wqkv_in = din("wqkv", [2, D, 3 * D], F32)
    woa_in = din("woa", [2, D, D], F32)
    wkvb_in = din("wkvb", [D, 512], F32)
    wqb_in = din("wqb", [2, D, D], F32)
    wob_in = din("wob", [2, D, D], F32)
    sinks_in = din("sinks", [1, 32], F32)
    wup_in = din("wup", [4, 2 * NPAIR, 128, 8 * 128], F32)
    wdn_in = din("wdn", [4, DFF, D], F32)
    cw_in = din("cw", [4, 128, 3 * 2 * NPAIR], F32)
    cb_in = din("cb", [4, 128, 2 * NPAIR], F32)
    yT_out = nc.dram_tensor("yT", [D, T], F32, kind="ExternalOutput").ap()

    Xs = dscr("Xs", [D, T], F32)
    Ms = dscr("Ms", [D, T], F32)
    qTs = dscr("qTs", [D, T], BF16)
    kTs = dscr("kTs", [D, T], BF16)
    vs = dscr("vs", [T, D], BF16)
    oTs = dscr("oTs", [D, T], BF16)
    aTs = dscr("aTs", [DFF, T], BF16)
    Cs = dscr("Cs", [128, T], F32)
    Ss = dscr("Ss", [128, T], F32)
    kTsh = dscr("kTsh", [256, T], BF16)
    vsh = dscr("vsh", [T, 256], BF16)

    stack = contextlib.ExitStack()
    with stack:
        P = Prog(nc, stack)
        arena_t = stack.enter_context(nc.sbuf_tensor("arena", [128, ARENA_BYTES], U8))
        A = Arena(arena_t, ARENA_BYTES)
        banks = [stack.enter_context(nc.psum_tensor("bank%d" % i, [128, 512], F32)) for i in range(8)]
        BK = [b[:, :] for b in banks]
        bkb = bufs(8)

        dX = bufs(NT); dM = bufs(NT)
        dq = Buf(); dk = Buf(); dv = Buf(); do = Buf(); da = Buf(); dcs = Buf()
        dksh = Buf(); dvsh = Buf()

        gains = A.alloc([80], F32)
        invf = A.alloc([1], F32)
        perm = A.alloc([128], BF16)
        identb = A.alloc([128], BF16)
        identf = A.alloc([128], F32)
        onesf = A.alloc([128], F32)
        cwt = A.alloc([4 * 3 * 2 * NPAIR], F32)
        cbt = A.alloc([4 * 2 * NPAIR], F32)
        sinkt = A.alloc([32], F32)
        esink = A.alloc([32], F32)
        cbuf = Buf()
        for dst, src_ in ((gains, gains_in[:, :]), (invf, invf_in[:, :]), (perm, perm_in[:, :]),
                          (identb, identb_in[:, :]), (identf, identf_in[:, :]),
                          (sinkt, sinks_in[0:1, :].broadcast_to([128, 32]))):
            P.dma("sp", dst, src_, writes=[cbuf])
        NCW = 3 * 2 * NPAIR
        NCB = 2 * NPAIR
        for l in range(4):
            P.dma("sp", cwt[:, l * NCW:(l + 1) * NCW], cw_in[l, :, :], writes=[cbuf])
            P.dma("sp", cbt[:, l * NCB:(l + 1) * NCB], cb_in[l, :, :], writes=[cbuf])
        P.op("dve", MS(onesf, 1.0), writes=[cbuf])
        P.op("act", ACTV(esink, sinkt, AF.Exp), reads=[cbuf], writes=[cbuf])
        P.barrier()
        BASE = A.off

        def stage_rope():
            A.reset(BASE)
            W = 512
            pi_ = A.alloc([W], I32); ang = A.alloc([W], F32); t_ = A.alloc([W], F32)
            ki = A.alloc([W], I32); kf = A.alloc([W], F32); r_ = A.alloc([W], F32)
            m_ = A.alloc([W], F32); rc = A.alloc([W], F32)
            so = A.alloc([W], F32); co = A.alloc([W], F32)
            b = Buf()
            PI = float(np.pi)

            def D_(fn):
                P.op("dve", fn, reads=[b, cbuf], writes=[b])

            def fix(rr):
                D_(TSS(m_, rr, PI, ALU.is_gt))
                D_(STT(rr, m_, -TWO_PI, rr, ALU.mult, ALU.add))
                D_(TSS(m_, rr, -PI, ALU.is_lt))
                D_(STT(rr, m_, TWO_PI, rr, ALU.mult, ALU.add))
                D_(TS(rr, rr, 3.141592, -3.141592, ALU.min, ALU.max))

            for ci in range(T // W):
                cs = slice(ci * W, (ci + 1) * W)
                P.dma("sp", pi_, pos_in[0:1, cs].broadcast_to([128, W]), writes=[b])
                D_(CP(ang, pi_))
                D_(TS(ang, ang, invf[:, 0:1], None, ALU.mult))
                D_(TS(t_, ang, 1.0 / TWO_PI, None, ALU.mult))
                D_(CP(ki, t_))
                D_(CP(kf, ki))
                D_(STT(r_, kf, -C1, ang, ALU.mult, ALU.add))
                D_(STT(r_, kf, -C2, r_, ALU.mult, ALU.add))
                fix(r_)
                D_(TS(rc, r_, PI / 2, None, ALU.add))
                fix(rc)
                P.op("act", ACTV(so, r_, AF.Sin), reads=[b], writes=[b])
                P.op("act", ACTV(co, rc, AF.Sin), reads=[b], writes=[b])
                P.dma("sp", Ss[:, cs], so, reads=[b], writes=[dcs])
                P.dma("sp", Cs[:, cs], co, reads=[b], writes=[dcs])
            P.barrier()

        def norm_rstd(xt, xtb, sq, sqb, ssbank, rstd, rstdb):
            P.op("act", ACTV(sq, xt, AF.Square), reads=[xtb], writes=[sqb])
            P.pe_group([mm(BK[ssbank], onesf, sq[:, c, :], start=(c == 0), stop=(c == 7)) for c in range(8)],
                       reads=[sqb, cbuf], writes=[bkb[ssbank]])
            P.op("act", ACTV(rstd, BK[ssbank], AF.Sqrt, scale=1.0 / D, bias=EPS), reads=[bkb[ssbank]], writes=[rstdb])
            P.op("dve", RCP(rstd, rstd), reads=[rstdb], writes=[rstdb])

        def norm_apply(xt, xtb, rstd, rstdb, gi, outs, outb):
            for c in range(8):
                P.op("dve", STT(outs[c], xt[:, c, :], gains[:, gi * 8 + c:gi * 8 + c + 1], rstd, ALU.mult, ALU.mult),
                     reads=[xtb, rstdb, cbuf], writes=[outb])

        def load_w(dst, src_ap, wb):
            P.dma("pool", dst, src_ap.rearrange("(kc p) n -> p kc n", p=128), writes=[wb])

        def stage_proj(l, Xsrc, dXsrc):
            A.reset(BASE)
            moba = l < 2
            mk_kv = (l == 2)
            wb = Buf()
            wkv = None
            if moba:
                wsb = A.alloc([8, 3 * D], BF16)
                load_w(wsb, wqkv_in[l], wb)
                nqk = 16
            else:
                wsb = A.alloc([8, D], BF16)
                load_w(wsb, wqb_in[l - 2], wb)
                nqk = 8
                if mk_kv:
                    wkv = A.alloc([8, 512], BF16)
                    load_w(wkv, wkvb_in, wb)
            xt = [A.alloc([8, 512], F32) for _ in range(2)]; xtb = bufs(2)
            ct = [A.alloc([512], F32) for _ in range(2)]; st = [A.alloc([512], F32) for _ in range(2)]; csb = bufs(2)
            sq = A.alloc([8, 512], F32); sqb = Buf()
            rstd = A.alloc([512], F32); rstdb = Buf()
            hT = [A.alloc([8, 512], BF16) for _ in range(2)]; hTb = bufs(2)
            hK = None; hKb = None
            if mk_kv:
                hK = [A.alloc([8, 512], BF16) for _ in range(2)]; hKb = bufs(2)
            qb = [A.alloc([512], BF16) for _ in range(2)]; qbb = bufs(2)
            t1 = [A.alloc([512], F32) for _ in range(2)]; t1b = bufs(2)
            u1 = [A.alloc([512], F32) for _ in range(2)]; u1b = bufs(2)
            qr = [A.alloc([512], BF16) for _ in range(3)]; qrb = bufs(3)
            vt = [A.alloc([4, 1024], BF16) for _ in range(2)]; vtb = bufs(2)
            SSB, PA, PB, PV = 0, (1, 2), (3, 4), (5, 6)

            def load_x(i):
                b = i % 2
                cs = slice(i * 512, (i + 1) * 512)
                P.dma("sp", xt[b], Xsrc[:, cs].rearrange("(c p) n -> p c n", p=128), reads=[dXsrc[i]], writes=[xtb[b]])

            def load_cs(i):
                b = i % 2
                cs = slice(i * 512, (i + 1) * 512)
                P.dma("sp", ct[b], Cs[:, cs], reads=[dcs], writes=[csb[b]])
                P.dma("sp", st[b], Ss[:, cs], reads=[dcs], writes=[csb[b]])

            state = {"k": 0, "kv": 0}
            PA3 = (1, 2, 7)

            def qk_front(job):
                b, wts, hsrc, hsrcb, dst_ap, dbuf = job
                k = state["k"]; state["k"] += 1
                pa = PA3[k % 3]; r2 = k % 2; r3 = k % 3
                P.pe_group([mm(BK[pa], wts[kc], hsrc[:, kc, :], start=(kc == 0), stop=(kc == 7)) for kc in range(8)],
                           reads=[wb, hsrcb], writes=[bkb[pa]])
                P.op("act", ACP(qb[r2], BK[pa]), reads=[bkb[pa]], writes=[qbb[r2]])
                return (k, r2, r3)

            def qk_back(job, tag):
                b, wts, hsrc, hsrcb, dst_ap, dbuf = job
                k, r2, r3 = tag
                pb = PB[k % 2]
                P.pe_group([mm(BK[pb], perm, qb[r2])], reads=[qbb[r2], cbuf], writes=[bkb[pb]])
                P.op("pool", TT(t1[r2], qb[r2], ct[b], ALU.mult), reads=[qbb[r2], csb[b]], writes=[t1b[r2]])
                P.op("dve", TT(u1[r2], BK[pb], st[b], ALU.mult), reads=[bkb[pb], csb[b]], writes=[u1b[r2]])
                P.op("dve", TT(qr[r3], u1[r2], t1[r2], ALU.add), reads=[u1b[r2], t1b[r2]], writes=[qrb[r3]])
                P.dma("sp", dst_ap, qr[r3], reads=[qrb[r3]], writes=[dbuf])

            def v_job(i, b, s, g):
                vb = i % 2
                pv = PV[state["kv"] % 2]; state["kv"] += 1
                if moba:
                    fns = [mm(BK[pv], hT[b][:, kc, s * 128:(s + 1) * 128], wsb[:, kc, 2048 + g * 512:2048 + (g + 1) * 512],
                              start=(kc == 0), stop=(kc == 7)) for kc in range(8)]
                    P.pe_group(fns, reads=[wb, hTb[b]], writes=[bkb[pv]])
                    P.op("act", ACP(vt[vb][:, s, g * 512:(g + 1) * 512], BK[pv]), reads=[bkb[pv]], writes=[vtb[vb]])
                else:
                    fns = [mm(BK[pv][:, 0:256], hK[b][:, kc, s * 128:(s + 1) * 128], wkv[:, kc, 256:512],
                              start=(kc == 0), stop=(kc == 7)) for kc in range(8)]
                    P.pe_group(fns, reads=[wb, hKb[b]], writes=[bkb[pv]])
                    P.op("act", ACP(vt[vb][:, s, 0:256], BK[pv][:, 0:256]), reads=[bkb[pv]], writes=[vtb[vb]])

            def do_norm(i):
                b = i % 2
                norm_rstd(xt[b], xtb[b], sq, sqb, SSB, rstd, rstdb)
                norm_apply(xt[b], xtb[b], rstd, rstdb, l, [hT[b][:, c, :] for c in range(8)], hTb[b])
                if mk_kv:
                    norm_apply(xt[b], xtb[b], rstd, rstdb, 8, [hK[b][:, c, :] for c in range(8)], hKb[b])

            load_x(0); load_cs(0)
            if NT > 1:
                load_x(1); load_cs(1)
            do_norm(0)
            for i in range(NT):
                b = i % 2
                cs = slice(i * 512, (i + 1) * 512)
                jobs = []
                for m in range(nqk):
                    if m < 8:
                        dst, dbuf = qTs[m * 128:(m + 1) * 128, cs], dq
                    else:
                        dst, dbuf = kTs[(m - 8) * 128:(m - 7) * 128, cs], dk
                    jobs.append((b, [wsb[:, kc, m * 128:(m + 1) * 128] for kc in range(8)], hT[b], hTb[b], dst, dbuf))
                if mk_kv:
                    for m in range(2):
                        jobs.append((b, [wkv[:, kc, m * 128:(m + 1) * 128] for kc in range(8)], hK[b], hKb[b],
                                     kTsh[m * 128:(m + 1) * 128, cs], dksh))
                vjobs = []
                if moba or mk_kv:
                    ngrp = 2 if moba else 1
                    vjobs = [(s, g) for s in range(4) for g in range(ngrp)]
                if i + 2 < NT:
                    load_x(i + 2)
                tags = {}
                tags[0] = qk_front(jobs[0])
                nj = len(jobs)
                vper = (len(vjobs) + nj - 1) // nj if vjobs else 0
                vi = 0
                for j in range(nj):
                    if j + 1 < nj:
                        tags[j + 1] = qk_front(jobs[j + 1])
                    for _ in range(vper):
                        if vi < len(vjobs):
                            v_job(i, b, *vjobs[vi]); vi += 1
                    qk_back(jobs[j], tags.pop(j))
                    if j == nj // 2 - 1 and i + 1 < NT:
                        do_norm(i + 1)
                while vi < len(vjobs):
                    v_job(i, b, *vjobs[vi]); vi += 1
                if moba or mk_kv:
                    vb = i % 2
                    if moba:
                        P.dma("sp", vs[cs, :].rearrange("(s p) f -> p s f", p=128), vt[vb], reads=[vtb[vb]], writes=[dv])
                    else:
                        P.dma("sp", vsh[cs, :].rearrange("(s p) f -> p s f", p=128), vt[vb][:, :, 0:256], reads=[vtb[vb]], writes=[dvsh])
                if i + 2 < NT:
                    load_cs(i + 2)
            P.barrier()

        def stage_moba():
            A.reset(BASE)
            G = NS * 16
            pastb = A.alloc([G], F32); past01 = A.alloc([G], F32); own01 = A.alloc([G], F32)
            causal = A.alloc([2, 256], BF16)
            tb = Buf()
            P.dma("sp", pastb, pastb_in[:, :], writes=[tb]); P.dma("sp", past01, past01_in[:, :], writes=[tb])
            P.dma("sp", own01, own01_in[:, :], writes=[tb])
            P.dma("sp", causal, causal_in[:, :].rearrange("p (a b) -> p a b", a=2), writes=[tb])
            Kaug = [A.alloc([T], BF16) for _ in range(2)]; Kb = bufs(2)
            Qaug = [A.alloc([T], BF16) for _ in range(2)]; Qhb = bufs(2); Qlb = bufs(2)
            Vh = [A.alloc([NS, 128], BF16) for _ in range(2)]; Vb = bufs(2)
            oh = [A.alloc([T], BF16, parts=64) for _ in range(2)]; ohb = bufs(2)
            km = A.alloc([16], F32); kmb16 = A.alloc([16], BF16); kmb = Buf()
            gm = A.alloc([G], F32); g2 = A.alloc([G], F32); eq = A.alloc([G], F32); sel = A.alloc([G], F32)
            mx = A.alloc([NS], F32); mbb = A.alloc([G + 128], BF16); gb = Buf()
            Pt = [A.alloc([512], BF16) for _ in range(4)]; Ptb = bufs(4)
            rden = A.alloc([256], F32); rdb = Buf()
            GB, TB_, SB, OB = 0, 1, (2, 3, 4, 7), (5, 6)
            tbk = BK[TB_].bitcast(BF16)

            def g3(ap):
                return ap.rearrange("p (a b) -> p a b", b=16)

            def bc(ap):
                return ap.unsqueeze(2).to_broadcast([128, NS, 16])

            P.op("dve", MS(km, 0.0), writes=[kmb])
            P.op("dve", MS(mbb, 0.0), writes=[gb])
            for b in range(2):
                P.dma("sp", Kaug[b][0:16, :], erows_in[:, :], writes=[Kb[b]])
                P.op("dve", MS(Qaug[b][0:16, :], 0.0), writes=[Qlb[b]])
                P.op("dve", MS(Vh[b][:, :, 64:128], 1.0), writes=[Vb[b]])

            def load(h):
                b = h % 2
                P.dma("sp", Kaug[b][16:80, :], kTs[h * 64:(h + 1) * 64, :], reads=[dk], writes=[Kb[b]])
                P.dma("sp", Qaug[b][16:80, :], qTs[h * 64:(h + 1) * 64, :], reads=[dq], writes=[Qhb[b]])
                P.dma("sp", Vh[b][:, :, 0:64], vs[:, h * 64:(h + 1) * 64].rearrange("(c p) d -> p c d", p=128), reads=[dv], writes=[Vb[b]])

            def GD(fn, extra=()):
                P.op("dve", fn, reads=[gb] + list(extra), writes=[gb])

            def gate_front(h):
                b = h % 2
                P.op("dve", TR(km[0:80, 0:NB], Kaug[b][0:80, :].rearrange("p (n k) -> p n k", k=BLK), ALU.add), reads=[Kb[b]], writes=[kmb])
                P.op("dve", MS(km[0:16, :], 0.0), reads=[kmb], writes=[kmb])
                P.op("dve", TS(kmb16[0:80, :], km[0:80, :], 1.0 / BLK, None, ALU.mult), reads=[kmb], writes=[kmb])
                P.pe_group([mm(BK[GB][:, s * 16:(s + 1) * 16], Qaug[b][0:80, s * 128:(s + 1) * 128], kmb16[0:80, :]) for s in range(NS)],
                           reads=[Qhb[b], Qlb[b], kmb], writes=[bkb[GB]])
                P.op("dve", TT(gm, BK[GB][:, 0:G], pastb, ALU.add), reads=[bkb[GB], tb], writes=[gb])
                GD(TR(mx, g3(gm), ALU.max))
                GD(TT(g3(eq), g3(gm), bc(mx), ALU.is_equal))
                GD(STT(g2, eq, -1e9, gm, ALU.mult, ALU.add))
                GD(TR(mx, g3(g2), ALU.max))
                GD(TT(g3(eq), g3(g2), bc(mx), ALU.is_equal))
                GD(STT(g2, eq, -1e9, g2, ALU.mult, ALU.add))
                GD(TR(mx, g3(g2), ALU.max))
                GD(TT(g3(sel), g3(gm), bc(mx), ALU.is_ge))
                GD(TT(sel, sel, past01, ALU.mult), [tb])
                GD(TT(sel, sel, own01, ALU.add), [tb])
                GD(TS(mbb[:, 0:G], sel, -NEGB, NEGB, ALU.mult, ALU.add))

            def gate_back(h):
                b = h % 2
                for grp in range((NS + 7) // 8):
                    n8 = min(8, NS - grp * 8)
                    P.pe_group([TP(tbk[:, j * 128:(j + 1) * 128], mbb[:, s * 16:s * 16 + 128], identb)
                                for j, s in enumerate(range(grp * 8, grp * 8 + n8))],
                               reads=[gb, cbuf], writes=[bkb[TB_]])
                    P.op("act", ACP(Qaug[b][0:16, grp * 1024:grp * 1024 + n8 * 128], tbk[0:16, 0:n8 * 128]),
                         reads=[bkb[TB_]], writes=[Qlb[b]])

            LOOK = 2
            load(0)
            gate_front(0)
            gate_back(0)
            sk = 0
            for h in range(NH):
                b = h % 2
                if h + 1 < NH:
                    load(h + 1)
                items = [(i, n) for i in range(NB) for n in range(i + 1)]
                slot = {}

                def emit_qk(t):
                    nonlocal sk
                    i, n = items[t]
                    sb = SB[sk % len(SB)]; pt = sk % len(Pt); sk += 1
                    slot[t] = (sb, pt)
                    qs = slice(i * BLK, (i + 1) * BLK)
                    fns = []
                    for c in range(2):
                        ks = slice((2 * n + c) * 128, (2 * n + c + 1) * 128)
                        if n < i:
                            fns.append(mm(BK[sb][:, c * 256:(c + 1) * 256], Kaug[b][0:80, ks], Qaug[b][0:80, qs]))
                        else:
                            fns.append(mm(BK[sb][:, c * 256:(c + 1) * 256], Kaug[b][0:80, ks], Qaug[b][0:80, qs], start=True, stop=False))
                            fns.append(mm(BK[sb][:, c * 256:(c + 1) * 256], identb, causal[:, c, :], start=False, stop=True))
                    P.pe_group(fns, reads=[Kb[b], Qhb[b], Qlb[b], tb, cbuf], writes=[bkb[sb]])
                    P.op("act", ACTV(Pt[pt], BK[sb], AF.Exp, scale=0.125), reads=[bkb[sb]], writes=[Ptb[pt]])

                def emit_pv(t):
                    i, n = items[t]
                    sb, pt = slot.pop(t)
                    ob = OB[i % 2]
                    qs = slice(i * BLK, (i + 1) * BLK)
                    fns = [mm(BK[ob][:, 0:256], Vh[b][:, 2 * n + c, :], Pt[pt][:, c * 256:(c + 1) * 256],
                              start=(n == 0 and c == 0), stop=(n == i and c == 1)) for c in range(2)]
                    P.pe_group(fns, reads=[Vb[b], Ptb[pt]], writes=[bkb[ob]])
                    if n == i:
                        P.op("dve", RCP(rden[64:128, :], BK[ob][64:128, 0:256]), reads=[bkb[ob]], writes=[rdb])
                        P.op("dve", TT(oh[b][0:64, qs], BK[ob][0:64, 0:256], rden[64:128, :], ALU.mult),
                             reads=[bkb[ob], rdb], writes=[ohb[b]])

                NI = len(items)
                gf_blk = min(1, NB - 1)
                gb_blk = max(gf_blk, NB - 3)
                for t in range(min(LOOK, NI)):
                    emit_qk(t)
                for t in range(NI):
                    if t + LOOK < NI:
                        emit_qk(t + LOOK)
                    emit_pv(t)
                    i_, n_ = items[t]
                    if h + 1 < NH and n_ == i_:
                        if i_ == gf_blk:
                            gate_front(h + 1)
                        if i_ == gb_blk:
                            gate_back(h + 1)
                P.dma("sp", oTs[h * 64:(h + 1) * 64, :], oh[b], reads=[ohb[b]], writes=[do])
            P.barrier()

        def stage_swa(l):
            A.reset(BASE)
            li = l - 2
            bandp = A.alloc([512], BF16); bando = A.alloc([512], BF16)
            es = A.alloc([16, 128], F32)
            tb = Buf()
            P.dma("sp", bandp, bandp_in[:, :], writes=[tb]); P.dma("sp", bando, bando_in[:, :], writes=[tb])
            P.op("dve", CP(es, esink[:, li * 16:(li + 1) * 16].unsqueeze(2).to_broadcast([128, 16, 128])), reads=[cbuf], writes=[tb])
            Kg = [A.alloc([T], BF16, parts=64) for _ in range(2)]; Kb = bufs(2)
            Vg = [A.alloc([NS, 128], BF16) for _ in range(2)]; Vb = bufs(2)
            Qg = [A.alloc([4, T], BF16, parts=64) for _ in range(2)]; Qb = bufs(2)
            og = [A.alloc([4, T], BF16, parts=64) for _ in range(2)]; ogb = bufs(2)
            Pt = [A.alloc([512], BF16) for _ in range(4)]; Ptb = bufs(4)
            den = A.alloc([512], F32); rdb = Buf()
            onesb = A.alloc([128], BF16)
            SB, OB, DB = (0, 1, 2, 3), (4, 5), (6, 7)
            P.op("dve", MS(onesb, 1.0), writes=[tb])
            for b in range(2):
                P.op("dve", MS(Vg[b][:, :, 64:128], 1.0), writes=[Vb[b]])

            def load(g):
                b = g % 2
                P.dma("sp", Kg[b], kTsh[g * 64:(g + 1) * 64, :], reads=[dksh], writes=[Kb[b]])
                P.dma("sp", Vg[b][:, :, 0:64], vsh[:, g * 64:(g + 1) * 64].rearrange("(c p) d -> p c d", p=128), reads=[dvsh], writes=[Vb[b]])
                for hh in range(4):
                    hd = g * 4 + hh
                    P.dma("sp", Qg[b][:, hh, :], qTs[hd * 64:(hd + 1) * 64, :], reads=[dq], writes=[Qb[b]])

            load(0)
            sk = 0
            for g in range(NKV):
                b = g % 2
                if g + 1 < NKV:
                    load(g + 1)
                fr = {}

                def front(s):
                    nonlocal sk
                    qs = slice(s * 128, (s + 1) * 128)
                    chunks = ([(s - 1, bandp)] if s > 0 else []) + [(s, bando)]
                    pts = []
                    for (kc, band) in chunks:
                        sb = SB[sk % 4]; pt = sk % 4; sk += 1
                        P.pe_group([mm(BK[sb].rearrange("p (a b) -> p a b", a=4), Kg[b][:, kc * 128:(kc + 1) * 128], Qg[b][:, :, qs], start=True, stop=False),
                                    mm(BK[sb], identb, band, start=False, stop=True)],
                                   reads=[Kb[b], Qb[b], tb, cbuf], writes=[bkb[sb]])
                        P.op("act", ACTV(Pt[pt], BK[sb], AF.Exp, scale=0.125), reads=[bkb[sb]], writes=[Ptb[pt]])
                        pts.append((kc, pt))
                    fr[s] = pts

                def back(s):
                    pts = fr.pop(s)
                    ob = OB[s % 2]; db = DB[s % 2]
                    qs = slice(s * 128, (s + 1) * 128)
                    P.pe_group([mm(BK[ob], Vg[b][:, kc, :], Pt[pt], start=(j == 0), stop=(j == len(pts) - 1)) for j, (kc, pt) in enumerate(pts)],
                               reads=[Vb[b]] + [Ptb[pt] for _, pt in pts], writes=[bkb[ob]])
                    P.pe_group([mm(BK[db], onesb, Pt[pt], start=(j == 0), stop=(j == len(pts) - 1)) for j, (kc, pt) in enumerate(pts)],
                               reads=[tb] + [Ptb[pt] for _, pt in pts], writes=[bkb[db]])
                    P.op("dve", TT(den[0:64, :], BK[db][0:64, :], es[0:64, g * 4:(g + 1) * 4, :].rearrange("p a b -> p (a b)"), ALU.add),
                         reads=[bkb[db], tb], writes=[rdb])
                    P.op("dve", RCP(den[0:64, :], den[0:64, :]), reads=[rdb], writes=[rdb])
                    P.op("dve", TT(og[b][:, :, qs], BK[ob][0:64, :].rearrange("p (a b) -> p a b", a=4),
                                   den[0:64, :].rearrange("p (a b) -> p a b", a=4), ALU.mult),
                         reads=[bkb[ob], rdb], writes=[ogb[b]])

                front(0)
                for s in range(NS):
                    if s + 1 < NS:
                        front(s + 1)
                    back(s)
                for hh in range(4):
                    hd = g * 4 + hh
                    P.dma("sp", oTs[hd * 64:(hd + 1) * 64, :], og[b][:, hh, :], reads=[ogb[b]], writes=[do])
            P.barrier()

        def stage_oproj_ffn_up(l, Xsrc, dXsrc):
            A.reset(BASE)
            wb = Buf()
            hall = A.alloc([8, T], BF16); hallb = Buf()
            mark = A.off
            wo = A.alloc([8, D], BF16)
            load_w(wo, (woa_in[l] if l < 2 else wob_in[l - 2]), wb)
            ot = [A.alloc([8, 512], BF16) for _ in range(2)]; otb = bufs(2)
            xt = [A.alloc([8, 512], F32) for _ in range(2)]; xtb = bufs(2)
            xm = [A.alloc([8, 512], F32) for _ in range(2)]; xmb = bufs(2)
            rstd = A.alloc([512], F32); rstdb = Buf()
            SSB, PO = 0, (1, 2, 3)

            def load(i):
                b = i % 2
                cs = slice(i * 512, (i + 1) * 512)
                P.dma("sp", ot[b], oTs[:, cs].rearrange("(c p) n -> p c n", p=128), reads=[do], writes=[otb[b]])
                P.dma("sp", xt[b], Xsrc[:, cs].rearrange("(c p) n -> p c n", p=128), reads=[dXsrc[i]], writes=[xtb[b]])

            load(0)
            k = 0
            for i in range(NT):
                b = i % 2
                cs = slice(i * 512, (i + 1) * 512)
                if i + 1 < NT:
                    load(i + 1)
                for m in range(8):
                    po = PO[k % 3]; k += 1
                    P.pe_group([mm(BK[po], wo[:, kc, m * 128:(m + 1) * 128], ot[b][:, kc, :], start=(kc == 0), stop=(kc == 7)) for kc in range(8)],
                               reads=[wb, otb[b]], writes=[bkb[po]])
                    P.op("dve", TT(xm[b][:, m, :], BK[po], xt[b][:, m, :], ALU.add), reads=[bkb[po], xtb[b]], writes=[xmb[b]])
                P.dma("sp", Ms[:, cs].rearrange("(c p) n -> p c n", p=128), xm[b], reads=[xmb[b]], writes=[dM[i]])
                norm_rstd(xm[b], xmb[b], xt[b], xtb[b], SSB, rstd, rstdb)
                norm_apply(xm[b], xmb[b], rstd, rstdb, 4 + l, [hall[:, c, cs] for c in range(8)], hallb)
            P.barrier()

            A.reset(mark)
            wg = [A.alloc([8, 128], BF16) for _ in range(2)]; wv = [A.alloc([8, 128], BF16) for _ in range(2)]; wgb = bufs(2)
            dg = [A.alloc([6, 128], BF16) for _ in range(2)]; dgb = bufs(2)
            ug = [A.alloc([514], BF16) for _ in range(3)]; uv = [A.alloc([514], BF16) for _ in range(3)]; ugb = bufs(3); uvb = bufs(3)
            sg = [A.alloc([512], F32) for _ in range(2)]; sgb = bufs(2)
            at = [A.alloc([512], BF16) for _ in range(3)]; atb = bufs(3)
            PG, PVv, PCG, PCV = (0, 1), (2, 3), (4, 5), (6, 7)

            def loadw(j):
                b = j % 2
                P.dma("pool", wg[b], wup_in[l, j].rearrange("p (kc n) -> p kc n", kc=8), writes=[wgb[b]])
                P.dma("pool", wv[b], wup_in[l, NPAIR + j].rearrange("p (kc n) -> p kc n", kc=8), writes=[wgb[b]])

            def pair_start(j):
                b = j % 2
                if j + 1 < NPAIR:
                    loadw(j + 1)
                for tap in range(3):
                    for gv in range(2):
                        col = l * NCW + tap * 2 * NPAIR + gv * NPAIR + j
                        P.op("dve", TS(dg[b][:, tap * 2 + gv, :], identf, cwt[:, col:col + 1], None, ALU.mult), reads=[cbuf], writes=[dgb[b]])
                u0 = (j * NT) % 3
                P.op("dve", MS(ug[u0][:, 0:2], 0.0), writes=[ugb[u0]])
                P.op("dve", MS(uv[u0][:, 0:2], 0.0), writes=[uvb[u0]])

            def front(k, j, i):
                b = j % 2
                cs = slice(i * 512, (i + 1) * 512)
                r = k % 2; ub = k % 3; un = (k + 1) % 3
                pg, pv = PG[r], PVv[r]
                P.pe_group([mm(BK[pg], wg[b][:, kc, :], hall[:, kc, cs], start=(kc == 0), stop=(kc == 7)) for kc in range(8)],
                           reads=[wgb[b], hallb], writes=[bkb[pg]])
                P.pe_group([mm(BK[pv], wv[b][:, kc, :], hall[:, kc, cs], start=(kc == 0), stop=(kc == 7)) for kc in range(8)],
                           reads=[wgb[b], hallb], writes=[bkb[pv]])
                P.op("act", ACP(ug[ub][:, 2:514], BK[pg]), reads=[bkb[pg]], writes=[ugb[ub]])
                P.op("act", ACP(uv[ub][:, 2:514], BK[pv]), reads=[bkb[pv]], writes=[uvb[ub]])
                if i + 1 < NT:
                    P.op("pool", CP(ug[un][:, 0:2], ug[ub][:, 512:514]), reads=[ugb[ub]], writes=[ugb[un]])
                    P.op("pool", CP(uv[un][:, 0:2], uv[ub][:, 512:514]), reads=[uvb[ub]], writes=[uvb[un]])

            def back(k, j, i):
                b = j % 2
                cs = slice(i * 512, (i + 1) * 512)
                r = k % 2; r3 = k % 3; ub = k % 3
                pcg, pcv = PCG[r], PCV[r]
                cg = l * NCB + j
                cv = l * NCB + NPAIR + j
                P.pe_group([mm(BK[pcg], dg[b][:, tap * 2 + 0, :], ug[ub][:, tap:tap + 512], start=(tap == 0), stop=(tap == 2)) for tap in range(3)],
                           reads=[dgb[b], ugb[ub]], writes=[bkb[pcg]])
                P.pe_group([mm(BK[pcv], dg[b][:, tap * 2 + 1, :], uv[ub][:, tap:tap + 512], start=(tap == 0), stop=(tap == 2)) for tap in range(3)],
                           reads=[dgb[b], uvb[ub]], writes=[bkb[pcv]])
                P.op("act", ACTV(sg[r], BK[pcg], AF.Silu, bias=cbt[:, cg:cg + 1]), reads=[bkb[pcg], cbuf], writes=[sgb[r]])
                P.op("dve", STT(at[r3], BK[pcv], cbt[:, cv:cv + 1], sg[r], ALU.add, ALU.mult),
                     reads=[bkb[pcv], sgb[r], cbuf], writes=[atb[r3]])
                P.dma("sp", aTs[j * 128:(j + 1) * 128, cs], at[r3], reads=[atb[r3]], writes=[da])

            loadw(0)
            items = [(j, i) for j in range(NPAIR) for i in range(NT)]
            pair_start(0)
            front(0, *items[0])
            for k, (j, i) in enumerate(items):
                if k + 1 < len(items):
                    jn, in_ = items[k + 1]
                    if in_ == 0:
                        pair_start(jn)
                    front(k + 1, jn, in_)
                back(k, j, i)
            P.barrier()

        def stage_ffn_down(l):
            A.reset(BASE)
            last = (l == 3)
            wb = Buf()
            wd = A.alloc([NPAIR, D], BF16)
            load_w(wd, wdn_in[l], wb)
            at = [A.alloc([NPAIR, 512], BF16) for _ in range(2)]; atb = bufs(2)
            xm = [A.alloc([8, 512], F32) for _ in range(2)]; xmb = bufs(2)
            xo = [A.alloc([8, 512], F32) for _ in range(2)]; xob = bufs(2)
            rstd = A.alloc([512], F32); rstdb = Buf()
            SSB, PO = 0, (1, 2, 3)
            dyo = Buf()

            def load(i):
                b = i % 2
                cs = slice(i * 512, (i + 1) * 512)
                P.dma("sp", at[b], aTs[:, cs].rearrange("(c p) n -> p c n", p=128), reads=[da], writes=[atb[b]])
                P.dma("sp", xm[b], Ms[:, cs].rearrange("(c p) n -> p c n", p=128), reads=[dM[i]], writes=[xmb[b]])

            load(0)
            k = 0
            for i in range(NT):
                b = i % 2
                cs = slice(i * 512, (i + 1) * 512)
                if i + 1 < NT:
                    load(i + 1)
                for m in range(8):
                    po = PO[k % 3]; k += 1
                    P.pe_group([mm(BK[po], wd[:, kc, m * 128:(m + 1) * 128], at[b][:, kc, :], start=(kc == 0), stop=(kc == NPAIR - 1)) for kc in range(NPAIR)],
                               reads=[wb, atb[b]], writes=[bkb[po]])
                    P.op("dve", TT(xo[b][:, m, :], BK[po], xm[b][:, m, :], ALU.add), reads=[bkb[po], xmb[b]], writes=[xob[b]])
                if not last:
                    P.dma("sp", Xs[:, cs].rearrange("(c p) n -> p c n", p=128), xo[b], reads=[xob[b]], writes=[dX[i]])
                else:
                    norm_rstd(xo[b], xob[b], xm[b], xmb[b], SSB, rstd, rstdb)
                    norm_apply(xo[b], xob[b], rstd, rstdb, 9, [xo[b][:, c, :] for c in range(8)], xob[b])
                    P.dma("sp", yT_out[:, cs].rearrange("(c p) n -> p c n", p=128), xo[b], reads=[xob[b]], writes=[dyo])
            P.barrier()

        stage_rope()
        dXin = bufs(NT)
        for l in range(4):
            Xsrc, dXsrc = (xT_in, dXin) if l == 0 else (Xs, dX)
            stage_proj(l, Xsrc, dXsrc)
            if l < 2:
                stage_moba()
            else:
                stage_swa(l)
            stage_oproj_ffn_up(l, Xsrc, dXsrc)
            stage_ffn_down(l)
        P.barrier()

        P.check()
        with nc.Block() as block:
            @block.tensor
            def _(e):
                P.run("pe", e)

            @block.scalar
            def _(e):
                P.run("act", e)

            @block.vector
            def _(e):
                P.run("dve", e)

            @block.gpsimd
            def _(e):
                P.run("pool", e)

            @block.sync
            def _(e):
                P.run("sp", e)
    return nc


def host_constants(T):
    bf = ml_dtypes.bfloat16
    NS = T // 128
    p = np.arange(128)
    d = p % 64
    invf8 = (np.float32(THETA) ** (-np.arange(0, 16, 2, dtype=np.float32) / np.float32(16))).astype(np.float32)
    invf = np.where(d < 16, invf8[d % 8], np.float32(0)).astype(np.float32).reshape(128, 1)
    perm = np.zeros((128, 128), np.float32)
    for m in range(128):
        dm = m % 64
        if dm < 8:
            perm[m + 8, m] = -1.0
        elif dm < 16:
            perm[m - 8, m] = 1.0
    ident = np.eye(128, dtype=np.float32)
    erows = np.zeros((16, T), np.float32)
    for n in range(T // BLK):
        erows[n, n * BLK:(n + 1) * BLK] = 1.0
    past01 = np.zeros((128, NS, 16), np.float32)
    own01 = np.zeros((128, NS, 16), np.float32)
    for s in range(NS):
        qb = (s * 128) // BLK
        past01[:, s, :qb] = 1.0
        own01[:, s, qb] = 1.0
    pastb = (past01 - 1.0) * 1e4
    kk = np.arange(128)[:, None]
    causal2 = np.zeros((128, 2, 256), np.float32)
    qq = np.arange(256)[None, :]
    for c in range(2):
        causal2[:, c, :] = np.where(c * 128 + kk <= qq, 0.0, NEGB)
    q1 = np.arange(128)[None, :]
    bandprev = np.tile(np.where(kk > q1, 0.0, NEGB), (1, 4))
    bandown = np.tile(np.where(kk <= q1, 0.0, NEGB), (1, 4))
    return {
        "invf": invf, "perm": perm.astype(bf), "identb": ident.astype(bf), "identf": ident,
        "erows": erows.astype(bf), "pastb": pastb.reshape(128, -1).astype(np.float32),
        "past01": past01.reshape(128, -1), "own01": own01.reshape(128, -1),
        "causal2": causal2.reshape(128, 512).astype(bf), "bandprev": bandprev.astype(bf), "bandown": bandown.astype(bf),
    }


def host_weights(attn_norm, w_qkv_a, w_o_a, kv_norm, w_kv_b, w_q_b, sinks_b, w_o_b, ffn_norm, w_up, conv_w, conv_b, w_down, final_norm):
    f = np.float32
    allg = np.concatenate([np.asarray(attn_norm, f), np.asarray(ffn_norm, f), np.asarray(kv_norm, f)[None], np.asarray(final_norm, f)[None]], 0)
    gains = np.ascontiguousarray(allg.reshape(10, 8, 128).transpose(2, 0, 1).reshape(128, 80))
    wup = np.asarray(w_up, f).reshape(4, 8, 128, 2 * NPAIR, 128).transpose(0, 3, 2, 1, 4)
    wup = np.ascontiguousarray(wup).reshape(4, 2 * NPAIR, 128, 1024)
    cw = np.asarray(conv_w, f).reshape(4, 3, 2 * NPAIR, 128).transpose(0, 3, 1, 2)
    cw = np.ascontiguousarray(cw).reshape(4, 128, 3 * 2 * NPAIR)
    cb = np.ascontiguousarray(np.asarray(conv_b, f).reshape(4, 2 * NPAIR, 128).transpose(0, 2, 1))
    return {
        "gains": gains,
        "wqkv": np.ascontiguousarray(np.asarray(w_qkv_a, f)), "woa": np.ascontiguousarray(np.asarray(w_o_a, f)),
        "wkvb": np.ascontiguousarray(np.asarray(w_kv_b, f)), "wqb": np.ascontiguousarray(np.asarray(w_q_b, f)),
        "wob": np.ascontiguousarray(np.asarray(w_o_b, f)), "sinks": np.ascontiguousarray(np.asarray(sinks_b, f).reshape(1, 32)),
        "wup": wup, "wdn": np.ascontiguousarray(np.asarray(w_down, f)), "cw": cw, "cb": cb,
    }


_CACHE = {}


def run_cores(x, positions, weights, T, debug=False):
    n = x.shape[0]
    key = (T, debug)
    if key not in _CACHE:
        _CACHE[key] = build_program(T, debug)
    nc = _CACHE[key]
    consts = host_constants(T)
    shared = dict(consts)
    shared.update(weights)
    in_maps = []
    for b in range(n):
        m = dict(shared)
        m["xT"] = np.ascontiguousarray(np.asarray(x[b], np.float32).T)
        m["pos"] = np.ascontiguousarray(np.asarray(positions[b], np.int32).reshape(1, T))
        in_maps.append(m)
    res = run_bass_kernel_spmd(nc, in_maps, core_ids=list(range(n)))
    return res


def kernel(x, positions, attn_norm, w_qkv_a, w_o_a, kv_norm, w_kv_b, w_q_b, sinks_b, w_o_b, ffn_norm, w_up, conv_w, conv_b, w_down, final_norm):
    x = np.asarray(x)
    B, T, _ = x.shape
    weights = host_weights(attn_norm, w_qkv_a, w_o_a, kv_norm, w_kv_b, w_q_b, sinks_b, w_o_b, ffn_norm, w_up, conv_w, conv_b, w_down, final_norm)
    res = run_cores(x, np.asarray(positions), weights, T)
    out = np.stack([np.ascontiguousarray(r["yT"].T) for r in res.results], 0)
    return out.astype(np.float32)
```

```python
import numpy as np
import ml_dtypes
import contextlib
import concourse.bass as bass
import concourse.mybir as mybir
from concourse.bass_utils import run_bass_kernel_spmd

F32 = mybir.dt.float32
BF16 = mybir.dt.bfloat16
I32 = mybir.dt.int32
U8 = mybir.dt.uint8
AF = mybir.ActivationFunctionType
ALU = mybir.AluOpType
AX = mybir.AxisListType

D = 1024
NH = 16
HD = 64
NKV = 4
DFF = 2816
NPAIR = DFF // 128
BLK = 256
WIN = 128
EPS = 1e-6
THETA = 500000.0
NEGB = -30000.0
TWO_PI = 6.283185307179586
C1 = 6.28125
C2 = TWO_PI - C1
ARENA_BYTES = 188 * 1024


class Buf:
    __slots__ = ("w", "r")

    def __init__(self):
        self.w = None
        self.r = {}


def bufs(n):
    return [Buf() for _ in range(n)]


class Prog:
    ENG = ("pe", "act", "dve", "pool", "sp")

    def __init__(self, nc, stack):
        self.nc = nc
        self.streams = {k: [] for k in self.ENG}
        self.sem = {}
        self.cnt = {}
        for k in ("pe", "act", "dve", "pool"):
            self._mk("s_" + k, stack)
        self.dsem = {"sp": [], "pool": [], "act": []}
        for k, n in (("sp", 14), ("pool", 6), ("act", 2)):
            for i in range(n):
                nm = "d_%s%d" % (k, i)
                self._mk(nm, stack)
                self.dsem[k].append(nm)
        self.drr = {"sp": 0, "pool": 0, "act": 0}
        self.waited = {k: {} for k in self.ENG}
        self.ninstr = 0

    def _mk(self, name, stack):
        self.sem[name] = stack.enter_context(self.nc.semaphore(name))
        self.cnt[name] = 0

    def wait(self, eng, tok):
        if tok is None:
            return
        name, val = tok
        if eng == "pe" and name == "s_pe":
            return
        if self.waited[eng].get(name, 0) >= val:
            return
        self.waited[eng][name] = val
        self.streams[eng].append(("w", name, val))

    def _deps(self, eng, reads, writes):
        for b in reads:
            self.wait(eng, b.w)
        for b in writes:
            self.wait(eng, b.w)
            for it in b.r.items():
                self.wait(eng, it)

    def _commit(self, tok, reads, writes):
        for b in reads:
            if b.r.get(tok[0], 0) < tok[1]:
                b.r[tok[0]] = tok[1]
        for b in writes:
            b.w = tok
            b.r = {}

    def op(self, eng, fn, reads=(), writes=()):
        self._deps(eng, reads, writes)
        name = "s_" + eng
        self.cnt[name] += 1
        tok = (name, self.cnt[name])
        self.streams[eng].append(("i", fn, name, 1))
        self._commit(tok, reads, writes)
        self.ninstr += 1
        return tok

    def pe_group(self, fns, reads=(), writes=()):
        self._deps("pe", reads, writes)
        for fn in fns[:-1]:
            self.streams["pe"].append(("i", fn, None, 0))
        self.cnt["s_pe"] += 1
        tok = ("s_pe", self.cnt["s_pe"])
        self.streams["pe"].append(("i", fns[-1], "s_pe", 1))
        self._commit(tok, reads, writes)
        self.ninstr += len(fns)
        return tok

    def dma(self, eng, out, in_, reads=(), writes=()):
        self._deps(eng, reads, writes)
        lst = self.dsem[eng]
        name = lst[self.drr[eng] % len(lst)]
        self.drr[eng] += 1
        if self.cnt[name] > 0:
            self.wait(eng, (name, self.cnt[name]))
        self.cnt[name] += 16
        tok = (name, self.cnt[name])
        self.streams[eng].append(("i", (lambda e, o=out, i=in_: e.dma_start(out=o, in_=i)), name, 16))
        self._commit(tok, reads, writes)
        self.ninstr += 1
        return tok

    def barrier(self):
        for eng in self.ENG:
            for name, c in self.cnt.items():
                if c > 0:
                    self.wait(eng, (name, c))

    def check(self):
        pos = {k: 0 for k in self.ENG}
        val = {k: 0 for k in self.cnt}
        progress = True
        while progress:
            progress = False
            for k in self.ENG:
                st = self.streams[k]
                while pos[k] < len(st):
                    it = st[pos[k]]
                    if it[0] == "w":
                        if val[it[1]] < it[2]:
                            break
                    elif it[2] is not None:
                        val[it[2]] += it[3]
                    pos[k] += 1
                    progress = True
        stuck = {k: (pos[k], len(self.streams[k])) for k in self.ENG if pos[k] < len(self.streams[k])}
        assert not stuck, ("DEADLOCK in emitted program", stuck, {k: self.streams[k][pos[k]] for k in stuck})
        for k, c in self.cnt.items():
            assert val[k] == c, (k, val[k], c)

    def run(self, eng, e):
        sem = self.sem
        for it in self.streams[eng]:
            if it[0] == "w":
                e.wait_ge(sem[it[1]], it[2])
            else:
                ins = it[1](e)
                if it[2] is not None:
                    ins.then_inc(sem[it[2]], it[3])


class Arena:
    def __init__(self, ap, nbytes):
        self.ap = ap
        self.nbytes = nbytes
        self.off = 0

    def reset(self, off=0):
        self.off = off

    def alloc(self, shape, dtype, parts=128):
        esz = {F32: 4, BF16: 2, I32: 4, U8: 1}[dtype]
        n = 1
        for s in shape:
            n *= s
        nb = (n * esz + 63) // 64 * 64
        assert self.off + nb <= self.nbytes, ("arena overflow", self.off, nb, self.nbytes)
        a = self.ap[0:parts, self.off:self.off + n * esz]
        self.off += nb
        if dtype != U8:
            a = a.bitcast(dtype)
        if len(shape) == 2:
            a = a.rearrange("p (a b) -> p a b", a=shape[0])
        elif len(shape) == 3:
            a = a.rearrange("p (a b c) -> p a b c", a=shape[0], b=shape[1])
        return a


def mm(out, lhsT, rhs, start=True, stop=True):
    return lambda e: e.matmul(out, lhsT=lhsT, rhs=rhs, start=start, stop=stop)


def TT(o, a, b, op):
    return lambda e: e.tensor_tensor(out=o, in0=a, in1=b, op=op)


def TS(o, a, s1, s2, op0, op1=None):
    if op1 is None:
        return lambda e: e.tensor_scalar(out=o, in0=a, scalar1=s1, scalar2=None, op0=op0)
    return lambda e: e.tensor_scalar(out=o, in0=a, scalar1=s1, scalar2=s2, op0=op0, op1=op1)


def STT(o, a, s, b, op0, op1):
    return lambda e: e.scalar_tensor_tensor(out=o, in0=a, scalar=s, in1=b, op0=op0, op1=op1)


def TR(o, a, op):
    return lambda e: e.tensor_reduce(out=o, in_=a, axis=AX.X, op=op)


def RCP(o, a):
    return lambda e: e.reciprocal(out=o, in_=a)


def CP(o, a):
    return lambda e: e.tensor_copy(out=o, in_=a)


def ACP(o, a):
    return lambda e: e.copy(out=o, in_=a)


def ACTV(o, a, f, **kw):
    return lambda e: e.activation(out=o, in_=a, func=f, **kw)


def MS(o, v):
    return lambda e: e.memset(o, v)


def TSS(o, a, s, op):
    return lambda e: e.tensor_single_scalar(out=o, in_=a, scalar=s, op=op)


def TP(o, a, ident):
    return lambda e: e.transpose(o, a, ident)


def build_program(T, debug=False):
    assert T % 512 == 0
    NT = T // 512
    NB = T // BLK
    NS = T // 128
    assert NB <= 16
    nc = bass.Bass("TRN2", target_bir_lowering=False)

    def din(name, shape, dt):
        return nc.dram_tensor(name, shape, dt, kind="ExternalInput").ap()

    def dscr(name, shape, dt):
        return nc.dram_tensor(name, shape, dt, kind=("ExternalOutput" if debug else "Internal")).ap()

    xT_in = din("xT", [D, T], F32)
    pos_in = din("pos", [1, T], I32)
    gains_in = din("gains", [128, 80], F32)
    invf_in = din("invf", [128, 1], F32)
    perm_in = din("perm", [128, 128], BF16)
    identb_in = din("identb", [128, 128], BF16)
    identf_in = din("identf", [128, 128], F32)
    erows_in = din("erows", [16, T], BF16)
    pastb_in = din("pastb", [128, NS * 16], F32)
    past01_in = din("past01", [128, NS * 16], F32)
    own01_in = din("own01", [128, NS * 16], F32)
    causal_in = din("causal2", [128, 512], BF16)
    bandp_in = din("bandprev", [128, 512], BF16)
    bando_in = din("bandown", [128, 512], BF16)
    wqkv_in = din("wqkv", [2, D, 3 * D], F32)
    woa_in = din("woa", [2, D, D], F32)
    wkvb_in = din("wkvb", [D, 512], F32)
    wqb_in = din("wqb", [2, D, D], F32)
    wob_in = din("wob", [2, D, D], F32)
    sinks_in = din("sinks", [1, 32], F32)
    wup_in = din("wup", [4, 2 * NPAIR, 128, 8 * 128], F32)
    wdn_in = din("wdn", [4, DFF, D], F32)
    cw_in = din("cw", [4, 128, 3 * 2 * NPAIR], F32)
    cb_in = din("cb", [4, 128, 2 * NPAIR], F32)
    yT_out = nc.dram_tensor("yT", [D, T], F32, kind="ExternalOutput").ap()

    Xs = dscr("Xs", [D, T], F32)
    Ms = dscr("Ms", [D, T], F32)
    qTs = dscr("qTs", [D, T], BF16)
    kTs = dscr("kTs", [D, T], BF16)
    vs = dscr("vs", [T, D], BF16)
    oTs = dscr("oTs", [D, T], BF16)
    aTs = dscr("aTs", [DFF, T], BF16)
    Cs = dscr("Cs", [128, T], F32)
    Ss = dscr("Ss", [128, T], F32)
    kTsh = dscr("kTsh", [256, T], BF16)
    vsh = dscr("vsh", [T, 256], BF16)

    stack = contextlib.ExitStack()
    with stack:
        P = Prog(nc, stack)
        arena_t = stack.enter_context(nc.sbuf_tensor("arena", [128, ARENA_BYTES], U8))
        A = Arena(arena_t, ARENA_BYTES)
        ps_all = stack.enter_context(nc.psum_tensor("ps_all", [128, 8 * 512], F32))
        BK = [ps_all[:, i * 512:(i + 1) * 512] for i in range(8)]
        bkb = bufs(8)

        dX = bufs(NT); dM = bufs(NT)
        dq = Buf(); dk = Buf(); dv = Buf(); do = Buf(); da = Buf(); dcs = Buf()
        dksh = Buf(); dvsh = Buf()

        gains = A.alloc([80], F32)
        invf = A.alloc([1], F32)
        perm = A.alloc([128], BF16)
        identb = A.alloc([128], BF16)
        identf = A.alloc([128], F32)
        onesf = A.alloc([128], F32)
        cwt = A.alloc([4 * 3 * 2 * NPAIR], F32)
        cbt = A.alloc([4 * 2 * NPAIR], F32)
        sinkt = A.alloc([32], F32)
        esink = A.alloc([32], F32)
        cbuf = Buf()
        for dst, src_ in ((gains, gains_in[:, :]), (invf, invf_in[:, :]), (perm, perm_in[:, :]),
                          (identb, identb_in[:, :]), (identf, identf_in[:, :]),
                          (sinkt, sinks_in[0:1, :].broadcast_to([128, 32]))):
            P.dma("sp", dst, src_, writes=[cbuf])
        NCW = 3 * 2 * NPAIR
        NCB = 2 * NPAIR
        for l in range(4):
            P.dma("sp", cwt[:, l * NCW:(l + 1) * NCW], cw_in[l, :, :], writes=[cbuf])
            P.dma("sp", cbt[:, l * NCB:(l + 1) * NCB], cb_in[l, :, :], writes=[cbuf])
        P.op("dve", MS(onesf, 1.0), writes=[cbuf])
        P.op("act", ACTV(esink, sinkt, AF.Exp), reads=[cbuf], writes=[cbuf])
        P.barrier()
        BASE = A.off
        WD_OFF = ARENA_BYTES - NPAIR * D * 2
        WO_OFF = ARENA_BYTES - 8 * D * 2
        pre = {}

        def fixed_alloc(off, shape, dtype):
            save = A.off
            A.off = off
            ap = A.alloc(shape, dtype)
            A.off = save
            return ap

        def prefetch_wo(l):
            wo = fixed_alloc(WO_OFF, [8, D], BF16)
            wb = Buf()
            load_w(wo, (woa_in[l] if l < 2 else wob_in[l - 2]), wb)
            pre["wo"] = (wo, wb)

        def prefetch_wd(l):
            wd = fixed_alloc(WD_OFF, [NPAIR, D], BF16)
            wb = Buf()
            load_w(wd, wdn_in[l], wb)
            pre["wd"] = (wd, wb)

        def stage_rope():
            A.reset(BASE)
            W = 1024
            pi_ = A.alloc([W], I32); ang = A.alloc([W], F32); t_ = A.alloc([W], F32)
            ki = A.alloc([W], I32); kf = A.alloc([W], F32); r_ = A.alloc([W], F32)
            m_ = A.alloc([W], F32); rc = A.alloc([W], F32)
            so = A.alloc([W], F32); co = A.alloc([W], F32)
            b = Buf()
            PI = float(np.pi)

            def D_(fn):
                P.op("dve", fn, reads=[b, cbuf], writes=[b])

            def fix(rr):
                D_(TSS(m_, rr, PI, ALU.is_gt))
                D_(STT(rr, m_, -TWO_PI, rr, ALU.mult, ALU.add))
                D_(TSS(m_, rr, -PI, ALU.is_lt))
                D_(STT(rr, m_, TWO_PI, rr, ALU.mult, ALU.add))
                D_(TS(rr, rr, 3.141592, -3.141592, ALU.min, ALU.max))

            for ci in range(T // W):
                cs = slice(ci * W, (ci + 1) * W)
                P.dma("sp", pi_, pos_in[0:1, cs].broadcast_to([128, W]), writes=[b])
                D_(CP(ang, pi_))
                D_(TS(ang, ang, invf[:, 0:1], None, ALU.mult))
                D_(TS(t_, ang, 1.0 / TWO_PI, None, ALU.mult))
                D_(CP(ki, t_))
                D_(CP(kf, ki))
                D_(STT(r_, kf, -C1, ang, ALU.mult, ALU.add))
                D_(STT(r_, kf, -C2, r_, ALU.mult, ALU.add))
                fix(r_)
                D_(TS(rc, r_, PI / 2, None, ALU.add))
                fix(rc)
                P.op("act", ACTV(so, r_, AF.Sin), reads=[b], writes=[b])
                P.op("act", ACTV(co, rc, AF.Sin), reads=[b], writes=[b])
                P.dma("sp", Ss[:, cs], so, reads=[b], writes=[dcs])
                P.dma("sp", Cs[:, cs], co, reads=[b], writes=[dcs])
            P.barrier()

        def norm_rstd(xt, xtb, sq, sqb, ssbank, rstd, rstdb):
            P.op("act", ACTV(sq, xt, AF.Square), reads=[xtb], writes=[sqb])
            P.pe_group([mm(BK[ssbank], onesf, sq[:, c, :], start=(c == 0), stop=(c == 7)) for c in range(8)],
                       reads=[sqb, cbuf], writes=[bkb[ssbank]])
            P.op("act", ACTV(rstd, BK[ssbank], AF.Sqrt, scale=1.0 / D, bias=EPS), reads=[bkb[ssbank]], writes=[rstdb])
            P.op("dve", RCP(rstd, rstd), reads=[rstdb], writes=[rstdb])

        def norm_apply(xt, xtb, rstd, rstdb, gi, outs, outb):
            for c in range(8):
                P.op("dve", STT(outs[c], xt[:, c, :], gains[:, gi * 8 + c:gi * 8 + c + 1], rstd, ALU.mult, ALU.mult),
                     reads=[xtb, rstdb, cbuf], writes=[outb])

        def load_w(dst, src_ap, wb):
            P.dma("pool", dst, src_ap.rearrange("(kc p) n -> p kc n", p=128), writes=[wb])

        def stage_proj(l, Xsrc, dXsrc):
            A.reset(BASE)
            moba = l < 2
            mk_kv = (l == 2)
            wb = Buf()
            wkv = None
            if moba:
                wsb = A.alloc([8, 3 * D], BF16)
                load_w(wsb, wqkv_in[l], wb)
                nqk = 16
            else:
                wsb = A.alloc([8, D], BF16)
                load_w(wsb, wqb_in[l - 2], wb)
                nqk = 8
                if mk_kv:
                    wkv = A.alloc([8, 512], BF16)
                    load_w(wkv, wkvb_in, wb)
            xt = [A.alloc([8, 512], F32) for _ in range(2)]; xtb = bufs(2)
            ct = [A.alloc([512], F32) for _ in range(2)]; st = [A.alloc([512], F32) for _ in range(2)]; csb = bufs(2)
            sq = A.alloc([8, 512], F32); sqb = Buf()
            rstd = A.alloc([512], F32); rstdb = Buf()
            hT = [A.alloc([8, 512], BF16) for _ in range(2)]; hTb = bufs(2)
            hK = None; hKb = None
            if mk_kv:
                hK = [A.alloc([8, 512], BF16) for _ in range(2)]; hKb = bufs(2)
            qb = [A.alloc([512], BF16) for _ in range(2)]; qbb = bufs(2)
            t1 = [A.alloc([512], F32) for _ in range(2)]; t1b = bufs(2)
            u1 = [A.alloc([512], F32) for _ in range(2)]; u1b = bufs(2)
            qr = [A.alloc([512], BF16) for _ in range(3)]; qrb = bufs(3)
            vt = [A.alloc([4, 1024], BF16) for _ in range(2)]; vtb = bufs(2)
            SSB, PA, PB, PV = 0, (1, 2), (3, 4), (5, 6)

            def load_x(i):
                b = i % 2
                cs = slice(i * 512, (i + 1) * 512)
                P.dma("sp", xt[b], Xsrc[:, cs].rearrange("(c p) n -> p c n", p=128), reads=[dXsrc[i]], writes=[xtb[b]])

            def load_cs(i):
                b = i % 2
                cs = slice(i * 512, (i + 1) * 512)
                P.dma("sp", ct[b], Cs[:, cs], reads=[dcs], writes=[csb[b]])
                P.dma("sp", st[b], Ss[:, cs], reads=[dcs], writes=[csb[b]])

            state = {"k": 0, "kv": 0}
            PA3 = (1, 2, 7)

            def qk_front(job):
                b, wts, hsrc, hsrcb, dst_ap, dbuf = job
                k = state["k"]; state["k"] += 1
                pa = PA3[k % 3]; r2 = k % 2; r3 = k % 3
                P.pe_group([mm(BK[pa], wts[kc], hsrc[:, kc, :], start=(kc == 0), stop=(kc == 7)) for kc in range(8)],
                           reads=[wb, hsrcb], writes=[bkb[pa]])
                P.op("act", ACP(qb[r2], BK[pa]), reads=[bkb[pa]], writes=[qbb[r2]])
                return (k, r2, r3)

            def qk_back(job, tag):
                b, wts, hsrc, hsrcb, dst_ap, dbuf = job
                k, r2, r3 = tag
                pb = PB[k % 2]
                P.pe_group([mm(BK[pb], perm, qb[r2])], reads=[qbb[r2], cbuf], writes=[bkb[pb]])
                P.op("pool", TT(t1[r2], qb[r2], ct[b], ALU.mult), reads=[qbb[r2], csb[b]], writes=[t1b[r2]])
                P.op("dve", TT(u1[r2], BK[pb], st[b], ALU.mult), reads=[bkb[pb], csb[b]], writes=[u1b[r2]])
                P.op("dve", TT(qr[r3], u1[r2], t1[r2], ALU.add), reads=[u1b[r2], t1b[r2]], writes=[qrb[r3]])
                P.dma("sp", dst_ap, qr[r3], reads=[qrb[r3]], writes=[dbuf])

            def v_job(i, b, s, g):
                vb = i % 2
                pv = PV[state["kv"] % 2]; state["kv"] += 1
                if moba:
                    fns = [mm(BK[pv], hT[b][:, kc, s * 128:(s + 1) * 128], wsb[:, kc, 2048 + g * 512:2048 + (g + 1) * 512],
                              start=(kc == 0), stop=(kc == 7)) for kc in range(8)]
                    P.pe_group(fns, reads=[wb, hTb[b]], writes=[bkb[pv]])
                    P.op("act", ACP(vt[vb][:, s, g * 512:(g + 1) * 512], BK[pv]), reads=[bkb[pv]], writes=[vtb[vb]])
                else:
                    fns = [mm(BK[pv][:, 0:256], hK[b][:, kc, s * 128:(s + 1) * 128], wkv[:, kc, 256:512],
                              start=(kc == 0), stop=(kc == 7)) for kc in range(8)]
                    P.pe_group(fns, reads=[wb, hKb[b]], writes=[bkb[pv]])
                    P.op("act", ACP(vt[vb][:, s, 0:256], BK[pv][:, 0:256]), reads=[bkb[pv]], writes=[vtb[vb]])

            def do_norm(i):
                b = i % 2
                norm_rstd(xt[b], xtb[b], sq, sqb, SSB, rstd, rstdb)
                norm_apply(xt[b], xtb[b], rstd, rstdb, l, [hT[b][:, c, :] for c in range(8)], hTb[b])
                if mk_kv:
                    norm_apply(xt[b], xtb[b], rstd, rstdb, 8, [hK[b][:, c, :] for c in range(8)], hKb[b])

            load_x(0); load_cs(0)
            if NT > 1:
                load_x(1); load_cs(1)
            do_norm(0)
            for i in range(NT):
                b = i % 2
                cs = slice(i * 512, (i + 1) * 512)
                jobs = []
                for m in range(nqk):
                    if m < 8:
                        dst, dbuf = qTs[m * 128:(m + 1) * 128, cs], dq
                    else:
                        dst, dbuf = kTs[(m - 8) * 128:(m - 7) * 128, cs], dk
                    jobs.append((b, [wsb[:, kc, m * 128:(m + 1) * 128] for kc in range(8)], hT[b], hTb[b], dst, dbuf))
                if mk_kv:
                    for m in range(2):
                        jobs.append((b, [wkv[:, kc, m * 128:(m + 1) * 128] for kc in range(8)], hK[b], hKb[b],
                                     kTsh[m * 128:(m + 1) * 128, cs], dksh))
                vjobs = []
                if moba or mk_kv:
                    ngrp = 2 if moba else 1
                    vjobs = [(s, g) for s in range(4) for g in range(ngrp)]
                if i + 2 < NT:
                    load_x(i + 2)
                tags = {}
                tags[0] = qk_front(jobs[0])
                nj = len(jobs)
                vper = (len(vjobs) + nj - 1) // nj if vjobs else 0
                vi = 0
                for j in range(nj):
                    if j + 1 < nj:
                        tags[j + 1] = qk_front(jobs[j + 1])
                    for _ in range(vper):
                        if vi < len(vjobs):
                            v_job(i, b, *vjobs[vi]); vi += 1
                    qk_back(jobs[j], tags.pop(j))
                    if j == nj // 2 - 1 and i + 1 < NT:
                        do_norm(i + 1)
                while vi < len(vjobs):
                    v_job(i, b, *vjobs[vi]); vi += 1
                if moba or mk_kv:
                    vb = i % 2
                    if moba:
                        P.dma("sp", vs[cs, :].rearrange("(s p) f -> p s f", p=128), vt[vb], reads=[vtb[vb]], writes=[dv])
                    else:
                        P.dma("sp", vsh[cs, :].rearrange("(s p) f -> p s f", p=128), vt[vb][:, :, 0:256], reads=[vtb[vb]], writes=[dvsh])
                if i + 2 < NT:
                    load_cs(i + 2)
            P.barrier()

        def stage_moba(l):
            A.reset(BASE)
            G = NS * 16
            pastb = A.alloc([G], F32); past01 = A.alloc([G], F32); own01 = A.alloc([G], F32)
            causal = A.alloc([2, 256], BF16)
            tb = Buf()
            P.dma("sp", pastb, pastb_in[:, :], writes=[tb]); P.dma("sp", past01, past01_in[:, :], writes=[tb])
            P.dma("sp", own01, own01_in[:, :], writes=[tb])
            P.dma("sp", causal, causal_in[:, :].rearrange("p (a b) -> p a b", a=2), writes=[tb])
            Kaug = [A.alloc([T], BF16) for _ in range(2)]; Kb = bufs(2)
            Qaug = [A.alloc([T], BF16) for _ in range(2)]; Qhb = bufs(2); Qlb = bufs(2)
            Vh = [A.alloc([NS, 128], BF16) for _ in range(2)]; Vb = bufs(2)
            oh = [A.alloc([T], BF16, parts=64) for _ in range(2)]; ohb = bufs(2)
            km = A.alloc([16], F32); kmb16 = A.alloc([16], BF16); kmb = Buf()
            gm = A.alloc([G], F32); g2 = A.alloc([G], F32); eq = A.alloc([G], F32); sel = A.alloc([G], F32)
            mx = A.alloc([NS], F32); mbb = A.alloc([G + 128], BF16); gb = Buf()
            NPT = 3
            Pt = [A.alloc([1024], BF16) for _ in range(NPT)]; Ptb = bufs(NPT)
            rden = A.alloc([256], F32); rdb = Buf()
            assert A.off <= WO_OFF
            prefetch_wo(l)
            GB = 0; TB_ = 0
            SS = ((1, 2), (3, 4))
            OB = (5, 6, 7)
            tbk = BK[TB_].bitcast(BF16)

            def g3(ap):
                return ap.rearrange("p (a b) -> p a b", b=16)

            def bc(ap):
                return ap.unsqueeze(2).to_broadcast([128, NS, 16])

            P.op("dve", MS(km, 0.0), writes=[kmb])
            P.op("dve", MS(mbb, 0.0), writes=[gb])
            for b in range(2):
                P.dma("sp", Kaug[b][0:16, :], erows_in[:, :], writes=[Kb[b]])
                P.op("dve", MS(Qaug[b][0:16, :], 0.0), writes=[Qlb[b]])
                P.op("dve", MS(Vh[b][:, :, 64:128], 1.0), writes=[Vb[b]])

            def load(h):
                b = h % 2
                P.dma("sp", Kaug[b][16:80, :], kTs[h * 64:(h + 1) * 64, :], reads=[dk], writes=[Kb[b]])
                P.dma("sp", Qaug[b][16:80, :], qTs[h * 64:(h + 1) * 64, :], reads=[dq], writes=[Qhb[b]])
                P.dma("sp", Vh[b][:, :, 0:64], vs[:, h * 64:(h + 1) * 64].rearrange("(c p) d -> p c d", p=128), reads=[dv], writes=[Vb[b]])

            def GD(fn, extra=()):
                P.op("dve", fn, reads=[gb] + list(extra), writes=[gb])

            def gate_front(h):
                b = h % 2
                P.op("dve", TR(km[0:80, 0:NB], Kaug[b][0:80, :].rearrange("p (n k) -> p n k", k=BLK), ALU.add), reads=[Kb[b]], writes=[kmb])
                P.op("dve", MS(km[0:16, :], 0.0), reads=[kmb], writes=[kmb])
                P.op("dve", TS(kmb16[0:80, :], km[0:80, :], 1.0 / BLK, None, ALU.mult), reads=[kmb], writes=[kmb])
                P.pe_group([mm(BK[GB][:, s * 16:(s + 1) * 16], Qaug[b][0:80, s * 128:(s + 1) * 128], kmb16[0:80, :]) for s in range(NS)],
                           reads=[Qhb[b], Qlb[b], kmb], writes=[bkb[GB]])
                P.op("dve", TT(gm, BK[GB][:, 0:G], pastb, ALU.add), reads=[bkb[GB], tb], writes=[gb])
                GD(TR(mx, g3(gm), ALU.max))
                GD(TT(g3(eq), g3(gm), bc(mx), ALU.is_equal))
                GD(STT(g2, eq, -1e9, gm, ALU.mult, ALU.add))
                GD(TR(mx, g3(g2), ALU.max))
                GD(TT(g3(eq), g3(g2), bc(mx), ALU.is_equal))
                GD(STT(g2, eq, -1e9, g2, ALU.mult, ALU.add))
                GD(TR(mx, g3(g2), ALU.max))
                GD(TT(g3(sel), g3(gm), bc(mx), ALU.is_ge))
                GD(TT(sel, sel, past01, ALU.mult), [tb])
                GD(TT(sel, sel, own01, ALU.add), [tb])
                GD(TS(mbb[:, 0:G], sel, -NEGB, NEGB, ALU.mult, ALU.add))

            def gate_back(h):
                b = h % 2
                for grp in range((NS + 7) // 8):
                    n8 = min(8, NS - grp * 8)
                    P.pe_group([TP(tbk[:, j * 128:(j + 1) * 128], mbb[:, s * 16:s * 16 + 128], identb)
                                for j, s in enumerate(range(grp * 8, grp * 8 + n8))],
                               reads=[gb, cbuf], writes=[bkb[TB_]])
                    P.op("act", ACP(Qaug[b][0:16, grp * 1024:grp * 1024 + n8 * 128], tbk[0:16, 0:n8 * 128]),
                         reads=[bkb[TB_]], writes=[Qlb[b]])

            load(0)
            gate_front(0)
            gate_back(0)
            sk = 0
            for h in range(NH):
                b = h % 2
                if h + 1 < NH:
                    load(h + 1)
                items = []
                for i in range(NB):
                    ns = list(range(i + 1))
                    for a in range(0, len(ns), 2):
                        items.append((i, ns[a:a + 2]))
                slot = {}

                def emit_qk(t):
                    nonlocal sk
                    i, nl = items[t]
                    s0, s1 = SS[sk % 2]; pt = sk % NPT; sk += 1
                    slot[t] = (s0, s1, pt)
                    qs = slice(i * BLK, (i + 1) * BLK)
                    fns = []
                    for j, n in enumerate(nl):
                        sb = (s0, s1)[j]
                        for c in range(2):
                            ks = slice((2 * n + c) * 128, (2 * n + c + 1) * 128)
                            if n < i:
                                fns.append(mm(BK[sb][:, c * 256:(c + 1) * 256], Kaug[b][0:80, ks], Qaug[b][0:80, qs]))
                            else:
                                fns.append(mm(BK[sb][:, c * 256:(c + 1) * 256], Kaug[b][0:80, ks], Qaug[b][0:80, qs], start=True, stop=False))
                                fns.append(mm(BK[sb][:, c * 256:(c + 1) * 256], identb, causal[:, c, :], start=False, stop=True))
                    w = 512 * len(nl)
                    P.pe_group(fns, reads=[Kb[b], Qhb[b], Qlb[b], tb, cbuf], writes=[bkb[s0], bkb[s1]])
                    P.op("act", ACTV(Pt[pt][:, 0:w], ps_all[:, s0 * 512:s0 * 512 + w], AF.Exp, scale=0.125),
                         reads=[bkb[s0], bkb[s1]], writes=[Ptb[pt]])

                def emit_pv(t):
                    i, nl = items[t]
                    s0, s1, pt = slot.pop(t)
                    ob = OB[i % 3]
                    qs = slice(i * BLK, (i + 1) * BLK)
                    fns = []
                    for j, n in enumerate(nl):
                        for c in range(2):
                            fns.append(mm(BK[ob][:, 0:256], Vh[b][:, 2 * n + c, :], Pt[pt][:, j * 512 + c * 256:j * 512 + (c + 1) * 256],
                                          start=(n == 0 and c == 0), stop=(n == i and c == 1)))
                    P.pe_group(fns, reads=[Vb[b], Ptb[pt]], writes=[bkb[ob]])
                    if nl[-1] == i:
                        P.op("dve", RCP(rden[64:128, :], BK[ob][64:128, 0:256]), reads=[bkb[ob]], writes=[rdb])
                        P.op("dve", TT(oh[b][0:64, qs], BK[ob][0:64, 0:256], rden[64:128, :], ALU.mult),
                             reads=[bkb[ob], rdb], writes=[ohb[b]])

                NI = len(items)
                LOOK = 1
                gf_blk = min(1, NB - 1)
                gb_blk = max(gf_blk, NB - 3)
                for t in range(min(LOOK, NI)):
                    emit_qk(t)
                for t in range(NI):
                    if t + LOOK < NI:
                        emit_qk(t + LOOK)
                    emit_pv(t)
                    i_, nl_ = items[t]
                    if h + 1 < NH and nl_[-1] == i_:
                        if i_ == gf_blk:
                            gate_front(h + 1)
                        if i_ == gb_blk:
                            gate_back(h + 1)
                P.dma("sp", oTs[h * 64:(h + 1) * 64, :], oh[b], reads=[ohb[b]], writes=[do])
            P.barrier()

        def stage_swa(l):
            A.reset(BASE)
            li = l - 2
            bandp = A.alloc([512], BF16); bando = A.alloc([512], BF16)
            es = A.alloc([16, 128], F32)
            tb = Buf()
            P.dma("sp", bandp, bandp_in[:, :], writes=[tb]); P.dma("sp", bando, bando_in[:, :], writes=[tb])
            P.op("dve", CP(es, esink[:, li * 16:(li + 1) * 16].unsqueeze(2).to_broadcast([128, 16, 128])), reads=[cbuf], writes=[tb])
            Kg = [A.alloc([T], BF16, parts=64) for _ in range(2)]; Kb = bufs(2)
            Vg = [A.alloc([NS, 128], BF16) for _ in range(2)]; Vb = bufs(2)
            Qg = [A.alloc([4, T], BF16, parts=64) for _ in range(2)]; Qb = bufs(2)
            og1 = A.alloc([4, T], BF16, parts=64); og = [og1, og1]; ogb1 = Buf(); ogb = [ogb1, ogb1]
            Pt = [A.alloc([512], BF16) for _ in range(4)]; Ptb = bufs(4)
            den = A.alloc([512], F32); rdb = Buf()
            assert A.off <= WO_OFF
            prefetch_wo(l)
            onesb = A.alloc([128], BF16)
            SB, OB, DB = (0, 1, 2, 3), (4, 5), (6, 7)
            P.op("dve", MS(onesb, 1.0), writes=[tb])
            for b in range(2):
                P.op("dve", MS(Vg[b][:, :, 64:128], 1.0), writes=[Vb[b]])

            def load(g):
                b = g % 2
                P.dma("sp", Kg[b], kTsh[g * 64:(g + 1) * 64, :], reads=[dksh], writes=[Kb[b]])
                P.dma("sp", Vg[b][:, :, 0:64], vsh[:, g * 64:(g + 1) * 64].rearrange("(c p) d -> p c d", p=128), reads=[dvsh], writes=[Vb[b]])
                for hh in range(4):
                    hd = g * 4 + hh
                    P.dma("sp", Qg[b][:, hh, :], qTs[hd * 64:(hd + 1) * 64, :], reads=[dq], writes=[Qb[b]])

            load(0)
            sk = 0
            for g in range(NKV):
                b = g % 2
                if g + 1 < NKV:
                    load(g + 1)
                fr = {}

                def front(s):
                    nonlocal sk
                    qs = slice(s * 128, (s + 1) * 128)
                    chunks = ([(s - 1, bandp)] if s > 0 else []) + [(s, bando)]
                    pts = []
                    for (kc, band) in chunks:
                        sb = SB[sk % 4]; pt = sk % 4; sk += 1
                        P.pe_group([mm(BK[sb].rearrange("p (a b) -> p a b", a=4), Kg[b][:, kc * 128:(kc + 1) * 128], Qg[b][:, :, qs], start=True, stop=False),
                                    mm(BK[sb], identb, band, start=False, stop=True)],
                                   reads=[Kb[b], Qb[b], tb, cbuf], writes=[bkb[sb]])
                        P.op("act", ACTV(Pt[pt], BK[sb], AF.Exp, scale=0.125), reads=[bkb[sb]], writes=[Ptb[pt]])
                        pts.append((kc, pt))
                    fr[s] = pts

                def back(s):
                    pts = fr.pop(s)
                    ob = OB[s % 2]; db = DB[s % 2]
                    qs = slice(s * 128, (s + 1) * 128)
                    P.pe_group([mm(BK[ob], Vg[b][:, kc, :], Pt[pt], start=(j == 0), stop=(j == len(pts) - 1)) for j, (kc, pt) in enumerate(pts)],
                               reads=[Vb[b]] + [Ptb[pt] for _, pt in pts], writes=[bkb[ob]])
                    P.pe_group([mm(BK[db], onesb, Pt[pt], start=(j == 0), stop=(j == len(pts) - 1)) for j, (kc, pt) in enumerate(pts)],
                               reads=[tb] + [Ptb[pt] for _, pt in pts], writes=[bkb[db]])
                    P.op("dve", TT(den[0:64, :], BK[db][0:64, :], es[0:64, g * 4:(g + 1) * 4, :].rearrange("p a b -> p (a b)"), ALU.add),
                         reads=[bkb[db], tb], writes=[rdb])
                    P.op("act", ACTV(den[0:64, :], den[0:64, :], AF.Ln), reads=[rdb], writes=[rdb])
                    P.op("act", ACTV(den[0:64, :], den[0:64, :], AF.Exp, scale=-1.0), reads=[rdb], writes=[rdb])
                    P.op("dve", TT(og[b][:, :, qs], BK[ob][0:64, :].rearrange("p (a b) -> p a b", a=4),
                                   den[0:64, :].rearrange("p (a b) -> p a b", a=4), ALU.mult),
                         reads=[bkb[ob], rdb], writes=[ogb[b]])

                front(0)
                for s in range(NS):
                    if s + 1 < NS:
                        front(s + 1)
                    back(s)
                for hh in range(4):
                    hd = g * 4 + hh
                    P.dma("sp", oTs[hd * 64:(hd + 1) * 64, :], og[b][:, hh, :], reads=[ogb[b]], writes=[do])
            P.barrier()

        def stage_oproj_ffn_up(l, Xsrc, dXsrc):
            A.reset(BASE)
            wb = Buf()
            hall = A.alloc([8, T], BF16); hallb = Buf()
            mark = A.off
            wo, wb = pre.pop("wo")
            ot = [A.alloc([8, 512], BF16) for _ in range(2)]; otb = bufs(2)
            xt = [A.alloc([8, 512], F32) for _ in range(2)]; xtb = bufs(2)
            xm = [A.alloc([8, 512], F32) for _ in range(2)]; xmb = bufs(2)
            rstd = A.alloc([512], F32); rstdb = Buf()
            SSB, PO = 0, (1, 2, 3)

            def load(i):
                b = i % 2
                cs = slice(i * 512, (i + 1) * 512)
                P.dma("sp", ot[b], oTs[:, cs].rearrange("(c p) n -> p c n", p=128), reads=[do], writes=[otb[b]])
                P.dma("sp", xt[b], Xsrc[:, cs].rearrange("(c p) n -> p c n", p=128), reads=[dXsrc[i]], writes=[xtb[b]])

            load(0)
            k = 0
            for i in range(NT):
                b = i % 2
                cs = slice(i * 512, (i + 1) * 512)
                if i + 1 < NT:
                    load(i + 1)
                for m in range(8):
                    po = PO[k % 3]; k += 1
                    P.pe_group([mm(BK[po], wo[:, kc, m * 128:(m + 1) * 128], ot[b][:, kc, :], start=(kc == 0), stop=(kc == 7)) for kc in range(8)],
                               reads=[wb, otb[b]], writes=[bkb[po]])
                    P.op("dve", TT(xm[b][:, m, :], BK[po], xt[b][:, m, :], ALU.add), reads=[bkb[po], xtb[b]], writes=[xmb[b]])
                P.dma("sp", Ms[:, cs].rearrange("(c p) n -> p c n", p=128), xm[b], reads=[xmb[b]], writes=[dM[i]])
                norm_rstd(xm[b], xmb[b], xt[b], xtb[b], SSB, rstd, rstdb)
                norm_apply(xm[b], xmb[b], rstd, rstdb, 4 + l, [hall[:, c, cs] for c in range(8)], hallb)
            P.barrier()

            A.reset(mark)
            prefetch_wd(l)
            wg = [A.alloc([8, 128], BF16) for _ in range(2)]; wv = [A.alloc([8, 128], BF16) for _ in range(2)]; wgb = bufs(2)
            dg = [A.alloc([6, 128], BF16) for _ in range(2)]; dgb = bufs(2)
            ug = [A.alloc([514], BF16) for _ in range(3)]; uv = [A.alloc([514], BF16) for _ in range(3)]; ugb = bufs(3); uvb = bufs(3)
            sg = [A.alloc([512], F32) for _ in range(2)]; sgb = bufs(2)
            at = [A.alloc([512], BF16) for _ in range(3)]; atb = bufs(3)
            PG, PVv, PCG, PCV = (0, 1), (2, 3), (4, 5), (6, 7)

            def loadw(j):
                b = j % 2
                P.dma("pool", wg[b], wup_in[l, j].rearrange("p (kc n) -> p kc n", kc=8), writes=[wgb[b]])
                P.dma("pool", wv[b], wup_in[l, NPAIR + j].rearrange("p (kc n) -> p kc n", kc=8), writes=[wgb[b]])

            def pair_start(j):
                b = j % 2
                if j + 1 < NPAIR:
                    loadw(j + 1)
                for tap in range(3):
                    for gv in range(2):
                        col = l * NCW + tap * 2 * NPAIR + gv * NPAIR + j
                        P.op("dve", TS(dg[b][:, tap * 2 + gv, :], identf, cwt[:, col:col + 1], None, ALU.mult), reads=[cbuf], writes=[dgb[b]])
                u0 = (j * NT) % 3
                P.op("dve", MS(ug[u0][:, 0:2], 0.0), writes=[ugb[u0]])
                P.op("dve", MS(uv[u0][:, 0:2], 0.0), writes=[uvb[u0]])

            def front(k, j, i):
                b = j % 2
                cs = slice(i * 512, (i + 1) * 512)
                r = k % 2; ub = k % 3; un = (k + 1) % 3
                pg, pv = PG[r], PVv[r]
                P.pe_group([mm(BK[pg], wg[b][:, kc, :], hall[:, kc, cs], start=(kc == 0), stop=(kc == 7)) for kc in range(8)],
                           reads=[wgb[b], hallb], writes=[bkb[pg]])
                P.pe_group([mm(BK[pv], wv[b][:, kc, :], hall[:, kc, cs], start=(kc == 0), stop=(kc == 7)) for kc in range(8)],
                           reads=[wgb[b], hallb], writes=[bkb[pv]])
                P.op("act", ACP(ug[ub][:, 2:514], BK[pg]), reads=[bkb[pg]], writes=[ugb[ub]])
                P.op("act", ACP(uv[ub][:, 2:514], BK[pv]), reads=[bkb[pv]], writes=[uvb[ub]])
                if i + 1 < NT:
                    P.op("pool", CP(ug[un][:, 0:2], ug[ub][:, 512:514]), reads=[ugb[ub]], writes=[ugb[un]])
                    P.op("pool", CP(uv[un][:, 0:2], uv[ub][:, 512:514]), reads=[uvb[ub]], writes=[uvb[un]])

            def back(k, j, i):
                b = j % 2
                cs = slice(i * 512, (i + 1) * 512)
                r = k % 2; r3 = k % 3; ub = k % 3
                pcg, pcv = PCG[r], PCV[r]
                cg = l * NCB + j
                cv = l * NCB + NPAIR + j
                P.pe_group([mm(BK[pcg], dg[b][:, tap * 2 + 0, :], ug[ub][:, tap:tap + 512], start=(tap == 0), stop=(tap == 2)) for tap in range(3)],
                           reads=[dgb[b], ugb[ub]], writes=[bkb[pcg]])
                P.pe_group([mm(BK[pcv], dg[b][:, tap * 2 + 1, :], uv[ub][:, tap:tap + 512], start=(tap == 0), stop=(tap == 2)) for tap in range(3)],
                           reads=[dgb[b], uvb[ub]], writes=[bkb[pcv]])
                P.op("act", ACTV(sg[r], BK[pcg], AF.Silu, bias=cbt[:, cg:cg + 1]), reads=[bkb[pcg], cbuf], writes=[sgb[r]])
                P.op("dve", STT(at[r3], BK[pcv], cbt[:, cv:cv + 1], sg[r], ALU.add, ALU.mult),
                     reads=[bkb[pcv], sgb[r], cbuf], writes=[atb[r3]])
                P.dma("sp", aTs[j * 128:(j + 1) * 128, cs], at[r3], reads=[atb[r3]], writes=[da])

            loadw(0)
            items = [(j, i) for j in range(NPAIR) for i in range(NT)]
            pair_start(0)
            front(0, *items[0])
            for k, (j, i) in enumerate(items):
                if k + 1 < len(items):
                    jn, in_ = items[k + 1]
                    if in_ == 0:
                        pair_start(jn)
                    front(k + 1, jn, in_)
                back(k, j, i)
            P.barrier()

        def stage_ffn_down(l):
            A.reset(BASE)
            last = (l == 3)
            wd, wb = pre.pop("wd")
            at = [A.alloc([NPAIR, 512], BF16) for _ in range(2)]; atb = bufs(2)
            xm = [A.alloc([8, 512], F32) for _ in range(2)]; xmb = bufs(2)
            xo = [A.alloc([8, 512], F32) for _ in range(2)]; xob = bufs(2)
            rstd = A.alloc([512], F32); rstdb = Buf()
            SSB, PO = 0, (1, 2, 3)
            dyo = Buf()

            def load(i):
                b = i % 2
                cs = slice(i * 512, (i + 1) * 512)
                P.dma("sp", at[b], aTs[:, cs].rearrange("(c p) n -> p c n", p=128), reads=[da], writes=[atb[b]])
                P.dma("sp", xm[b], Ms[:, cs].rearrange("(c p) n -> p c n", p=128), reads=[dM[i]], writes=[xmb[b]])

            load(0)
            k = 0
            for i in range(NT):
                b = i % 2
                cs = slice(i * 512, (i + 1) * 512)
                if i + 1 < NT:
                    load(i + 1)
                for m in range(8):
                    po = PO[k % 3]; k += 1
                    P.pe_group([mm(BK[po], wd[:, kc, m * 128:(m + 1) * 128], at[b][:, kc, :], start=(kc == 0), stop=(kc == NPAIR - 1)) for kc in range(NPAIR)],
                               reads=[wb, atb[b]], writes=[bkb[po]])
                    P.op("dve", TT(xo[b][:, m, :], BK[po], xm[b][:, m, :], ALU.add), reads=[bkb[po], xmb[b]], writes=[xob[b]])
                if not last:
                    P.dma("sp", Xs[:, cs].rearrange("(c p) n -> p c n", p=128), xo[b], reads=[xob[b]], writes=[dX[i]])
                else:
                    norm_rstd(xo[b], xob[b], xm[b], xmb[b], SSB, rstd, rstdb)
                    norm_apply(xo[b], xob[b], rstd, rstdb, 9, [xo[b][:, c, :] for c in range(8)], xob[b])
                    P.dma("sp", yT_out[:, cs].rearrange("(c p) n -> p c n", p=128), xo[b], reads=[xob[b]], writes=[dyo])
            P.barrier()

        stage_rope()
        dXin = bufs(NT)
        for l in range(4):
            Xsrc, dXsrc = (xT_in, dXin) if l == 0 else (Xs, dX)
            stage_proj(l, Xsrc, dXsrc)
            if l < 2:
                stage_moba(l)
            else:
                stage_swa(l)
            stage_oproj_ffn_up(l, Xsrc, dXsrc)
            stage_ffn_down(l)
        P.barrier()

        P.check()
        with nc.Block() as block:
            @block.tensor
            def _(e):
                P.run("pe", e)

            @block.scalar
            def _(e):
                P.run("act", e)

            @block.vector
            def _(e):
                P.run("dve", e)

            @block.gpsimd
            def _(e):
                P.run("pool", e)

            @block.sync
            def _(e):
                P.run("sp", e)
    return nc


def host_constants(T):
    bf = ml_dtypes.bfloat16
    NS = T // 128
    p = np.arange(128)
    d = p % 64
    invf8 = (np.float32(THETA) ** (-np.arange(0, 16, 2, dtype=np.float32) / np.float32(16))).astype(np.float32)
    invf = np.where(d < 16, invf8[d % 8], np.float32(0)).astype(np.float32).reshape(128, 1)
    perm = np.zeros((128, 128), np.float32)
    for m in range(128):
        dm = m % 64
        if dm < 8:
            perm[m + 8, m] = -1.0
        elif dm < 16:
            perm[m - 8, m] = 1.0
    ident = np.eye(128, dtype=np.float32)
    erows = np.zeros((16, T), np.float32)
    for n in range(T // BLK):
        erows[n, n * BLK:(n + 1) * BLK] = 1.0
    past01 = np.zeros((128, NS, 16), np.float32)
    own01 = np.zeros((128, NS, 16), np.float32)
    for s in range(NS):
        qb = (s * 128) // BLK
        past01[:, s, :qb] = 1.0
        own01[:, s, qb] = 1.0
    pastb = (past01 - 1.0) * 1e4
    kk = np.arange(128)[:, None]
    causal2 = np.zeros((128, 2, 256), np.float32)
    qq = np.arange(256)[None, :]
    for c in range(2):
        causal2[:, c, :] = np.where(c * 128 + kk <= qq, 0.0, NEGB)
    q1 = np.arange(128)[None, :]
    bandprev = np.tile(np.where(kk > q1, 0.0, NEGB), (1, 4))
    bandown = np.tile(np.where(kk <= q1, 0.0, NEGB), (1, 4))
    return {
        "invf": invf, "perm": perm.astype(bf), "identb": ident.astype(bf), "identf": ident,
        "erows": erows.astype(bf), "pastb": pastb.reshape(128, -1).astype(np.float32),
        "past01": past01.reshape(128, -1), "own01": own01.reshape(128, -1),
        "causal2": causal2.reshape(128, 512).astype(bf), "bandprev": bandprev.astype(bf), "bandown": bandown.astype(bf),
    }


def host_weights(attn_norm, w_qkv_a, w_o_a, kv_norm, w_kv_b, w_q_b, sinks_b, w_o_b, ffn_norm, w_up, conv_w, conv_b, w_down, final_norm):
    f = np.float32
    allg = np.concatenate([np.asarray(attn_norm, f), np.asarray(ffn_norm, f), np.asarray(kv_norm, f)[None], np.asarray(final_norm, f)[None]], 0)
    gains = np.ascontiguousarray(allg.reshape(10, 8, 128).transpose(2, 0, 1).reshape(128, 80))
    wup = np.asarray(w_up, f).reshape(4, 8, 128, 2 * NPAIR, 128).transpose(0, 3, 2, 1, 4)
    wup = np.ascontiguousarray(wup).reshape(4, 2 * NPAIR, 128, 1024)
    cw = np.asarray(conv_w, f).reshape(4, 3, 2 * NPAIR, 128).transpose(0, 3, 1, 2)
    cw = np.ascontiguousarray(cw).reshape(4, 128, 3 * 2 * NPAIR)
    cb = np.ascontiguousarray(np.asarray(conv_b, f).reshape(4, 2 * NPAIR, 128).transpose(0, 2, 1))
    return {
        "gains": gains,
        "wqkv": np.ascontiguousarray(np.asarray(w_qkv_a, f)), "woa": np.ascontiguousarray(np.asarray(w_o_a, f)),
        "wkvb": np.ascontiguousarray(np.asarray(w_kv_b, f)), "wqb": np.ascontiguousarray(np.asarray(w_q_b, f)),
        "wob": np.ascontiguousarray(np.asarray(w_o_b, f)), "sinks": np.ascontiguousarray(np.asarray(sinks_b, f).reshape(1, 32)),
        "wup": wup, "wdn": np.ascontiguousarray(np.asarray(w_down, f)), "cw": cw, "cb": cb,
    }


_CACHE = {}


def run_cores(x, positions, weights, T, debug=False):
    n = x.shape[0]
    key = (T, debug)
    if key not in _CACHE:
        _CACHE[key] = build_program(T, debug)
    nc = _CACHE[key]
    consts = host_constants(T)
    shared = dict(consts)
    shared.update(weights)
    in_maps = []
    for b in range(n):
        m = dict(shared)
        m["xT"] = np.ascontiguousarray(np.asarray(x[b], np.float32).T)
        m["pos"] = np.ascontiguousarray(np.asarray(positions[b], np.int32).reshape(1, T))
        in_maps.append(m)
    res = run_bass_kernel_spmd(nc, in_maps, core_ids=list(range(n)))
    return res


def kernel(x, positions, attn_norm, w_qkv_a, w_o_a, kv_norm, w_kv_b, w_q_b, sinks_b, w_o_b, ffn_norm, w_up, conv_w, conv_b, w_down, final_norm):
    x = np.asarray(x)
    B, T, _ = x.shape
    weights = host_weights(attn_norm, w_qkv_a, w_o_a, kv_norm, w_kv_b, w_q_b, sinks_b, w_o_b, ffn_norm, w_up, conv_w, conv_b, w_down, final_norm)
    res = run_cores(x, np.asarray(positions), weights, T)
    out = np.stack([np.ascontiguousarray(r["yT"].T) for r in res.results], 0)
    return out.astype(np.float32)
```
